# Optimizing a Trainium2 kernel written in Bass

```python
import math
import jax, jax.numpy as jnp
from jax import lax
import numpy as np

D_MODEL = 1024
BATCH = 4
SEQ = 8192
DEPTH = 1

EPS = 1e-6
D_RNN = D_MODEL
N_RNN_BLOCKS = 8
RNN_BLOCK = D_RNN // N_RNN_BLOCKS
RNN_CONV = 4
RG_C = 8.0
HEAD_DIM = 64
N_HEADS = D_MODEL // (2 * HEAD_DIM)
V_HEAD_DIM = 2 * HEAD_DIM
D_QK = N_HEADS * 2 * HEAD_DIM
D_V = N_HEADS * V_HEAD_DIM
ATTN_SCALE = HEAD_DIM ** -0.5
Q_BLOCK = 128
D_FF = 3 * D_MODEL
FFN_CONV = 3
SPLITS = (D_RNN, 2 * D_RNN, 2 * D_RNN + D_QK, 2 * D_RNN + 2 * D_QK,
          2 * D_RNN + 2 * D_QK + D_V, 2 * D_RNN + 2 * D_QK + D_V + D_MODEL)
D_IN = SPLITS[-1] + D_MODEL

kernel_name = "hybrid_rglru_diffattn_convffn"


def lambda_init_for(layer_idx):
    return 0.8 - 0.6 * math.exp(-0.3 * layer_idx)


def rms_norm(x, g, eps=EPS):
    xf = x.astype(jnp.float32)
    y = xf * lax.rsqrt(jnp.mean(xf * xf, axis=-1, keepdims=True) + eps)
    return (y * g.astype(jnp.float32)).astype(x.dtype)


def causal_dwconv(x, w, b):
    k_width, c = w.shape
    y = lax.conv_general_dilated(
        x, w[:, None, :].astype(x.dtype), window_strides=(1,),
        padding=[(k_width - 1, 0)], dimension_numbers=("NWC", "WIO", "NWC"),
        feature_group_count=c)
    return y + b


def _linear_recurrence(left, right):
    a_l, b_l = left
    a_r, b_r = right
    return a_l * a_r, a_r * b_l + b_r


def rg_lru(x, wa, ba, wx, bx, lam):
    bsz, s, _ = x.shape
    xb = x.reshape(bsz, s, N_RNN_BLOCKS, RNN_BLOCK)
    r = jax.nn.sigmoid(jnp.einsum('bsni,nij->bsnj', xb, wa).reshape(bsz, s, D_RNN) + ba)
    i = jax.nn.sigmoid(jnp.einsum('bsni,nij->bsnj', xb, wx).reshape(bsz, s, D_RNN) + bx)
    log_a = (-RG_C * r.astype(jnp.float32)) * jax.nn.softplus(-lam.astype(jnp.float32))
    a = jnp.exp(log_a)
    mult = jnp.sqrt(jnp.maximum(-jnp.expm1(2.0 * log_a), 0.0))
    b = mult * (i * x).astype(jnp.float32)
    _, h = lax.associative_scan(_linear_recurrence, (a, b), axis=1)
    return h.astype(x.dtype)


def diff_attention(q, k, v, lam):
    bsz, s = q.shape[:2]
    nb = s // Q_BLOCK
    qb = q.reshape(bsz, nb, Q_BLOCK, N_HEADS, 2, HEAD_DIM).transpose(1, 0, 2, 3, 4, 5)
    starts = jnp.arange(nb, dtype=jnp.int32) * Q_BLOCK
    k_pos = jnp.arange(s, dtype=jnp.int32)
    lam32 = lam.astype(jnp.float32)

    def one_block(args):
        q_blk, start = args
        sc = jnp.einsum('bqhmd,bkhmd->bhmqk', q_blk, k).astype(jnp.float32) * ATTN_SCALE
        q_pos = start + jnp.arange(Q_BLOCK, dtype=jnp.int32)
        causal = k_pos[None, :] <= q_pos[:, None]
        p = jax.nn.softmax(jnp.where(causal, sc, -jnp.inf), axis=-1)
        attn = p[:, :, 0] - lam32 * p[:, :, 1]
        return jnp.einsum('bhqk,bkhe->bqhe', attn.astype(v.dtype), v)

    o = lax.map(one_block, (qb, starts))
    return o.transpose(1, 0, 2, 3, 4).reshape(bsz, s, N_HEADS, V_HEAD_DIM)


def setup_inputs(seed: int = 0) -> dict:
    key = jax.random.key(seed)
    ks = jax.random.split(key, 32)
    f32 = jnp.float32

    def nrm(k, shape, scale):
        return jax.random.normal(k, shape, f32) * scale

    def gain(k, shape):
        return 1.0 + 0.02 * jax.random.normal(k, shape, f32)

    L = DEPTH
    x = jax.random.normal(ks[0], (BATCH, SEQ, D_MODEL), f32)
    attn_norm_g = gain(ks[1], (L, D_MODEL))
    w_in = nrm(ks[2], (L, D_MODEL, D_IN), D_MODEL ** -0.5)
    rnn_conv_w = nrm(ks[3], (L, RNN_CONV, D_RNN), RNN_CONV ** -0.5)
    rnn_conv_b = nrm(ks[4], (L, D_RNN), 0.02)
    rg_wa = nrm(ks[5], (L, N_RNN_BLOCKS, RNN_BLOCK, RNN_BLOCK), RNN_BLOCK ** -0.5)
    rg_ba = nrm(ks[6], (L, D_RNN), 0.02)
    rg_wx = nrm(ks[7], (L, N_RNN_BLOCKS, RNN_BLOCK, RNN_BLOCK), RNN_BLOCK ** -0.5)
    rg_bx = nrm(ks[8], (L, D_RNN), 0.02)
    u = jax.random.uniform(ks[9], (L, D_RNN), f32, 0.9, 0.999)
    s_a = u ** (1.0 / RG_C)
    rg_lambda = jnp.log(s_a) - jnp.log1p(-s_a)
    lam_q1 = nrm(ks[10], (L, HEAD_DIM), 0.1)
    lam_k1 = nrm(ks[11], (L, HEAD_DIM), 0.1)
    lam_q2 = nrm(ks[12], (L, HEAD_DIM), 0.1)
    lam_k2 = nrm(ks[13], (L, HEAD_DIM), 0.1)
    subln_g = gain(ks[14], (L, V_HEAD_DIM))
    w_proj_rnn = nrm(ks[15], (L, D_RNN, D_MODEL), D_RNN ** -0.5)
    w_proj_attn = nrm(ks[16], (L, D_V, D_MODEL), D_V ** -0.5)
    w_out = nrm(ks[17], (L, D_MODEL, D_MODEL), D_MODEL ** -0.5)
    mlp_norm_g = gain(ks[18], (L, D_MODEL))
    w_up = nrm(ks[19], (L, D_MODEL, 2 * D_FF), D_MODEL ** -0.5)
    ffn_conv_w = nrm(ks[20], (L, FFN_CONV, D_FF), FFN_CONV ** -0.5)
    ffn_conv_b = nrm(ks[21], (L, D_FF), 0.02)
    w_down = nrm(ks[22], (L, D_FF, D_MODEL), D_FF ** -0.5)
    final_norm_g = gain(ks[23], (D_MODEL,))
    return {"x": x, "attn_norm_g": attn_norm_g, "w_in": w_in,
            "rnn_conv_w": rnn_conv_w, "rnn_conv_b": rnn_conv_b,
            "rg_wa": rg_wa, "rg_ba": rg_ba, "rg_wx": rg_wx, "rg_bx": rg_bx,
            "rg_lambda": rg_lambda, "lam_q1": lam_q1, "lam_k1": lam_k1,
            "lam_q2": lam_q2, "lam_k2": lam_k2, "subln_g": subln_g,
            "w_proj_rnn": w_proj_rnn, "w_proj_attn": w_proj_attn, "w_out": w_out,
            "mlp_norm_g": mlp_norm_g, "w_up": w_up, "ffn_conv_w": ffn_conv_w,
            "ffn_conv_b": ffn_conv_b, "w_down": w_down, "final_norm_g": final_norm_g}


def reference(x, attn_norm_g, w_in, rnn_conv_w, rnn_conv_b, rg_wa, rg_ba, rg_wx, rg_bx,
              rg_lambda, lam_q1, lam_k1, lam_q2, lam_k2, subln_g, w_proj_rnn, w_proj_attn,
              w_out, mlp_norm_g, w_up, ffn_conv_w, ffn_conv_b, w_down, final_norm_g):
    bsz, s, _ = x.shape
    for l in range(DEPTH):
        lambda_init = lambda_init_for(l)
        h = rms_norm(x, attn_norm_g[l])
        proj = h @ w_in[l]
        xr, gr, q, k, v, g_rnn, g_attn = jnp.split(proj, SPLITS, axis=-1)
        xr = causal_dwconv(xr, rnn_conv_w[l], rnn_conv_b[l])
        y_rnn = rg_lru(xr, rg_wa[l], rg_ba[l], rg_wx[l], rg_bx[l], rg_lambda[l])
        y_rnn = y_rnn * jax.nn.gelu(gr)
        lam = (jnp.exp(jnp.sum(lam_q1[l].astype(jnp.float32) * lam_k1[l].astype(jnp.float32)))
               - jnp.exp(jnp.sum(lam_q2[l].astype(jnp.float32) * lam_k2[l].astype(jnp.float32)))
               + lambda_init)
        qh = q.reshape(bsz, s, N_HEADS, 2, HEAD_DIM)
        kh = k.reshape(bsz, s, N_HEADS, 2, HEAD_DIM)
        vh = v.reshape(bsz, s, N_HEADS, V_HEAD_DIM)
        o = diff_attention(qh, kh, vh, lam)
        o = rms_norm(o, subln_g[l], eps=1e-5) * (1.0 - lambda_init)
        y_attn = o.reshape(bsz, s, D_V)
        merged = (jax.nn.sigmoid(g_rnn) * (y_rnn @ w_proj_rnn[l])
                  + jax.nn.sigmoid(g_attn) * (y_attn @ w_proj_attn[l]))
        x = x + merged @ w_out[l]
        h = rms_norm(x, mlp_norm_g[l])
        u_gate, u_val = jnp.split(h @ w_up[l], 2, axis=-1)
        u_gate = causal_dwconv(u_gate, ffn_conv_w[l], ffn_conv_b[l])
        x = x + (jax.nn.gelu(u_gate) * u_val) @ w_down[l]
    return rms_norm(x, final_norm_g)
```

```python
import math
import os
from contextlib import ExitStack

import numpy as np
import concourse.bass as bass
import concourse.mybir as mybir
from concourse.bass_utils import run_bass_kernel_spmd

F32 = mybir.dt.float32
BF16 = mybir.dt.bfloat16
AF = mybir.ActivationFunctionType
ALU = mybir.AluOpType

D = 1024
NT = 8192
NOWN = 4096
Q0 = 3968
NQ = NT - Q0
CH = [(Q0, 128)] + [(NOWN + 512 * i, 512) for i in range(8)]
EPS = 1e-6
LAMBDA_INIT = 0.8 - 0.6 * math.exp(-0.3 * 0)
DFF = 3072
GELU = AF.Gelu_apprx_tanh

ENGS = ("sp", "act", "dve", "pool", "pe")
N_DSEM = 44
N_BG = 6


class Buf:
    __slots__ = ("name", "w", "rs", "dsem")

    def __init__(self, name="b"):
        self.name = name
        self.w = []
        self.rs = []
        self.dsem = None


class Op:
    __slots__ = ("eng", "fn", "deps", "sig", "sem", "val", "dma")

    def __init__(self, eng, fn, dma=False):
        self.eng = eng
        self.fn = fn
        self.deps = ()
        self.sig = dma
        self.sem = None
        self.val = 0
        self.dma = dma


class Ctx:
    def __init__(self, nc, stack):
        self.nc = nc
        self.esem = {e: stack.enter_context(nc.semaphore("es_" + e)) for e in ENGS}
        self.ecount = {e: 0 for e in ENGS}
        self.dsems = [stack.enter_context(nc.semaphore("ds%d" % i)) for i in range(N_DSEM)]
        self.dcount = [0] * N_DSEM
        self.nops = 0
        self.bgsems = [stack.enter_context(nc.semaphore("bg%d" % i)) for i in range(N_BG)]
        self.bgcount = [0] * N_BG
        self.bglast = [None] * N_BG


class Pass:
    def __init__(self, ctx, name="p"):
        self.ctx = ctx
        self.name = name
        self.ops = {e: [] for e in ENGS}
        self.next_dsem = 0
        self.used_dsems = set()

    def _record(self, o, reads, writes):
        deps = set()
        for b in reads:
            deps.update(b.w)
        for b in writes:
            deps.update(b.w)
            deps.update(b.rs)
        if o.eng == "pe":
            deps = {d for d in deps if d.eng != "pe"}
        o.deps = deps
        for b in reads:
            b.rs.append(o)
        for b in writes:
            b.w = [o]
            b.rs = []
        self.ops[o.eng].append(o)
        return o

    def op(self, eng, fn, reads=(), writes=()):
        return self._record(Op(eng, fn), reads, writes)

    def dma_bg(self, queue, out, in_, gid, extra_deps=()):
        ctx = self.ctx
        ctx.bgcount[gid] += 16
        o = Op(queue, lambda e: e.dma_start(out=out, in_=in_), dma=True)
        o.sem = ctx.bgsems[gid]
        o.val = ctx.bgcount[gid]
        o.deps = set(extra_deps)
        ctx.bglast[gid] = o
        self.ops[queue].append(o)
        return o

    def dma(self, queue, out, in_, sbuf, load, extra_deps=(), **kw):
        ctx = self.ctx
        if sbuf.dsem is None:
            assert self.next_dsem < N_DSEM, "out of DMA semaphores"
            sbuf.dsem = self.next_dsem
            self.next_dsem += 1
        i = sbuf.dsem
        self.used_dsems.add(i)
        ctx.dcount[i] += 16
        o = Op(queue, lambda e: e.dma_start(out=out, in_=in_, **kw), dma=True)
        o.sem = ctx.dsems[i]
        o.val = ctx.dcount[i]
        if load:
            self._record(o, [], [sbuf])
        else:
            self._record(o, [sbuf], [])
        if extra_deps:
            o.deps = set(o.deps) | {d for d in extra_deps if d is not None}
        return o

    def emit(self):
        ctx = self.ctx
        nc = ctx.nc
        for e in ENGS:
            for o in self.ops[e]:
                for d in o.deps:
                    d.sig = True
        for e in ENGS:
            for o in self.ops[e]:
                if o.dma:
                    continue
                if o.sig:
                    ctx.ecount[e] += 1
                    o.sem = ctx.esem[e]
                    o.val = ctx.ecount[e]
        final_d = [(ctx.dsems[i], ctx.dcount[i]) for i in sorted(self.used_dsems)]
        ops = self.ops
        engmap = {"sp": "sync", "act": "scalar", "dve": "vector", "pool": "gpsimd", "pe": "tensor"}

        def run(ename):
            def body(e):
                waited = {}
                for o in ops[ename]:
                    for d in o.deps:
                        k = id(d.sem)
                        if waited.get(k, -1) >= d.val:
                            continue
                        e.wait_ge(d.sem, d.val)
                        waited[k] = d.val
                    ins = o.fn(e)
                    if o.sig:
                        ins.then_inc(o.sem, 16 if o.dma else 1)
                if ename == "sp":
                    for (s, v) in final_d:
                        if v > 0 and waited.get(id(s), -1) < v:
                            e.wait_ge(s, v)
            return body

        with nc.Block(no_gpsimd_drain=True) as block:
            for ename in ENGS:
                if ops[ename] or ename == "sp":
                    getattr(block, engmap[ename])(run(ename))
        ctx.nops += sum(len(v) for v in ops.values())


def build_nc(debug=False, upto=99):
    nc = bass.Bass("TRN2", target_bir_lowering=False)
    IN = lambda n, s: nc.dram_tensor(n, s, F32, kind="ExternalInput").ap()
    xin = IN("xin", [NT, D])
    w_in = IN("w_in", [D, 7168])
    w_pa = IN("w_pa", [D, D])
    w_pb = IN("w_pb", [D, D])
    w_o = IN("w_o", [D, D])
    w_up = IN("w_up", [D, 2 * DFF])
    w_dn = IN("w_dn", [DFF, D])
    rg_wa = IN("rg_wa", [8, 128, 128])
    rg_wx = IN("rg_wx", [8, 128, 128])
    g1 = IN("g1", [D])
    g2 = IN("g2", [D])
    g3 = IN("g3", [D])
    par_rnn = IN("par_rnn", [128, 8, 8])
    par_ffn = IN("par_ffn", [128, 4, 24])
    lamv = IN("lamv", [4, 64])
    sublg_in = IN("sublg", [128, 1])
    flags = IN("flags", [128, 2])
    out = nc.dram_tensor("out", [NOWN, D], F32, kind="ExternalOutput").ap()

    def SCR(n, s, dt):
        if debug:
            return nc.dram_tensor(n, s, dt, kind="ExternalOutput").ap()
        return nc.dram_tensor(n, s, dt).ap()
    HT = SCR("HT", [8, 128, NT], BF16)
    KT = SCR("KT", [8, 128, NT], BF16)
    VV = SCR("VV", [NT, D], BF16)
    QT = SCR("QT", [8, 128, NQ], BF16)
    YA = SCR("YA", [8, 128, NQ], BF16)
    SG = SCR("SG", [16, 128, NQ], BF16)
    HD = SCR("HD", [8, 128, NQ], F32)
    X1 = SCR("X1", [NQ, D], F32)
    H2T = SCR("H2T", [8, 128, NQ], BF16)
    AT = SCR("AT", [24, 128, NOWN], BF16)

    WB = {"w_in": nc.dram_tensor("wb_in", [D, 7168], BF16).ap(),
          "w_pa": nc.dram_tensor("wb_pa", [D, D], BF16).ap(),
          "w_pb": nc.dram_tensor("wb_pb", [D, D], BF16).ap(),
          "w_o": nc.dram_tensor("wb_o", [D, D], BF16).ap(),
          "w_up": nc.dram_tensor("wb_up", [D, 2 * DFF], BF16).ap(),
          "w_dn": nc.dram_tensor("wb_dn", [DFF, D], BF16).ap(),
          "rg_wa": nc.dram_tensor("wb_wa", [8, 128, 128], BF16).ap(),
          "rg_wx": nc.dram_tensor("wb_wx", [8, 128, 128], BF16).ap()}
    WF = {"w_in": w_in, "w_pa": w_pa, "w_pb": w_pb, "w_o": w_o, "w_up": w_up, "w_dn": w_dn}
    GA, GC, GB, GE, GF1, GF2 = range(6)

    with ExitStack() as gst:
        ctx = Ctx(nc, gst)
        GT = lambda n, s, d: gst.enter_context(nc.sbuf_tensor(n, s, d))
        ident = GT("ident", [128, 128], BF16)
        ones_bf = GT("ones_bf", [128, 512], BF16)
        onesf = GT("onesf", [128, 128], F32)
        ones1f = GT("ones1f", [128, 128], F32)
        masks = GT("masks", [128, 4, 512], BF16)
        flg = GT("flg", [128, 2], F32)
        lamt = GT("lamt", [128, 4], F32)
        sublg = GT("sublg_t", [128, 1], F32)
        prn = GT("prn", [128, 8, 8], F32)
        c12 = GT("c12", [128, 2, 8], F32)
        pff = GT("pff", [128, 4, 24], F32)
        ctxb = flg[:, 0:1]
        ctxf = flg[:, 1:2]

        def cast_bg(P, name, gid, r0, r1, c0, c1, extra_deps=()):
            for r in range(r0, r1, 128):
                P.dma_bg("pool", WB[name][r:r + 128, c0:c1], WF[name][r:r + 128, c0:c1], gid, extra_deps=extra_deps)

        def load_w(P, dst, name, gid, row0, col0, ncols, kc_n, buf):
            v = WB[name][row0:row0 + 128 * kc_n, :].rearrange("(kc p) c -> p kc c", p=128)
            for kc in range(0, kc_n, 2):
                P.dma("sp", dst[:, kc:kc + 2, :], v[:, kc:kc + 2, col0:col0 + ncols], buf, True,
                      extra_deps=[ctx.bglast[gid]])

        stWA = ExitStack()
        wk = stWA.enter_context(nc.sbuf_tensor("wk", [128, 8, 1024], BF16, side="right"))
        wv = stWA.enter_context(nc.sbuf_tensor("wv", [128, 8, 1024], BF16, side="right"))
        with ExitStack() as st:
            T = lambda n, s, d: st.enter_context(nc.sbuf_tensor(n, s, d))
            lv = T("lv", [128, 4, 64], F32)
            junk = T("junk_s", [128, 64], F32)
            dots = T("dots", [128, 2], F32)
            tmp8 = T("tmp8", [128, 8], F32)
            P = Pass(ctx, "setup")
            B = {k: Buf(k) for k in ["ones", "ident", "onesf", "masks", "flg", "lv", "junk", "dots", "lamt", "sublg", "prn", "c12", "pff", "tmp8"]}
            wv_ = w_in.rearrange("(kc p) c -> p kc c", p=128)
            for kc in range(8):
                P.dma_bg("pool", wk[:, kc, :], wv_[:, kc, 3072:4096], GA)
            for kc in range(8):
                P.dma_bg("pool", wv[:, kc, :], wv_[:, kc, 4096:5120], GA)
            P.op("pool", lambda e: e.memset(ones_bf[:], 1.0), [], [B["ones"]])
            P.op("pool", lambda e: e.memset(onesf[:], 1.0 / 128.0), [], [B["onesf"]])
            P.op("pool", lambda e: e.memset(ones1f[:], 1.0), [], [Buf()])
            P.op("pool", lambda e: e.affine_select(out=ident[:], in_=ones_bf[:, 0:128], pattern=[[-1, 128]],
                                                   compare_op=ALU.is_equal, fill=0.0, base=0, channel_multiplier=1),
                 [B["ones"]], [B["ident"]])
            for j in range(4):
                P.op("pool", lambda e, j=j: e.affine_select(out=masks[:, j, :], in_=ones_bf[:], pattern=[[1, 512]],
                                                            compare_op=ALU.is_ge, fill=0.0, base=-128 * j,
                                                            channel_multiplier=-1),
                     [B["ones"]], [B["masks"]])
            P.dma("sp", flg[:], flags, B["flg"], True)
            P.dma("sp", prn[:], par_rnn, B["prn"], True)
            P.dma("sp", pff[:], par_ffn, B["pff"], True)
            P.dma("sp", sublg[:], sublg_in, B["sublg"], True)
            for i in range(4):
                P.dma("sp", lv[:, i, :], lamv[i, :].partition_broadcast(128), B["lv"], True)
            for i in range(2):
                P.op("dve", lambda e, i=i: e.scalar_tensor_tensor(out=junk[:], in0=lv[:, 2 * i, :], scalar=1.0,
                                                                  in1=lv[:, 2 * i + 1, :], op0=ALU.mult, op1=ALU.mult,
                                                                  accum_out=dots[:, i:i + 1]),
                     [B["lv"]], [B["junk"], B["dots"]])
            P.op("act", lambda e: e.activation(out=dots[:], in_=dots[:], func=AF.Exp), [B["dots"]], [B["dots"]])
            P.op("dve", lambda e: e.scalar_tensor_tensor(out=lamt[:, 0:1], in0=dots[:, 0:1], scalar=LAMBDA_INIT,
                                                         in1=dots[:, 1:2], op0=ALU.add, op1=ALU.subtract),
                 [B["dots"]], [B["lamt"]])
            P.op("dve", lambda e: e.tensor_scalar(out=lamt[:, 1:2], in0=lamt[:, 0:1], scalar1=-1.0, scalar2=None,
                                                  op0=ALU.mult), [B["lamt"]], [B["lamt"]])
            P.op("dve", lambda e: e.tensor_scalar(out=sublg[:], in0=sublg[:], scalar1=(1.0 - LAMBDA_INIT), scalar2=None,
                                                  op0=ALU.mult), [B["sublg"]], [B["sublg"]])
            P.op("act", lambda e: e.activation(out=tmp8[:], in_=prn[:, 7, :], func=AF.Exp, scale=-1.0),
                 [B["prn"]], [B["tmp8"]])
            P.op("act", lambda e: e.activation(out=tmp8[:], in_=tmp8[:], func=AF.Ln, bias=1.0),
                 [B["tmp8"]], [B["tmp8"]])
            P.op("dve", lambda e: e.tensor_scalar(out=c12[:, 0, :], in0=tmp8[:], scalar1=-8.0, scalar2=None, op0=ALU.mult),
                 [B["tmp8"]], [B["c12"]])
            P.op("dve", lambda e: e.tensor_scalar(out=c12[:, 1, :], in0=tmp8[:], scalar1=-16.0, scalar2=None, op0=ALU.mult),
                 [B["tmp8"]], [B["c12"]])
            P.emit()

        def rms_T(P, xt, Bx, gb, Bg, junk, Bjunk, st4, Bst4, hn, Bhn, pT, BpT):
            P.op("dve", lambda e: e.scalar_tensor_tensor(out=junk[:], in0=xt, scalar=1.0, in1=xt, op0=ALU.mult,
                                                         op1=ALU.mult, accum_out=st4[:, 0:1]),
                 [Bx], [Bjunk, Bst4])
            P.op("dve", lambda e: e.tensor_scalar(out=st4[:, 1:2], in0=st4[:, 0:1], scalar1=1.0 / D, scalar2=EPS,
                                                  op0=ALU.mult, op1=ALU.add), [Bst4], [Bst4])
            P.op("act", lambda e: e.activation(out=st4[:, 2:3], in_=st4[:, 1:2], func=AF.Sqrt), [Bst4], [Bst4])
            P.op("dve", lambda e: e.reciprocal(out=st4[:, 3:4], in_=st4[:, 2:3]), [Bst4], [Bst4])
            P.op("dve", lambda e: e.scalar_tensor_tensor(out=hn[:], in0=xt, scalar=st4[:, 3:4], in1=gb[:],
                                                         op0=ALU.mult, op1=ALU.mult), [Bx, Bst4, Bg], [Bhn])
            if pT is not None:
                for c in range(8):
                    P.op("pe", lambda e, c=c: e.transpose(out=pT[:, c, :], in_=hn[:, 128 * c:128 * (c + 1)],
                                                          identity=ident[:]), [Bhn], [BpT])

        def WT(stk, name, shape, side):
            return stk.enter_context(nc.sbuf_tensor(name, shape, BF16, side=side))


        def prefetch_A(P):
            load_w(P, wk, "w_in", GA, 0, 2048 + 1024, 1024, 8, Buf())
            load_w(P, wv, "w_in", GA, 0, 2048 + 2048, 1024, 8, Buf())

        def mm8(P, ps, lhs_fn, rhs_fn, reads, Bps, n=8):
            for k in range(n):
                P.op("pe", lambda e, k=k: e.matmul(ps, lhsT=lhs_fn(k), rhs=rhs_fn(k), start=(k == 0), stop=(k == n - 1)),
                     reads, [Bps])

        evac_rr = [0]

        def evac(P, out_ap, in_ap, reads, writes):
            evac_rr[0] += 1
            if evac_rr[0] % 2 == 0:
                P.op("act", lambda e: e.activation(out=out_ap, in_=in_ap, func=AF.Copy), reads, writes)
            else:
                P.op("dve", lambda e: e.tensor_copy(out=out_ap, in_=in_ap), reads, writes)

        stWC = ExitStack()
        wq = WT(stWC, "wq", [128, 8, 1024], "left")
        wg = WT(stWC, "wg", [128, 8, 2048], "left")

        def prefetch_C(P):
            load_w(P, wq, "w_in", GC, 0, 2048, 1024, 8, Buf())
            load_w(P, wg, "w_in", GC, 0, 5120, 2048, 8, Buf())

        if upto >= 1:
            with ExitStack() as st:
                T = lambda n, s, d: st.enter_context(nc.sbuf_tensor(n, s, d))
                PT = lambda n, s, d: st.enter_context(nc.psum_tensor(n, s, d))
                gb = T("gb", [128, D], F32)
                xt = [T("xt%d" % i, [128, D], F32) for i in range(3)]
                junk = T("junk", [128, D], F32)
                st4 = [T("st4_%d" % i, [128, 4], F32) for i in range(2)]
                hn = [T("hn%d" % i, [128, D], BF16) for i in range(2)]
                pT = [PT("pT%d" % i, [128, 8, 128], BF16) for i in range(2)]
                hts = [T("hts%d" % i, [128, 8, 512], BF16) for i in range(2)]
                kts = [T("kts%d" % i, [128, 8, 512], BF16) for i in range(2)]
                vs = [T("vs%d" % i, [128, 1024], BF16) for i in range(2)]
                ps = [PT("psA%d" % i, [128, 512], F32) for i in range(4)]
                P = Pass(ctx, "p0A")
                Bgb, Bjunk, Bwk, Bwv = Buf(), Buf(), Buf(), Buf()
                Bxt = [Buf() for _ in range(3)]
                Bst4 = [Buf() for _ in range(2)]
                Bhn = [Buf() for _ in range(2)]
                BpT = [Buf() for _ in range(2)]
                Bhts = [[Buf() for _ in range(4)] for _ in range(2)]
                Bkts = [Buf(), Buf()]
                Bvs = [Buf(), Buf()]
                Bps = [Buf() for _ in range(4)]
                P.dma("sp", gb[:], g1.partition_broadcast(128), Bgb, True)
                NTILE = NT // 128
                P.dma("sp", xt[0][:], xin[0:128, :], Bxt[0], True)
                P.dma("sp", xt[1][:], xin[128:256, :], Bxt[1], True)
                Bwk.w = [ctx.bglast[GA]]
                Bwv.w = [ctx.bglast[GA]]
                cast_bg(P, "w_in", GC, 0, D, 2048, 3072)
                cast_bg(P, "w_in", GC, 0, D, 5120, 7168)
                cast_bg(P, "w_in", GB, 0, D, 0, 2048)
                P.dma_bg("pool", WB["rg_wa"], rg_wa, GB)
                P.dma_bg("pool", WB["rg_wx"], rg_wx, GB)
                cast_bg(P, "w_pa", GE, 0, D, 0, D)
                cast_bg(P, "w_pb", GE, 0, D, 0, D)
                cast_bg(P, "w_o", GE, 0, D, 0, D)

                def norm_a(i):
                    if i >= NTILE:
                        return
                    s3, s2 = i % 3, i % 2
                    if i + 2 < NTILE:
                        P.dma("sp", xt[(i + 2) % 3][:], xin[128 * (i + 2):128 * (i + 3), :], Bxt[(i + 2) % 3], True)
                    rms_T(P, xt[s3][:], Bxt[s3], gb, Bgb, junk, Bjunk, st4[s2], Bst4[s2], hn[s2], Bhn[s2], None, None)

                def norm_sub(t, q):
                    S = t % 2
                    i = 4 * t + q
                    s2 = i % 2
                    for c in range(8):
                        P.op("pe", lambda e, c=c: e.transpose(out=pT[s2][:, c, :], in_=hn[s2][:, 128 * c:128 * (c + 1)],
                                                              identity=ident[:]), [Bhn[s2]], [BpT[s2]])
                    P.op("act", lambda e: e.activation(out=hts[S][:, :, 128 * q:128 * (q + 1)], in_=pT[s2][:], func=AF.Copy),
                         [BpT[s2]], [Bhts[S][q]])
                    norm_a(i + 2)
                    if q == 3:
                        t0 = 512 * t
                        o = P.dma("sp", HT[:, :, t0:t0 + 512].rearrange("c p t -> p c t"), hts[S][:], Bhts[S][0], False)
                        for qq in range(1, 4):
                            o.deps = set(o.deps) | set(Bhts[S][qq].w)
                            Bhts[S][qq].rs.append(o)

                cnt = {"pi": 0, "vi": 0}

                def kv_part(i, p):
                    s = i % 2
                    for hd in (2 * p, 2 * p + 1):
                        b = cnt["pi"] % 4
                        cnt["pi"] += 1
                        mm8(P, ps[b][:], lambda k, hd=hd: wk[:, k, 128 * hd:128 * (hd + 1)], lambda k: hts[s][:, k, :],
                            [Bwk] + Bhts[s], Bps[b])
                        evac(P, kts[s][:, hd, :], ps[b][:], [Bps[b]], [Bkts[s]])
                    if p == 3:
                        P.dma("sp", KT[:, :, 512 * i:512 * (i + 1)].rearrange("h p t -> p h t"), kts[s][:], Bkts[s], False)
                    j = p
                    v = cnt["vi"] % 2
                    cnt["vi"] += 1
                    for half in range(2):
                        b = cnt["pi"] % 4
                        cnt["pi"] += 1
                        mm8(P, ps[b][:], lambda k: hts[s][:, k, 128 * j:128 * (j + 1)],
                            lambda k, half=half: wv[:, k, 512 * half:512 * (half + 1)], [Bwv] + Bhts[s], Bps[b])
                        evac(P, vs[v][:, 512 * half:512 * (half + 1)], ps[b][:], [Bps[b]], [Bvs[v]])
                    r0 = 512 * i + 128 * j
                    P.dma("sp", VV[r0:r0 + 128, :], vs[v][:], Bvs[v], False)

                norm_a(0)
                norm_a(1)
                for q in range(4):
                    norm_sub(0, q)
                for t in range(16):
                    if t == 2:
                        prefetch_C(P)
                    for p in range(4):
                        if t + 1 < 16:
                            norm_sub(t + 1, p)
                        kv_part(t, p)
                P.emit()

        stWA.close()
        stWB = ExitStack()
        wxr = WT(stWB, "wxr", [128, 8, 1024], "right")
        wgr = WT(stWB, "wgr", [128, 8, 1024], "right")
        wa = WT(stWB, "wa", [128, 8, 128], "right")
        wx = WT(stWB, "wx", [128, 8, 128], "right")

        def prefetch_B(P):
            load_w(P, wxr, "w_in", GB, 0, 0, 1024, 8, Buf())
            load_w(P, wgr, "w_in", GB, 0, 1024, 1024, 8, Buf())
            P.dma("sp", wa[:], WB["rg_wa"].rearrange("n i j -> i n j"), Buf(), True, extra_deps=[ctx.bglast[GB]])
            P.dma("sp", wx[:], WB["rg_wx"].rearrange("n i j -> i n j"), Buf(), True, extra_deps=[ctx.bglast[GB]])

        if upto >= 3:
            with ExitStack() as st:
                T = lambda n, s, d: st.enter_context(nc.sbuf_tensor(n, s, d))
                PT = lambda n, s, d: st.enter_context(nc.psum_tensor(n, s, d))
                htt = [T("httC%d" % i, [128, 8, 512], BF16) for i in range(2)]
                qs = [T("qs%d" % i, [128, 8, 512], BF16) for i in range(2)]
                sgs = [T("sgs%d" % i, [128, 16, 512], BF16) for i in range(2)]
                ps = [PT("psC%d" % i, [128, 512], F32) for i in range(4)]
                P = Pass(ctx, "pC")
                Bwq, Bwg = Buf(), Buf()
                Bhtt = [Buf(), Buf()]
                Bqs = [Buf(), Buf()]
                Bsgs = [Buf(), Buf()]
                Bps = [Buf() for _ in range(4)]
                cast_bg(P, "w_up", GF1, 0, D, 0, 2 * DFF)
                g0, n = CH[0]
                P.dma("sp", htt[0][:, :, 0:n], HT[:, :, g0:g0 + n].rearrange("c p t -> p c t"), Bhtt[0], True)
                pi = 0
                for ci, (g0, n) in enumerate(CH):
                    s = ci % 2
                    loc = g0 - Q0
                    if ci + 1 < len(CH):
                        g1_, n1 = CH[ci + 1]
                        P.dma("sp", htt[1 - s][:, :, 0:n1], HT[:, :, g1_:g1_ + n1].rearrange("c p t -> p c t"), Bhtt[1 - s], True)
                    if ci == 2:
                        prefetch_B(P)
                    for hd in range(8):
                        b = pi % 4
                        pi += 1
                        mm8(P, ps[b][:, 0:n], lambda k, hd=hd: wq[:, k, 128 * hd:128 * (hd + 1)],
                            lambda k, s=s, n=n: htt[s][:, k, 0:n], [Bwq, Bhtt[s]], Bps[b])
                        evac(P, qs[s][:, hd, 0:n], ps[b][:, 0:n], [Bps[b]], [Bqs[s]])
                    P.dma("sp", QT[:, :, loc:loc + n].rearrange("h p t -> p h t"), qs[s][:, :, 0:n], Bqs[s], False)
                    for c in range(16):
                        b = pi % 4
                        pi += 1
                        mm8(P, ps[b][:, 0:n], lambda k, c=c: wg[:, k, 128 * c:128 * (c + 1)],
                            lambda k, s=s, n=n: htt[s][:, k, 0:n], [Bwg, Bhtt[s]], Bps[b])
                        P.op("act", lambda e, c=c, b=b, s=s, n=n: e.activation(out=sgs[s][:, c, 0:n], in_=ps[b][:, 0:n],
                                                                                func=AF.Sigmoid), [Bps[b]], [Bsgs[s]])
                    P.dma("sp", SG[:, :, loc:loc + n].rearrange("h p t -> p h t"), sgs[s][:, :, 0:n], Bsgs[s], False)
                P.emit()

        stWC.close()
        if upto >= 2:
            with ExitStack() as st:
                T = lambda n, s, d: st.enter_context(nc.sbuf_tensor(n, s, d))
                PT = lambda n, s, d: st.enter_context(nc.psum_tensor(n, s, d))
                htt = [T("httB%d" % i, [128, 8, 512], BF16) for i in range(2)]
                xrb = T("xrb", [128, 8, 515], F32)
                xc = T("xc", [128, 8, 512], F32)
                xcb = T("xcb", [128, 8, 512], BF16)
                ra = T("ra", [128, 8, 512], F32)
                a2 = T("a2", [128, 8, 512], F32)
                ib = T("ib", [128, 8, 512], F32)
                hh = a2
                gg = T("gg", [128, 8, 512], BF16)
                ys = [T("ys%d" % i, [128, 8, 512], BF16) for i in range(1)]
                hst = T("hst", [128, 8], F32)
                ps = [PT("psB%d" % i, [128, 512], F32) for i in range(8)]
                P = Pass(ctx, "pB")
                Bwxr, Bwgr, Bwa, Bwx, Bhst = Buf(), Buf(), Buf(), Buf(), Buf()
                Bhtt = [Buf(), Buf()]
                Bxrb = [Buf() for _ in range(8)]
                Bxc = [Buf() for _ in range(8)]
                Bxcb = [Buf() for _ in range(8)]
                Bra = [Buf() for _ in range(8)]
                Ba2 = [Buf() for _ in range(8)]
                Bib = [Buf() for _ in range(8)]
                Bhh = Ba2
                Bgg = [Buf() for _ in range(8)]
                Bys = [[Buf() for _ in range(8)] for _ in range(1)]
                Bps = [Buf() for _ in range(8)]
                Bhs = [Buf() for _ in range(8)]
                cast_bg(P, "w_dn", GF2, 0, DFF, 0, D)
                P.op("pool", lambda e: e.memset(xrb[:, :, 0:3], 0.0), [], Bxrb)
                P.op("pool", lambda e: e.memset(hst[:], 0.0), [], Bhs)
                P.dma("sp", htt[0][:], HT[:, :, 0:512].rearrange("c p t -> p c t"), Bhtt[0], True)
                pi = 0
                yi = 0
                for i in range(16):
                    s = i % 2
                    own = i >= 7
                    if i + 1 < 16:
                        P.dma("sp", htt[1 - s][:], HT[:, :, 512 * (i + 1):512 * (i + 2)].rearrange("c p t -> p c t"),
                              Bhtt[1 - s], True)
                    for ct in range(8):
                        b = pi % 8
                        pi += 1
                        mm8(P, ps[b][:], lambda k, ct=ct: wxr[:, k, 128 * ct:128 * (ct + 1)], lambda k, s=s: htt[s][:, k, :],
                            [Bwxr, Bhtt[s]], Bps[b])
                        P.op("act", lambda e, ct=ct, b=b: e.activation(out=xrb[:, ct, 3:515], in_=ps[b][:], func=AF.Copy),
                             [Bps[b]], [Bxrb[ct]])
                    for ct in range(8):
                        P.op("dve", lambda e, ct=ct: e.tensor_scalar(out=xc[:, ct, :], in0=xrb[:, ct, 0:512],
                                                                     scalar1=prn[:, 0, ct:ct + 1], scalar2=prn[:, 4, ct:ct + 1],
                                                                     op0=ALU.mult, op1=ALU.add), [Bxrb[ct]], [Bxc[ct]])
                    for k in range(1, 4):
                        for ct in range(8):
                            P.op("dve", lambda e, ct=ct, k=k: e.scalar_tensor_tensor(
                                out=xc[:, ct, :], in0=xrb[:, ct, k:k + 512], scalar=prn[:, k, ct:ct + 1], in1=xc[:, ct, :],
                                op0=ALU.mult, op1=ALU.add), [Bxrb[ct], Bxc[ct]], [Bxc[ct]])
                    for ct in range(8):
                        P.op("pool", lambda e, ct=ct: e.tensor_copy(out=xrb[:, ct, 0:3], in_=xrb[:, ct, 512:515]),
                             [Bxrb[ct]], [Bxrb[ct]])
                        P.op("act", lambda e, ct=ct: e.activation(out=xcb[:, ct, :], in_=xc[:, ct, :], func=AF.Copy),
                             [Bxc[ct]], [Bxcb[ct]])
                    gps = []
                    for ct in range(8):
                        b1 = pi % 8
                        pi += 1
                        P.op("pe", lambda e, ct=ct, b1=b1: e.matmul(ps[b1][:], lhsT=wa[:, ct, :], rhs=xcb[:, ct, :],
                                                                     start=True, stop=True), [Bwa, Bxcb[ct]], [Bps[b1]])
                        P.op("act", lambda e, ct=ct, b1=b1: e.activation(out=ra[:, ct, :], in_=ps[b1][:], func=AF.Sigmoid,
                                                                          bias=prn[:, 5, ct:ct + 1]), [Bps[b1]], [Bra[ct]])
                    for ct in range(8):
                        b2 = pi % 8
                        pi += 1
                        P.op("pe", lambda e, ct=ct, b2=b2: e.matmul(ps[b2][:], lhsT=wx[:, ct, :], rhs=xcb[:, ct, :],
                                                                     start=True, stop=True), [Bwx, Bxcb[ct]], [Bps[b2]])
                        P.op("act", lambda e, ct=ct, b2=b2: e.activation(out=ib[:, ct, :], in_=ps[b2][:], func=AF.Sigmoid,
                                                                          bias=prn[:, 6, ct:ct + 1]), [Bps[b2]], [Bib[ct]])
                    for ct in range(8):
                        P.op("act", lambda e, ct=ct: e.activation(out=a2[:, ct, :], in_=ra[:, ct, :], func=AF.Exp,
                                                                  scale=c12[:, 1, ct:ct + 1]), [Bra[ct]], [Ba2[ct]])
                    for ct in range(8):
                        P.op("act", lambda e, ct=ct: e.activation(out=ra[:, ct, :], in_=ra[:, ct, :], func=AF.Exp,
                                                                  scale=c12[:, 0, ct:ct + 1]), [Bra[ct], Ba2[ct]], [Bra[ct]])
                    for ct in range(8):
                        P.op("dve", lambda e, ct=ct: e.tensor_scalar(out=a2[:, ct, :], in0=a2[:, ct, :], scalar1=1.0,
                                                                     scalar2=-1.0, op0=ALU.min, op1=ALU.mult),
                             [Ba2[ct]], [Ba2[ct]])
                        P.op("pool", lambda e, ct=ct: e.tensor_tensor(out=ib[:, ct, :], in0=ib[:, ct, :], in1=xc[:, ct, :],
                                                                      op=ALU.mult), [Bib[ct], Bxc[ct]], [Bib[ct]])
                    for ct in range(8):
                        P.op("act", lambda e, ct=ct: e.activation(out=a2[:, ct, :], in_=a2[:, ct, :], func=AF.Sqrt, bias=1.0),
                             [Ba2[ct]], [Ba2[ct]])
                    for ct in range(8):
                        P.op("dve", lambda e, ct=ct: e.tensor_tensor(out=ib[:, ct, :], in0=ib[:, ct, :], in1=a2[:, ct, :],
                                                                     op=ALU.mult), [Bib[ct], Ba2[ct]], [Bib[ct]])
                    for ct in range(8):
                        P.op("dve", lambda e, ct=ct: e.tensor_tensor_scan(out=hh[:, ct, :], data0=ra[:, ct, :],
                                                                          data1=ib[:, ct, :], initial=hst[:, ct:ct + 1],
                                                                          op0=ALU.mult, op1=ALU.add),
                             [Bra[ct], Bib[ct], Bhs[ct]], [Bhh[ct]])
                        if i == 7:
                            P.op("pool", lambda e, ct=ct: e.tensor_scalar(out=hst[:, ct:ct + 1], in0=hh[:, ct, 511:512],
                                                                          scalar1=ctxf, scalar2=None, op0=ALU.mult),
                                 [Bhh[ct]], [Bhs[ct]])
                        else:
                            P.op("pool", lambda e, ct=ct: e.tensor_copy(out=hst[:, ct:ct + 1], in_=hh[:, ct, 511:512]),
                                 [Bhh[ct]], [Bhs[ct]])
                    if own:
                        y = 0
                        for ct in range(8):
                            b = pi % 8
                            pi += 1
                            mm8(P, ps[b][:], lambda k, ct=ct: wgr[:, k, 128 * ct:128 * (ct + 1)],
                                lambda k, s=s: htt[s][:, k, :], [Bwgr, Bhtt[s]], Bps[b])
                            P.op("act", lambda e, ct=ct, b=b: e.activation(out=gg[:, ct, :], in_=ps[b][:], func=GELU),
                                 [Bps[b]], [Bgg[ct]])
                        for ct in range(8):
                            P.op("pool", lambda e, ct=ct, y=y: e.tensor_tensor(out=ys[y][:, ct, :], in0=hh[:, ct, :],
                                                                               in1=gg[:, ct, :], op=ALU.mult),
                                 [Bhh[ct], Bgg[ct]], [Bys[y][ct]])
                        if i == 7:
                            o = P.dma("sp", YA[:, :, 0:128].rearrange("c p t -> p c t"), ys[y][:, :, 384:512], Bys[y][0], False)
                        else:
                            l0 = 128 + 512 * (i - 8)
                            o = P.dma("sp", YA[:, :, l0:l0 + 512].rearrange("c p t -> p c t"), ys[y][:], Bys[y][0], False)
                        for ct in range(1, 8):
                            o.deps = set(o.deps) | set(Bys[y][ct].w)
                            Bys[y][ct].rs.append(o)
                P.emit()

        stWB.close()
        stWE = ExitStack()
        wpa = WT(stWE, "wpa", [128, 8, 1024], "right")
        wpb = WT(stWE, "wpb", [128, 8, 1024], "right")
        wo = WT(stWE, "wo", [128, 8, 1024], "right")

        def prefetch_E(P):
            load_w(P, wpa, "w_pa", GE, 0, 0, 1024, 8, Buf())
            load_w(P, wpb, "w_pb", GE, 0, 0, 1024, 8, Buf())
            load_w(P, wo, "w_o", GE, 0, 0, 1024, 8, Buf())

        if upto >= 4:
            with ExitStack() as st:
                T = lambda n, s, d: st.enter_context(nc.sbuf_tensor(n, s, d))
                PT = lambda n, s, d: st.enter_context(nc.psum_tensor(n, s, d))
                kth = [T("kth%d" % i, [128, NT], BF16) for i in range(2)]
                vh = [T("vh%d" % i, [128, 64, 128], BF16) for i in range(2)]
                qh = [T("qh%d" % i, [128, NQ], BF16) for i in range(2)]
                pp = [T("pp%d" % i, [128, 2, 512], BF16) for i in range(6)]
                plc = T("plc", [128, 512], F32)
                Bplc = Buf()
                ls = [T("ls%d" % i, [128, 2, 512], F32) for i in range(2)]
                os_ = [T("os%d" % i, [128, 2, 512], F32) for i in range(2)]
                hdt = [T("hdt%d" % i, [128, 512], F32) for i in range(2)]
                accD = T("accD", [128, 512], F32)
                accP = T("accP", [128, 512], F32)
                BaccD, BaccP = Buf(), Buf()
                ps = [PT("psS%d" % i, [128, 2, 512], F32) for i in range(2)]
                po = PT("po", [128, 2, 512], F32)
                pl = PT("pl", [128, 2, 512], F32)
                P = Pass(ctx, "pD")
                Bk = [Buf(), Buf()]
                Bv = [Buf(), Buf()]
                Bq = [Buf(), Buf()]
                Bpp = [[Buf(), Buf()] for _ in range(6)]
                Bs = [Buf(), Buf()]
                Bo = [Buf(), Buf()]
                Bl = [Buf(), Buf()]
                Bls = [[Buf(), Buf()] for _ in range(2)]
                Bos = [[Buf(), Buf()] for _ in range(2)]
                Bhd = [Buf(), Buf()]

                def load_head(hd, s):
                    P.dma("sp", qh[s][:], QT[hd, :, :], Bq[s], True)
                    for q4 in range(4):
                        P.dma("sp", kth[s][:, 2048 * q4:2048 * (q4 + 1)], KT[hd, :, 2048 * q4:2048 * (q4 + 1)], Bk[s], True)
                    vsrc = VV[:, 128 * hd:128 * (hd + 1)].rearrange("(j p) e -> p j e", p=128)
                    for q4 in range(4):
                        P.dma("sp", vh[s][:, 16 * q4:16 * (q4 + 1), :], vsrc[:, 16 * q4:16 * (q4 + 1), :], Bv[s], True)

                LG = 4
                steps = []
                for hd in range(8):
                    for ci, (g0, n) in enumerate(CH):
                        nkt = (g0 + n) // 128
                        for kt in range(nkt):
                            steps.append((hd, ci, kt, nkt))

                def col0(i):
                    hd, ci, kt, nkt = steps[i]
                    g0, n = CH[ci]
                    return max(128 * kt - g0, 0)

                def emit_qk(i):
                    hd, ci, kt, nkt = steps[i]
                    g0, n = CH[ci]
                    s, sb, k0, loc = hd % 2, i % 2, 128 * kt, g0 - Q0
                    c0 = col0(i)
                    P.op("pe", lambda e: e.matmul(ps[sb][:, 0, c0:n], lhsT=kth[s][0:64, k0:k0 + 128],
                                                  rhs=qh[s][0:64, loc + c0:loc + n], start=True, stop=True),
                         [Bk[s], Bq[s]], [Bs[sb]])
                    P.op("pe", lambda e: e.matmul(ps[sb][:, 1, c0:n], lhsT=kth[s][64:128, k0:k0 + 128],
                                                  rhs=qh[s][64:128, loc + c0:loc + n], start=True, stop=True),
                         [Bk[s], Bq[s]], [Bs[sb]])

                def emit_exp(i):
                    hd, ci, kt, nkt = steps[i]
                    g0, n = CH[ci]
                    sb, pb, k0 = i % 2, i % 6, 128 * kt
                    bias = ctxb if kt < 32 else 0.0
                    c0 = col0(i)
                    P.op("act", lambda e: e.activation(out=pp[pb][:, :, c0:n], in_=ps[sb][:, :, c0:n], func=AF.Exp,
                                                       scale=0.125, bias=bias), [Bs[sb]], Bpp[pb])
                    if k0 >= g0:
                        j = (k0 - g0) // 128
                        c1 = c0 + 128
                        P.op("dve", lambda e: e.tensor_tensor(out=pp[pb][:, 0, c0:c1], in0=pp[pb][:, 0, c0:c1],
                                                              in1=masks[:, j, c0:c1], op=ALU.mult), [Bpp[pb][0]], [Bpp[pb][0]])
                        P.op("dve", lambda e: e.tensor_tensor(out=pp[pb][:, 1, c0:c1], in0=pp[pb][:, 1, c0:c1],
                                                              in1=masks[:, j, c0:c1], op=ALU.mult), [Bpp[pb][1]], [Bpp[pb][1]])

                def emit_av(i):
                    hd, ci, kt, nkt = steps[i]
                    g0, n = CH[ci]
                    s, pb = hd % 2, i % 6
                    first, last = kt == 0, kt == nkt - 1
                    c0 = col0(i)
                    for m in range(2):
                        P.op("pe", lambda e, m=m: e.matmul(po[:, m, c0:n], lhsT=vh[s][:, kt, :], rhs=pp[pb][:, m, c0:n],
                                                           start=first, stop=last), [Bv[s], Bpp[pb][m]], [Bo[m]])
                    if kt % LG == LG - 1:
                        for jj in range(LG):
                            slot = (i - (LG - 1) + jj) % 6
                            cj = col0(i - (LG - 1) + jj)
                            P.op("pe", lambda e, jj=jj, slot=slot, cj=cj: e.matmul(
                                pl[32 * jj:32 * (jj + 1), 0, cj:n], lhsT=ones_bf[:, 0:32], rhs=pp[slot][:, 0, cj:n],
                                start=(kt == LG - 1), stop=last, tile_position=(0, 32 * jj)), [Bpp[slot][0]], [Bl[0]])
                    if kt == 0:
                        P.op("dve", lambda e: e.tensor_copy(out=accP[:, 0:n], in_=pp[pb][:, 1, 0:n]), [Bpp[pb][1]], [BaccP])
                    else:
                        P.op("dve", lambda e: e.tensor_tensor(out=accP[:, c0:n], in0=accP[:, c0:n], in1=pp[pb][:, 1, c0:n],
                                                              op=ALU.add), [Bpp[pb][1], BaccP], [BaccP])
                    if last:
                        P.op("pe", lambda e: e.matmul(pl[:, 1, 0:n], lhsT=ones1f[:], rhs=accP[:, 0:n], start=True, stop=True),
                             [BaccP], [Bl[1]])
                        P.op("dve", lambda e: e.tensor_copy(out=plc[:, 0:n], in_=pl[:, 0, 0:n]), [Bl[0]], [Bplc])
                        P.op("pe", lambda e: e.matmul(pl[:, 0, 0:n], lhsT=onesf[:], rhs=plc[:, 0:n], start=True, stop=True),
                             [Bplc], [Bl[0]])

                fin = [0]

                def finalize(i):
                    hd, ci, kt, nkt = steps[i]
                    g0, n = CH[ci]
                    loc = g0 - Q0
                    y = fin[0] % 2
                    fin[0] += 1
                    for m in range(2):
                        P.op("dve", lambda e, m=m: e.tensor_scalar(out=ls[y][:, m, 0:n], in0=pl[:, m, 0:n],
                                                                   scalar1=(4.0 if m == 0 else 1.0), scalar2=1e-30,
                                                                   op0=ALU.mult, op1=ALU.add), [Bl[m]], [Bls[y][m]])
                        P.op("act", lambda e, m=m: e.activation(out=os_[y][:, m, 0:n], in_=po[:, m, 0:n], func=AF.Copy),
                             [Bo[m]], [Bos[y][m]])
                    def part_norm(m):
                        P.op("dve", lambda e: e.reciprocal(out=ls[y][:, m, 0:n], in_=ls[y][:, m, 0:n]),
                             [Bls[y][m]], [Bls[y][m]])
                        P.op("pool", lambda e: e.tensor_tensor(out=os_[y][:, m, 0:n], in0=os_[y][:, m, 0:n],
                                                               in1=ls[y][:, m, 0:n], op=ALU.mult),
                             [Bos[y][m], Bls[y][m]], [Bos[y][m]])

                    def part_out():
                        P.op("dve", lambda e: e.scalar_tensor_tensor(out=hdt[y][:, 0:n], in0=os_[y][:, 1, 0:n],
                                                                     scalar=lamt[:, 1:2], in1=os_[y][:, 0, 0:n],
                                                                     op0=ALU.mult, op1=ALU.add), Bos[y], [Bhd[y]])
                        P.dma("sp", HD[hd, :, loc:loc + n], hdt[y][:, 0:n], Bhd[y], False)

                    deferred.setdefault(i + 3, []).append(lambda: part_norm(0))
                    deferred.setdefault(i + 7, []).append(lambda: part_norm(1))
                    deferred.setdefault(i + 11, []).append(part_out)

                deferred = {}
                load_head(0, 0)
                prefetch_E(P)
                emit_qk(0)
                emit_qk(1)
                for i in range(len(steps)):
                    hd, ci, kt, nkt = steps[i]
                    if ci == 0 and kt == 0 and hd + 1 < 8:
                        load_head(hd + 1, 1 - hd % 2)
                    emit_exp(i)
                    if i + 2 < len(steps):
                        emit_qk(i + 2)
                    for fn in deferred.pop(i, []):
                        fn()
                    emit_av(i)
                    if kt == nkt - 1:
                        finalize(i)
                for j in sorted(deferred):
                    for fn in deferred[j]:
                        fn()
                P.emit()

        if upto >= 5:
            with ExitStack() as st:
                T = lambda n, s, d: st.enter_context(nc.sbuf_tensor(n, s, d))
                PT = lambda n, s, d: st.enter_context(nc.psum_tensor(n, s, d))
                gb = T("gb2", [128, D], F32)
                ya = [T("ya%d" % i, [128, 8, 512], BF16) for i in range(2)]
                hq = [T("hq%d" % i, [128, 8, 512], F32) for i in range(2)]
                yb = [T("yb%d" % i, [128, 8, 512], BF16) for i in range(2)]
                sqt = [T("sqt%d" % i, [128, 512], F32) for i in range(2)]
                lnt = [T("lnt%d" % i, [128, 512], F32) for i in range(2)]
                sg = [T("sg%d" % i, [128, 16, 512], BF16) for i in range(1)]
                t1 = [T("t1_%d" % i, [128, 512], F32) for i in range(2)]
                t2 = [T("t2_%d" % i, [128, 512], F32) for i in range(2)]
                mg = T("mg", [128, 8, 512], BF16)
                xt = [T("xtE%d" % i, [128, D], F32) for i in range(2)]
                x1 = [T("x1_%d" % i, [128, D], F32) for i in range(2)]
                junk = T("junkE", [128, D], F32)
                st4 = [T("st4E%d" % i, [128, 4], F32) for i in range(2)]
                hn = [T("hnE%d" % i, [128, D], BF16) for i in range(2)]
                h2s = [T("h2s%d" % i, [128, 8, 512], BF16) for i in range(1)]
                ppa = [PT("ppa%d" % i, [128, 512], F32) for i in range(2)]
                ppb = [PT("ppb%d" % i, [128, 512], F32) for i in range(2)]
                pw = [PT("pw%d" % i, [128, 512], F32) for i in range(2)]
                pT = [PT("pTE%d" % i, [128, 8, 128], BF16) for i in range(1)]
                pms = PT("pms", [128, 512], F32)
                P = Pass(ctx, "pE")
                Bwpa, Bwpb, Bwo, Bgb, Bjunk = Buf(), Buf(), Buf(), Buf(), Buf()
                Bya, Bhq, Bsg = [Buf(), Buf()], [Buf(), Buf()], [Buf()]
                Byb_ = [[Buf() for _ in range(8)] for _ in range(2)]
                Bsqt, Blnt, Bpms = [Buf(), Buf()], [Buf(), Buf()], Buf()
                Bt1, Bt2 = [Buf(), Buf()], [Buf(), Buf()]
                Bmg = [Buf() for _ in range(8)]
                Bxt, Bx1, Bst4, Bhn = [Buf(), Buf()], [Buf(), Buf()], [Buf(), Buf()], [Buf(), Buf()]
                Bh2s = [[Buf() for _ in range(4)] for _ in range(1)]
                Bppa, Bppb, Bpw, BpT = [Buf(), Buf()], [Buf(), Buf()], [Buf(), Buf()], [Buf()]
                P.dma("sp", gb[:], g2.partition_broadcast(128), Bgb, True)

                def load_chunk(ci):
                    g0, n = CH[ci]
                    loc = g0 - Q0
                    s = ci % 2
                    P.dma("sp", ya[s][:, :, 0:n], YA[:, :, loc:loc + n].rearrange("c p t -> p c t"), Bya[s], True)
                    P.dma("sp", hq[s][:, :, 0:n], HD[:, :, loc:loc + n].rearrange("c p t -> p c t"), Bhq[s], True)

                def load_sg(ci):
                    g0, n = CH[ci]
                    loc = g0 - Q0
                    P.dma("sp", sg[0][:, :, 0:n], SG[:, :, loc:loc + n].rearrange("c p t -> p c t"), Bsg[0], True)

                def sub_ln(ci, heads):
                    g0, n = CH[ci]
                    s = ci % 2
                    for hd in heads:
                        q2 = hd % 2
                        P.op("pool", lambda e, hd=hd, q2=q2: e.tensor_tensor(out=sqt[q2][:, 0:n], in0=hq[s][:, hd, 0:n],
                                                                             in1=hq[s][:, hd, 0:n], op=ALU.mult),
                             [Bhq[s]], [Bsqt[q2]])
                        P.op("pe", lambda e, q2=q2: e.matmul(pms[:, 0:n], lhsT=onesf[:], rhs=sqt[q2][:, 0:n], start=True,
                                                             stop=True), [Bsqt[q2]], [Bpms])
                        P.op("act", lambda e, q2=q2: e.activation(out=lnt[q2][:, 0:n], in_=pms[:, 0:n], func=AF.Ln, bias=1e-5),
                             [Bpms], [Blnt[q2]])
                        P.op("act", lambda e, q2=q2: e.activation(out=lnt[q2][:, 0:n], in_=lnt[q2][:, 0:n], func=AF.Exp,
                                                                  scale=-0.5), [Blnt[q2]], [Blnt[q2]])
                        P.op("dve", lambda e, hd=hd, q2=q2: e.scalar_tensor_tensor(
                            out=yb[s][:, hd, 0:n], in0=hq[s][:, hd, 0:n], scalar=sublg[:, 0:1], in1=lnt[q2][:, 0:n],
                            op0=ALU.mult, op1=ALU.mult), [Bhq[s], Blnt[q2]], [Byb_[s][hd]])

                load_chunk(0)
                ti = 0
                for ci, (g0, n) in enumerate(CH):
                    s = ci % 2
                    loc = g0 - Q0
                    if ci + 1 < len(CH):
                        load_chunk(ci + 1)
                    load_sg(ci)
                    sub_ln(ci, range(8))
                    for c in range(8):
                        b = c % 2
                        mm8(P, ppa[b][:, 0:n], lambda k, c=c: wpa[:, k, 128 * c:128 * (c + 1)],
                            lambda k, s=s, n=n: ya[s][:, k, 0:n], [Bwpa, Bya[s]], Bppa[b])
                        mm8(P, ppb[b][:, 0:n], lambda k, c=c: wpb[:, k, 128 * c:128 * (c + 1)],
                            lambda k, s=s, n=n: yb[s][:, k, 0:n], [Bwpb] + Byb_[s], Bppb[b])
                        P.op("dve", lambda e, b=b, c=c, n=n: e.tensor_tensor(out=t1[b][:, 0:n], in0=ppa[b][:, 0:n],
                                                                            in1=sg[0][:, c, 0:n], op=ALU.mult),
                             [Bppa[b], Bsg[0]], [Bt1[b]])
                        P.op("dve", lambda e, b=b, c=c, n=n: e.tensor_tensor(out=t2[b][:, 0:n], in0=ppb[b][:, 0:n],
                                                                            in1=sg[0][:, 8 + c, 0:n], op=ALU.mult),
                             [Bppb[b], Bsg[0]], [Bt2[b]])
                        P.op("pool", lambda e, b=b, c=c, n=n: e.tensor_tensor(out=mg[:, c, 0:n], in0=t1[b][:, 0:n],
                                                                              in1=t2[b][:, 0:n], op=ALU.add),
                             [Bt1[b], Bt2[b]], [Bmg[c]])
                    S = 0
                    nj = n // 128
                    pend = None

                    def emit_T(u, j):
                        for c in range(8):
                            P.op("pe", lambda e, c=c: e.transpose(out=pT[0][:, c, :], in_=hn[u][:, 128 * c:128 * (c + 1)],
                                                                  identity=ident[:]), [Bhn[u]], [BpT[0]])
                        P.op("act", lambda e: e.activation(out=h2s[S][:, :, 128 * j:128 * (j + 1)], in_=pT[0][:], func=AF.Copy),
                             [BpT[0]], [Bh2s[S][j]])
                    for j in range(n // 128):
                        u = ti % 2
                        ti += 1
                        r0 = g0 + 128 * j
                        P.dma("sp", xt[u][:], xin[r0:r0 + 128, :], Bxt[u], True)
                        for half in range(2):
                            mm8(P, pw[half][:], lambda k, j=j: mg[:, k, 128 * j:128 * (j + 1)],
                                lambda k, half=half: wo[:, k, 512 * half:512 * (half + 1)], [Bwo] + Bmg, Bpw[half])
                            P.op("dve", lambda e, u=u, half=half: e.tensor_tensor(
                                out=x1[u][:, 512 * half:512 * (half + 1)], in0=pw[half][:],
                                in1=xt[u][:, 512 * half:512 * (half + 1)], op=ALU.add), [Bpw[half], Bxt[u]], [Bx1[u]])
                        P.dma("sp", X1[loc + 128 * j:loc + 128 * (j + 1), :], x1[u][:], Bx1[u], False)
                        rms_T(P, x1[u][:], Bx1[u], gb, Bgb, junk, Bjunk, st4[u], Bst4[u], hn[u], Bhn[u], None, None)
                        if pend is not None:
                            emit_T(*pend)
                        pend = (u, j)
                    emit_T(*pend)
                    o = P.dma("sp", H2T[:, :, loc:loc + n].rearrange("c p t -> p c t"), h2s[S][:, :, 0:n], Bh2s[S][0], False)
                    for qq in range(1, n // 128):
                        o.deps = set(o.deps) | set(Bh2s[S][qq].w)
                        Bh2s[S][qq].rs.append(o)
                P.emit()

        stWE.close()
        stWD = ExitStack()
        wd = WT(stWD, "wd", [128, 24, D], "right")

        def prefetch_F2(P):
            for q in range(3):
                load_w(P, wd[:, 8 * q:8 * (q + 1), :], "w_dn", GF2, 1024 * q, 0, 1024, 8, Buf())

        if upto >= 6:
            with ExitStack() as st:
                T = lambda n, s, d: st.enter_context(nc.sbuf_tensor(n, s, d))
                PT = lambda n, s, d: st.enter_context(nc.psum_tensor(n, s, d))
                wu = T("wu", [128, 8, 2 * DFF], BF16)
                h2t = [T("h2t%d" % i, [128, 8, 512], BF16) for i in range(2)]
                carry = T("carry", [128, 24, 2], F32)
                ugb = [T("ugb%d" % i, [128, 514], F32) for i in range(3)]
                cv = [T("cv%d" % i, [128, 512], F32) for i in range(2)]
                ge = [T("ge%d" % i, [128, 512], F32) for i in range(2)]
                ats = [T("ats%d" % i, [128, 24, 512], BF16) for i in range(1)]
                pg = [PT("pg%d" % i, [128, 512], F32) for i in range(3)]
                pv = [PT("pv%d" % i, [128, 512], F32) for i in range(3)]
                P = Pass(ctx, "pF1")
                Bwu = Buf()
                Bwuv = Buf()
                Bh2t = [Buf(), Buf()]
                Bcar = [Buf() for _ in range(24)]
                Bug = [Buf() for _ in range(3)]
                Bcv, Bge = [Buf(), Buf()], [Buf(), Buf()]
                Bats = [[Buf() for _ in range(24)] for _ in range(1)]
                Bpg, Bpv = [Buf() for _ in range(3)], [Buf() for _ in range(3)]
                load_w(P, wu[:, :, 0:DFF], "w_up", GF1, 0, 0, DFF, 8, Bwu)
                load_w(P, wu[:, :, DFF:2 * DFF], "w_up", GF1, 0, DFF, DFF, 8, Bwuv)
                g0, n = CH[0]
                P.dma("sp", h2t[0][:, :, 0:n], H2T[:, :, 0:n].rearrange("c p t -> p c t"), Bh2t[0], True)
                it = 0
                for ci, (g0, n) in enumerate(CH):
                    s = ci % 2
                    loc = g0 - Q0
                    if ci + 1 < len(CH):
                        g1_, n1 = CH[ci + 1]
                        l1 = g1_ - Q0
                        P.dma("sp", h2t[1 - s][:, :, 0:n1], H2T[:, :, l1:l1 + n1].rearrange("c p t -> p c t"), Bh2t[1 - s], True)
                    A = 0
                    if ci == 2:
                        prefetch_F2(P)
                    for fc in range(24):
                        b = it % 3
                        c2 = it % 2
                        it += 1
                        mm8(P, pg[b][:, 0:n], lambda k, fc=fc: wu[:, k, 128 * fc:128 * (fc + 1)],
                            lambda k, s=s, n=n: h2t[s][:, k, 0:n], [Bwu, Bh2t[s]], Bpg[b])
                        if ci == 0:
                            P.op("dve", lambda e, fc=fc, b=b: e.tensor_scalar(out=carry[:, fc, :], in0=pg[b][:, 126:128],
                                                                              scalar1=ctxf, scalar2=None, op0=ALU.mult),
                                 [Bpg[b]], [Bcar[fc]])
                            continue
                        mm8(P, pv[b][:, 0:n], lambda k, fc=fc: wu[:, k, DFF + 128 * fc:DFF + 128 * (fc + 1)],
                            lambda k, s=s, n=n: h2t[s][:, k, 0:n], [Bwuv, Bh2t[s]], Bpv[b])
                        P.op("pool", lambda e, fc=fc, b=b: e.tensor_copy(out=ugb[b][:, 0:2], in_=carry[:, fc, :]),
                             [Bcar[fc]], [Bug[b]])
                        P.op("act", lambda e, b=b: e.activation(out=ugb[b][:, 2:514], in_=pg[b][:], func=AF.Copy),
                             [Bpg[b]], [Bug[b]])
                        P.op("pool", lambda e, fc=fc, b=b: e.tensor_copy(out=carry[:, fc, :], in_=ugb[b][:, 512:514]),
                             [Bug[b]], [Bcar[fc]])
                        P.op("dve", lambda e, fc=fc, b=b, c2=c2: e.tensor_scalar(
                            out=cv[c2][:], in0=ugb[b][:, 0:512], scalar1=pff[:, 0, fc:fc + 1], scalar2=pff[:, 3, fc:fc + 1],
                            op0=ALU.mult, op1=ALU.add), [Bug[b]], [Bcv[c2]])
                        for k in range(1, 3):
                            P.op("dve", lambda e, fc=fc, b=b, c2=c2, k=k: e.scalar_tensor_tensor(
                                out=cv[c2][:], in0=ugb[b][:, k:k + 512], scalar=pff[:, k, fc:fc + 1], in1=cv[c2][:],
                                op0=ALU.mult, op1=ALU.add), [Bug[b], Bcv[c2]], [Bcv[c2]])
                        P.op("act", lambda e, c2=c2: e.activation(out=ge[c2][:], in_=cv[c2][:], func=GELU), [Bcv[c2]], [Bge[c2]])
                        P.op("dve", lambda e, fc=fc, b=b, c2=c2, A=A: e.tensor_tensor(out=ats[A][:, fc, :], in0=pv[b][:],
                                                                                     in1=ge[c2][:], op=ALU.mult),
                             [Bpv[b], Bge[c2]], [Bats[A][fc]])
                        if ci >= 1 and fc % 12 == 11:
                            t0 = g0 - NOWN
                            f0 = fc - 11
                            o = P.dma("sp", AT[f0:f0 + 12, :, t0:t0 + 512].rearrange("c p t -> p c t"), ats[A][:, f0:f0 + 12, :],
                                      Bats[A][f0], False)
                            for f2 in range(f0 + 1, f0 + 12):
                                o.deps = set(o.deps) | set(Bats[A][f2].w)
                                Bats[A][f2].rs.append(o)
                P.emit()

        if upto >= 7:
            with ExitStack() as st:
                T = lambda n, s, d: st.enter_context(nc.sbuf_tensor(n, s, d))
                PT = lambda n, s, d: st.enter_context(nc.psum_tensor(n, s, d))
                gb = T("gb3", [128, D], F32)
                att = [T("att%d" % i, [128, 24, 512], BF16) for i in range(2)]
                x1t = [T("x1t%d" % i, [128, D], F32) for i in range(2)]
                x2 = [T("x2_%d" % i, [128, D], F32) for i in range(2)]
                junk = T("junkF", [128, D], F32)
                st4 = [T("st4F%d" % i, [128, 4], F32) for i in range(2)]
                ot = [T("ot%d" % i, [128, D], F32) for i in range(2)]
                pd = [PT("pd%d" % i, [128, 512], F32) for i in range(4)]
                P = Pass(ctx, "pF2")
                Bwd, Bgb, Bjunk = Buf(), Buf(), Buf()
                Batt, Bx1t, Bx2, Bst4, Bot = [Buf(), Buf()], [Buf(), Buf()], [Buf(), Buf()], [Buf(), Buf()], [Buf(), Buf()]
                Bpd = [Buf() for _ in range(4)]
                P.dma("sp", gb[:], g3.partition_broadcast(128), Bgb, True)
                P.dma("sp", att[0][:], AT[:, :, 0:512].rearrange("c p t -> p c t"), Batt[0], True)
                ti = 0
                pi = 0
                for ci in range(8):
                    s = ci % 2
                    if ci + 1 < 8:
                        P.dma("sp", att[1 - s][:], AT[:, :, 512 * (ci + 1):512 * (ci + 2)].rearrange("c p t -> p c t"),
                              Batt[1 - s], True)
                    for j in range(4):
                        u = ti % 2
                        ti += 1
                        t0 = 512 * ci + 128 * j
                        P.dma("sp", x1t[u][:], X1[128 + t0:128 + t0 + 128, :], Bx1t[u], True)
                        for half in range(2):
                            b = pi % 4
                            pi += 1
                            mm8(P, pd[b][:], lambda k, s=s, j=j: att[s][:, k, 128 * j:128 * (j + 1)],
                                lambda k, half=half: wd[:, k, 512 * half:512 * (half + 1)], [Bwd, Batt[s]], Bpd[b], n=24)
                            P.op("dve", lambda e, u=u, half=half, b=b: e.tensor_tensor(
                                out=x2[u][:, 512 * half:512 * (half + 1)], in0=pd[b][:],
                                in1=x1t[u][:, 512 * half:512 * (half + 1)], op=ALU.add), [Bpd[b], Bx1t[u]], [Bx2[u]])
                        rms_T(P, x2[u][:], Bx2[u], gb, Bgb, junk, Bjunk, st4[u], Bst4[u], ot[u], Bot[u], None, None)
                        P.dma("sp", out[t0:t0 + 128, :], ot[u][:], Bot[u], False)
                P.emit()
    nc._mk_nops = ctx.nops
    return nc


def make_in_maps(inputs):
    f = lambda a: np.ascontiguousarray(np.asarray(a, dtype=np.float32))
    x = f(inputs["x"])
    B, S, _ = x.shape
    half = S // 2
    shared = {
        "w_in": f(inputs["w_in"][0]),
        "w_pa": f(inputs["w_proj_rnn"][0]),
        "w_pb": f(inputs["w_proj_attn"][0]),
        "w_o": f(inputs["w_out"][0]),
        "w_up": f(inputs["w_up"][0]),
        "w_dn": f(inputs["w_down"][0]),
        "rg_wa": f(inputs["rg_wa"][0]),
        "rg_wx": f(inputs["rg_wx"][0]),
        "g1": f(inputs["attn_norm_g"][0]),
        "g2": f(inputs["mlp_norm_g"][0]),
        "g3": f(inputs["final_norm_g"]),
        "lamv": f(np.stack([inputs["lam_q1"][0], inputs["lam_k1"][0], inputs["lam_q2"][0], inputs["lam_k2"][0]])),
        "sublg": f(np.asarray(inputs["subln_g"][0]).reshape(128, 1)),
    }
    cw = np.asarray(inputs["rnn_conv_w"][0], np.float32)
    rows = [cw[0], cw[1], cw[2], cw[3], inputs["rnn_conv_b"][0], inputs["rg_ba"][0], inputs["rg_bx"][0],
            inputs["rg_lambda"][0]]
    pr = np.stack([np.asarray(r, np.float32).reshape(8, 128) for r in rows])
    shared["par_rnn"] = f(pr.transpose(2, 0, 1))
    fw_ = np.asarray(inputs["ffn_conv_w"][0], np.float32)
    rows = [fw_[0], fw_[1], fw_[2], inputs["ffn_conv_b"][0]]
    pf = np.stack([np.asarray(r, np.float32).reshape(24, 128) for r in rows])
    shared["par_ffn"] = f(pf.transpose(2, 0, 1))
    in_maps = []
    for b in range(B):
        for h in range(2):
            m = dict(shared)
            xi = np.zeros((NT, D), np.float32)
            if h == 1:
                xi[:] = x[b]
            else:
                xi[half:] = x[b, :half]
            m["xin"] = xi
            fl = np.zeros((128, 2), np.float32)
            fl[:, 0] = 0.0 if h == 1 else -30000.0
            fl[:, 1] = 1.0 if h == 1 else 0.0
            m["flags"] = fl
            in_maps.append(m)
    return in_maps


_NC_CACHE = {}


def kernel(**inputs):
    in_maps = make_in_maps(inputs)
    if "nc" not in _NC_CACHE:
        _NC_CACHE["nc"] = build_nc()
    nc = _NC_CACHE["nc"]
    res = run_bass_kernel_spmd(nc, in_maps, core_ids=list(range(8)))
    x = np.asarray(inputs["x"])
    B, S, _ = x.shape
    outp = np.empty((B, S, D), np.float32)
    k = 0
    for b in range(B):
        for h in range(2):
            outp[b, h * NOWN:(h + 1) * NOWN] = res.results[k]["out"]
            k += 1
    return outp
```

```python
import math
import os
from contextlib import ExitStack

import numpy as np
import concourse.bass as bass
import concourse.mybir as mybir
from concourse.bass_utils import run_bass_kernel_spmd

F32 = mybir.dt.float32
BF16 = mybir.dt.bfloat16
AF = mybir.ActivationFunctionType
ALU = mybir.AluOpType

D = 1024
NT = 8192
NOWN = 4096
Q0 = 3968
NQ = NT - Q0
CH = [(Q0, 128)] + [(NOWN + 512 * i, 512) for i in range(8)]
EPS = 1e-6
LAMBDA_INIT = 0.8 - 0.6 * math.exp(-0.3 * 0)
DFF = 3072
GELU = AF.Gelu_apprx_tanh

ENGS = ("sp", "act", "dve", "pool", "pe")
N_DSEM = 44
N_BG = 6


class Buf:
    __slots__ = ("name", "w", "rs", "dsem")

    def __init__(self, name="b"):
        self.name = name
        self.w = []
        self.rs = []
        self.dsem = None


class Op:
    __slots__ = ("eng", "fn", "deps", "sig", "sem", "val", "dma")

    def __init__(self, eng, fn, dma=False):
        self.eng = eng
        self.fn = fn
        self.deps = ()
        self.sig = dma
        self.sem = None
        self.val = 0
        self.dma = dma


class Ctx:
    def __init__(self, nc, stack):
        self.nc = nc
        self.esem = {e: stack.enter_context(nc.semaphore("es_" + e)) for e in ENGS}
        self.ecount = {e: 0 for e in ENGS}
        self.dsems = [stack.enter_context(nc.semaphore("ds%d" % i)) for i in range(N_DSEM)]
        self.dcount = [0] * N_DSEM
        self.nops = 0
        self.bgsems = [stack.enter_context(nc.semaphore("bg%d" % i)) for i in range(N_BG)]
        self.bgcount = [0] * N_BG
        self.bglast = [None] * N_BG


class Pass:
    def __init__(self, ctx, name="p"):
        self.ctx = ctx
        self.name = name
        self.ops = {e: [] for e in ENGS}
        self.next_dsem = 0
        self.used_dsems = set()

    def _record(self, o, reads, writes):
        deps = set()
        for b in reads:
            deps.update(b.w)
        for b in writes:
            deps.update(b.w)
            deps.update(b.rs)
        if o.eng == "pe":
            deps = {d for d in deps if d.eng != "pe"}
        o.deps = deps
        for b in reads:
            b.rs.append(o)
        for b in writes:
            b.w = [o]
            b.rs = []
        self.ops[o.eng].append(o)
        return o

    def op(self, eng, fn, reads=(), writes=()):
        return self._record(Op(eng, fn), reads, writes)

    def dma_bg(self, queue, out, in_, gid, extra_deps=()):
        ctx = self.ctx
        ctx.bgcount[gid] += 16
        o = Op(queue, lambda e: e.dma_start(out=out, in_=in_), dma=True)
        o.sem = ctx.bgsems[gid]
        o.val = ctx.bgcount[gid]
        o.deps = set(extra_deps)
        ctx.bglast[gid] = o
        self.ops[queue].append(o)
        return o

    def dma(self, queue, out, in_, sbuf, load, extra_deps=(), **kw):
        ctx = self.ctx
        if sbuf.dsem is None:
            assert self.next_dsem < N_DSEM, "out of DMA semaphores"
            sbuf.dsem = self.next_dsem
            self.next_dsem += 1
        i = sbuf.dsem
        self.used_dsems.add(i)
        ctx.dcount[i] += 16
        o = Op(queue, lambda e: e.dma_start(out=out, in_=in_, **kw), dma=True)
        o.sem = ctx.dsems[i]
        o.val = ctx.dcount[i]
        if load:
            self._record(o, [], [sbuf])
        else:
            self._record(o, [sbuf], [])
        if extra_deps:
            o.deps = set(o.deps) | {d for d in extra_deps if d is not None}
        return o

    def emit(self):
        ctx = self.ctx
        nc = ctx.nc
        for e in ENGS:
            for o in self.ops[e]:
                for d in o.deps:
                    d.sig = True
        for e in ENGS:
            for o in self.ops[e]:
                if o.dma:
                    continue
                if o.sig:
                    ctx.ecount[e] += 1
                    o.sem = ctx.esem[e]
                    o.val = ctx.ecount[e]
        final_d = [(ctx.dsems[i], ctx.dcount[i]) for i in sorted(self.used_dsems)]
        ops = self.ops
        engmap = {"sp": "sync", "act": "scalar", "dve": "vector", "pool": "gpsimd", "pe": "tensor"}

        def run(ename):
            def body(e):
                waited = {}
                for o in ops[ename]:
                    for d in o.deps:
                        k = id(d.sem)
                        if waited.get(k, -1) >= d.val:
                            continue
                        e.wait_ge(d.sem, d.val)
                        waited[k] = d.val
                    ins = o.fn(e)
                    if o.sig:
                        ins.then_inc(o.sem, 16 if o.dma else 1)
                if ename == "sp":
                    for (s, v) in final_d:
                        if v > 0 and waited.get(id(s), -1) < v:
                            e.wait_ge(s, v)
            return body

        with nc.Block(no_gpsimd_drain=True) as block:
            for ename in ENGS:
                if ops[ename] or ename == "sp":
                    getattr(block, engmap[ename])(run(ename))
        ctx.nops += sum(len(v) for v in ops.values())


def build_nc(debug=False, upto=99):
    nc = bass.Bass("TRN2", target_bir_lowering=False)
    IN = lambda n, s: nc.dram_tensor(n, s, F32, kind="ExternalInput").ap()
    xin = IN("xin", [NT, D])
    w_in = IN("w_in", [D, 7168])
    w_pa = IN("w_pa", [D, D])
    w_pb = IN("w_pb", [D, D])
    w_o = IN("w_o", [D, D])
    w_up = IN("w_up", [D, 2 * DFF])
    w_dn = IN("w_dn", [DFF, D])
    rg_wa = IN("rg_wa", [8, 128, 128])
    rg_wx = IN("rg_wx", [8, 128, 128])
    g1 = IN("g1", [D])
    g2 = IN("g2", [D])
    g3 = IN("g3", [D])
    par_rnn = IN("par_rnn", [128, 8, 8])
    par_ffn = IN("par_ffn", [128, 4, 24])
    lamv = IN("lamv", [4, 64])
    sublg_in = IN("sublg", [128, 1])
    flags = IN("flags", [128, 2])
    out = nc.dram_tensor("out", [NOWN, D], F32, kind="ExternalOutput").ap()

    def SCR(n, s, dt):
        if debug:
            return nc.dram_tensor(n, s, dt, kind="ExternalOutput").ap()
        return nc.dram_tensor(n, s, dt).ap()
    HT = SCR("HT", [8, 128, NT], BF16)
    KT = SCR("KT", [8, 128, NT], BF16)
    VV = SCR("VV", [NT, D], BF16)
    QT = SCR("QT", [8, 128, NQ], BF16)
    YA = SCR("YA", [8, 128, NQ], BF16)
    SG = SCR("SG", [16, 128, NQ], BF16)
    HD = SCR("HD", [8, 128, NQ], F32)
    X1 = SCR("X1", [NQ, D], F32)
    H2T = SCR("H2T", [8, 128, NQ], BF16)
    AT = SCR("AT", [24, 128, NOWN], BF16)

    WB = {"w_in": nc.dram_tensor("wb_in", [D, 7168], BF16).ap(),
          "w_pa": nc.dram_tensor("wb_pa", [D, D], BF16).ap(),
          "w_pb": nc.dram_tensor("wb_pb", [D, D], BF16).ap(),
          "w_o": nc.dram_tensor("wb_o", [D, D], BF16).ap(),
          "w_up": nc.dram_tensor("wb_up", [D, 2 * DFF], BF16).ap(),
          "w_dn": nc.dram_tensor("wb_dn", [DFF, D], BF16).ap(),
          "rg_wa": nc.dram_tensor("wb_wa", [8, 128, 128], BF16).ap(),
          "rg_wx": nc.dram_tensor("wb_wx", [8, 128, 128], BF16).ap()}
    WF = {"w_in": w_in, "w_pa": w_pa, "w_pb": w_pb, "w_o": w_o, "w_up": w_up, "w_dn": w_dn}
    GA, GC, GB, GE, GF1, GF2 = range(6)

    with ExitStack() as gst:
        ctx = Ctx(nc, gst)
        GT = lambda n, s, d: gst.enter_context(nc.sbuf_tensor(n, s, d))
        ident = GT("ident", [128, 128], BF16)
        ones_bf = GT("ones_bf", [128, 512], BF16)
        onesf = GT("onesf", [128, 128], F32)
        ones1f = GT("ones1f", [128, 128], F32)
        masks = GT("masks", [128, 4, 512], BF16)
        flg = GT("flg", [128, 2], F32)
        lamt = GT("lamt", [128, 4], F32)
        sublg = GT("sublg_t", [128, 1], F32)
        prn = GT("prn", [128, 8, 8], F32)
        c12 = GT("c12", [128, 2, 8], F32)
        pff = GT("pff", [128, 4, 24], F32)
        ctxb = flg[:, 0:1]
        ctxf = flg[:, 1:2]

        def cast_bg(P, name, gid, r0, r1, c0, c1, extra_deps=()):
            for r in range(r0, r1, 128):
                P.dma_bg("pool", WB[name][r:r + 128, c0:c1], WF[name][r:r + 128, c0:c1], gid, extra_deps=extra_deps)

        def load_w(P, dst, name, gid, row0, col0, ncols, kc_n, buf):
            v = WB[name][row0:row0 + 128 * kc_n, :].rearrange("(kc p) c -> p kc c", p=128)
            for kc in range(0, kc_n, 2):
                P.dma("sp", dst[:, kc:kc + 2, :], v[:, kc:kc + 2, col0:col0 + ncols], buf, True,
                      extra_deps=[ctx.bglast[gid]])

        stWA = ExitStack()
        wk = stWA.enter_context(nc.sbuf_tensor("wk", [128, 8, 1024], BF16, side="right"))
        wv = stWA.enter_context(nc.sbuf_tensor("wv", [128, 8, 1024], BF16, side="right"))
        with ExitStack() as st:
            T = lambda n, s, d: st.enter_context(nc.sbuf_tensor(n, s, d))
            lv = T("lv", [128, 4, 64], F32)
            junk = T("junk_s", [128, 64], F32)
            dots = T("dots", [128, 2], F32)
            tmp8 = T("tmp8", [128, 8], F32)
            P = Pass(ctx, "setup")
            B = {k: Buf(k) for k in ["ones", "ident", "onesf", "masks", "flg", "lv", "junk", "dots", "lamt", "sublg", "prn", "c12", "pff", "tmp8"]}
            wv_ = w_in.rearrange("(kc p) c -> p kc c", p=128)
            for kc in range(8):
                P.dma_bg("pool", wk[:, kc, :], wv_[:, kc, 3072:4096], GA)
            for kc in range(8):
                P.dma_bg("pool", wv[:, kc, :], wv_[:, kc, 4096:5120], GA)
            P.op("pool", lambda e: e.memset(ones_bf[:], 1.0), [], [B["ones"]])
            P.op("pool", lambda e: e.memset(onesf[:], 1.0 / 128.0), [], [B["onesf"]])
            P.op("pool", lambda e: e.memset(ones1f[:], 1.0), [], [Buf()])
            P.op("pool", lambda e: e.affine_select(out=ident[:], in_=ones_bf[:, 0:128], pattern=[[-1, 128]],
                                                   compare_op=ALU.is_equal, fill=0.0, base=0, channel_multiplier=1),
                 [B["ones"]], [B["ident"]])
            for j in range(4):
                P.op("pool", lambda e, j=j: e.affine_select(out=masks[:, j, :], in_=ones_bf[:], pattern=[[1, 512]],
                                                            compare_op=ALU.is_ge, fill=0.0, base=-128 * j,
                                                            channel_multiplier=-1),
                     [B["ones"]], [B["masks"]])
            P.dma("sp", flg[:], flags, B["flg"], True)
            P.dma("sp", prn[:], par_rnn, B["prn"], True)
            P.dma("sp", pff[:], par_ffn, B["pff"], True)
            P.dma("sp", sublg[:], sublg_in, B["sublg"], True)
            for i in range(4):
                P.dma("sp", lv[:, i, :], lamv[i, :].partition_broadcast(128), B["lv"], True)
            for i in range(2):
                P.op("dve", lambda e, i=i: e.scalar_tensor_tensor(out=junk[:], in0=lv[:, 2 * i, :], scalar=1.0,
                                                                  in1=lv[:, 2 * i + 1, :], op0=ALU.mult, op1=ALU.mult,
                                                                  accum_out=dots[:, i:i + 1]),
                     [B["lv"]], [B["junk"], B["dots"]])
            P.op("act", lambda e: e.activation(out=dots[:], in_=dots[:], func=AF.Exp), [B["dots"]], [B["dots"]])
            P.op("dve", lambda e: e.scalar_tensor_tensor(out=lamt[:, 0:1], in0=dots[:, 0:1], scalar=LAMBDA_INIT,
                                                         in1=dots[:, 1:2], op0=ALU.add, op1=ALU.subtract),
                 [B["dots"]], [B["lamt"]])
            P.op("dve", lambda e: e.tensor_scalar(out=lamt[:, 1:2], in0=lamt[:, 0:1], scalar1=-1.0, scalar2=None,
                                                  op0=ALU.mult), [B["lamt"]], [B["lamt"]])
            P.op("dve", lambda e: e.tensor_scalar(out=sublg[:], in0=sublg[:], scalar1=(1.0 - LAMBDA_INIT), scalar2=None,
                                                  op0=ALU.mult), [B["sublg"]], [B["sublg"]])
            P.op("act", lambda e: e.activation(out=tmp8[:], in_=prn[:, 7, :], func=AF.Exp, scale=-1.0),
                 [B["prn"]], [B["tmp8"]])
            P.op("act", lambda e: e.activation(out=tmp8[:], in_=tmp8[:], func=AF.Ln, bias=1.0),
                 [B["tmp8"]], [B["tmp8"]])
            P.op("dve", lambda e: e.tensor_scalar(out=c12[:, 0, :], in0=tmp8[:], scalar1=-8.0, scalar2=None, op0=ALU.mult),
                 [B["tmp8"]], [B["c12"]])
            P.op("dve", lambda e: e.tensor_scalar(out=c12[:, 1, :], in0=tmp8[:], scalar1=-16.0, scalar2=None, op0=ALU.mult),
                 [B["tmp8"]], [B["c12"]])
            P.emit()

        def rms_T(P, xt, Bx, gb, Bg, junk, Bjunk, st4, Bst4, hn, Bhn, pT, BpT):
            P.op("dve", lambda e: e.scalar_tensor_tensor(out=junk[:], in0=xt, scalar=1.0, in1=xt, op0=ALU.mult,
                                                         op1=ALU.mult, accum_out=st4[:, 0:1]),
                 [Bx], [Bjunk, Bst4])
            P.op("dve", lambda e: e.tensor_scalar(out=st4[:, 1:2], in0=st4[:, 0:1], scalar1=1.0 / D, scalar2=EPS,
                                                  op0=ALU.mult, op1=ALU.add), [Bst4], [Bst4])
            P.op("act", lambda e: e.activation(out=st4[:, 2:3], in_=st4[:, 1:2], func=AF.Sqrt), [Bst4], [Bst4])
            P.op("dve", lambda e: e.reciprocal(out=st4[:, 3:4], in_=st4[:, 2:3]), [Bst4], [Bst4])
            P.op("dve", lambda e: e.scalar_tensor_tensor(out=hn[:], in0=xt, scalar=st4[:, 3:4], in1=gb[:],
                                                         op0=ALU.mult, op1=ALU.mult), [Bx, Bst4, Bg], [Bhn])
            if pT is not None:
                for c in range(8):
                    P.op("pe", lambda e, c=c: e.transpose(out=pT[:, c, :], in_=hn[:, 128 * c:128 * (c + 1)],
                                                          identity=ident[:]), [Bhn], [BpT])

        def WT(stk, name, shape, side):
            return stk.enter_context(nc.sbuf_tensor(name, shape, BF16, side=side))


        def prefetch_A(P):
            load_w(P, wk, "w_in", GA, 0, 2048 + 1024, 1024, 8, Buf())
            load_w(P, wv, "w_in", GA, 0, 2048 + 2048, 1024, 8, Buf())

        def mm8(P, ps, lhs_fn, rhs_fn, reads, Bps, n=8):
            for k in range(n):
                P.op("pe", lambda e, k=k: e.matmul(ps, lhsT=lhs_fn(k), rhs=rhs_fn(k), start=(k == 0), stop=(k == n - 1)),
                     reads, [Bps])

        evac_rr = [0]

        def evac(P, out_ap, in_ap, reads, writes):
            evac_rr[0] += 1
            if evac_rr[0] % 2 == 0:
                P.op("act", lambda e: e.activation(out=out_ap, in_=in_ap, func=AF.Copy), reads, writes)
            else:
                P.op("dve", lambda e: e.tensor_copy(out=out_ap, in_=in_ap), reads, writes)

        stWC = ExitStack()
        wq = WT(stWC, "wq", [128, 8, 1024], "left")
        wg = WT(stWC, "wg", [128, 8, 2048], "left")

        def prefetch_C(P):
            load_w(P, wq, "w_in", GC, 0, 2048, 1024, 8, Buf())
            load_w(P, wg, "w_in", GC, 0, 5120, 2048, 8, Buf())

        if upto >= 1:
            with ExitStack() as st:
                T = lambda n, s, d: st.enter_context(nc.sbuf_tensor(n, s, d))
                PT = lambda n, s, d: st.enter_context(nc.psum_tensor(n, s, d))
                gb = T("gb", [128, D], F32)
                xt = [T("xt%d" % i, [128, D], F32) for i in range(3)]
                junk = T("junk", [128, D], F32)
                st4 = [T("st4_%d" % i, [128, 4], F32) for i in range(2)]
                hn = [T("hn%d" % i, [128, D], BF16) for i in range(2)]
                pT = [PT("pT%d" % i, [128, 8, 128], BF16) for i in range(2)]
                hts = [T("hts%d" % i, [128, 8, 512], BF16) for i in range(2)]
                kts = [T("kts%d" % i, [128, 8, 512], BF16) for i in range(2)]
                vs = [T("vs%d" % i, [128, 1024], BF16) for i in range(2)]
                ps = [PT("psA%d" % i, [128, 512], F32) for i in range(4)]
                P = Pass(ctx, "p0A")
                Bgb, Bjunk, Bwk, Bwv = Buf(), Buf(), Buf(), Buf()
                Bxt = [Buf() for _ in range(3)]
                Bst4 = [Buf() for _ in range(2)]
                Bhn = [Buf() for _ in range(2)]
                BpT = [Buf() for _ in range(2)]
                Bhts = [[Buf() for _ in range(4)] for _ in range(2)]
                Bkts = [Buf(), Buf()]
                Bvs = [Buf(), Buf()]
                Bps = [Buf() for _ in range(4)]
                P.dma("sp", gb[:], g1.partition_broadcast(128), Bgb, True)
                NTILE = NT // 128
                P.dma("sp", xt[0][:], xin[0:128, :], Bxt[0], True)
                P.dma("sp", xt[1][:], xin[128:256, :], Bxt[1], True)
                Bwk.w = [ctx.bglast[GA]]
                Bwv.w = [ctx.bglast[GA]]
                cast_bg(P, "w_in", GC, 0, D, 2048, 3072)
                cast_bg(P, "w_in", GC, 0, D, 5120, 7168)
                cast_bg(P, "w_in", GB, 0, D, 0, 2048)
                P.dma_bg("pool", WB["rg_wa"], rg_wa, GB)
                P.dma_bg("pool", WB["rg_wx"], rg_wx, GB)
                cast_bg(P, "w_pa", GE, 0, D, 0, D)
                cast_bg(P, "w_pb", GE, 0, D, 0, D)
                cast_bg(P, "w_o", GE, 0, D, 0, D)

                def norm_a(i):
                    if i >= NTILE:
                        return
                    s3, s2 = i % 3, i % 2
                    if i + 2 < NTILE:
                        P.dma("sp", xt[(i + 2) % 3][:], xin[128 * (i + 2):128 * (i + 3), :], Bxt[(i + 2) % 3], True)
                    rms_T(P, xt[s3][:], Bxt[s3], gb, Bgb, junk, Bjunk, st4[s2], Bst4[s2], hn[s2], Bhn[s2], None, None)

                def norm_sub(t, q):
                    S = t % 2
                    i = 4 * t + q
                    s2 = i % 2
                    for c in range(8):
                        P.op("pe", lambda e, c=c: e.transpose(out=pT[s2][:, c, :], in_=hn[s2][:, 128 * c:128 * (c + 1)],
                                                              identity=ident[:]), [Bhn[s2]], [BpT[s2]])
                    P.op("act", lambda e: e.activation(out=hts[S][:, :, 128 * q:128 * (q + 1)], in_=pT[s2][:], func=AF.Copy),
                         [BpT[s2]], [Bhts[S][q]])
                    norm_a(i + 2)
                    if q == 3:
                        t0 = 512 * t
                        o = P.dma("sp", HT[:, :, t0:t0 + 512].rearrange("c p t -> p c t"), hts[S][:], Bhts[S][0], False)
                        for qq in range(1, 4):
                            o.deps = set(o.deps) | set(Bhts[S][qq].w)
                            Bhts[S][qq].rs.append(o)

                cnt = {"pi": 0, "vi": 0}

                def kv_part(i, p):
                    s = i % 2
                    for hd in (2 * p, 2 * p + 1):
                        b = cnt["pi"] % 4
                        cnt["pi"] += 1
                        mm8(P, ps[b][:], lambda k, hd=hd: wk[:, k, 128 * hd:128 * (hd + 1)], lambda k: hts[s][:, k, :],
                            [Bwk] + Bhts[s], Bps[b])
                        evac(P, kts[s][:, hd, :], ps[b][:], [Bps[b]], [Bkts[s]])
                    if p == 3:
                        P.dma("sp", KT[:, :, 512 * i:512 * (i + 1)].rearrange("h p t -> p h t"), kts[s][:], Bkts[s], False)
                    j = p
                    v = cnt["vi"] % 2
                    cnt["vi"] += 1
                    for half in range(2):
                        b = cnt["pi"] % 4
                        cnt["pi"] += 1
                        mm8(P, ps[b][:], lambda k: hts[s][:, k, 128 * j:128 * (j + 1)],
                            lambda k, half=half: wv[:, k, 512 * half:512 * (half + 1)], [Bwv] + Bhts[s], Bps[b])
                        evac(P, vs[v][:, 512 * half:512 * (half + 1)], ps[b][:], [Bps[b]], [Bvs[v]])
                    r0 = 512 * i + 128 * j
                    P.dma("sp", VV[r0:r0 + 128, :], vs[v][:], Bvs[v], False)

                norm_a(0)
                norm_a(1)
                for q in range(4):
                    norm_sub(0, q)
                for t in range(16):
                    if t == 2:
                        prefetch_C(P)
                    for p in range(4):
                        if t + 1 < 16:
                            norm_sub(t + 1, p)
                        kv_part(t, p)
                P.emit()

        stWA.close()
        stWB = ExitStack()
        wxr = WT(stWB, "wxr", [128, 8, 1024], "right")
        wgr = WT(stWB, "wgr", [128, 8, 1024], "right")
        wa = WT(stWB, "wa", [128, 8, 128], "right")
        wx = WT(stWB, "wx", [128, 8, 128], "right")

        def prefetch_B(P):
            load_w(P, wxr, "w_in", GB, 0, 0, 1024, 8, Buf())
            load_w(P, wgr, "w_in", GB, 0, 1024, 1024, 8, Buf())
            P.dma("sp", wa[:], WB["rg_wa"].rearrange("n i j -> i n j"), Buf(), True, extra_deps=[ctx.bglast[GB]])
            P.dma("sp", wx[:], WB["rg_wx"].rearrange("n i j -> i n j"), Buf(), True, extra_deps=[ctx.bglast[GB]])

        if upto >= 3:
            with ExitStack() as st:
                T = lambda n, s, d: st.enter_context(nc.sbuf_tensor(n, s, d))
                PT = lambda n, s, d: st.enter_context(nc.psum_tensor(n, s, d))
                htt = [T("httC%d" % i, [128, 8, 512], BF16) for i in range(2)]
                qs = [T("qs%d" % i, [128, 8, 512], BF16) for i in range(2)]
                sgs = [T("sgs%d" % i, [128, 16, 512], BF16) for i in range(2)]
                ps = [PT("psC%d" % i, [128, 512], F32) for i in range(4)]
                P = Pass(ctx, "pC")
                Bwq, Bwg = Buf(), Buf()
                Bhtt = [Buf(), Buf()]
                Bqs = [Buf(), Buf()]
                Bsgs = [Buf(), Buf()]
                Bps = [Buf() for _ in range(4)]
                cast_bg(P, "w_up", GF1, 0, D, 0, 2 * DFF)
                g0, n = CH[0]
                P.dma("sp", htt[0][:, :, 0:n], HT[:, :, g0:g0 + n].rearrange("c p t -> p c t"), Bhtt[0], True)
                pi = 0
                for ci, (g0, n) in enumerate(CH):
                    s = ci % 2
                    loc = g0 - Q0
                    if ci + 1 < len(CH):
                        g1_, n1 = CH[ci + 1]
                        P.dma("sp", htt[1 - s][:, :, 0:n1], HT[:, :, g1_:g1_ + n1].rearrange("c p t -> p c t"), Bhtt[1 - s], True)
                    if ci == 2:
                        prefetch_B(P)
                    for hd in range(8):
                        b = pi % 4
                        pi += 1
                        mm8(P, ps[b][:, 0:n], lambda k, hd=hd: wq[:, k, 128 * hd:128 * (hd + 1)],
                            lambda k, s=s, n=n: htt[s][:, k, 0:n], [Bwq, Bhtt[s]], Bps[b])
                        evac(P, qs[s][:, hd, 0:n], ps[b][:, 0:n], [Bps[b]], [Bqs[s]])
                    P.dma("sp", QT[:, :, loc:loc + n].rearrange("h p t -> p h t"), qs[s][:, :, 0:n], Bqs[s], False)
                    for c in range(16):
                        b = pi % 4
                        pi += 1
                        mm8(P, ps[b][:, 0:n], lambda k, c=c: wg[:, k, 128 * c:128 * (c + 1)],
                            lambda k, s=s, n=n: htt[s][:, k, 0:n], [Bwg, Bhtt[s]], Bps[b])
                        P.op("act", lambda e, c=c, b=b, s=s, n=n: e.activation(out=sgs[s][:, c, 0:n], in_=ps[b][:, 0:n],
                                                                                func=AF.Sigmoid), [Bps[b]], [Bsgs[s]])
                    P.dma("sp", SG[:, :, loc:loc + n].rearrange("h p t -> p h t"), sgs[s][:, :, 0:n], Bsgs[s], False)
                P.emit()

        stWC.close()
        if upto >= 2:
            with ExitStack() as st:
                T = lambda n, s, d: st.enter_context(nc.sbuf_tensor(n, s, d))
                PT = lambda n, s, d: st.enter_context(nc.psum_tensor(n, s, d))
                htt = [T("httB%d" % i, [128, 8, 512], BF16) for i in range(2)]
                xrb = T("xrb", [128, 8, 515], F32)
                xc = T("xc", [128, 8, 512], F32)
                xcb = T("xcb", [128, 8, 512], BF16)
                ra = T("ra", [128, 8, 512], F32)
                a2 = T("a2", [128, 8, 512], F32)
                ib = T("ib", [128, 8, 512], F32)
                hh = a2
                gg = T("gg", [128, 8, 512], BF16)
                ys = [T("ys%d" % i, [128, 8, 512], BF16) for i in range(1)]
                hst = T("hst", [128, 8], F32)
                ps = [PT("psB%d" % i, [128, 512], F32) for i in range(8)]
                P = Pass(ctx, "pB")
                Bwxr, Bwgr, Bwa, Bwx, Bhst = Buf(), Buf(), Buf(), Buf(), Buf()
                Bhtt = [Buf(), Buf()]
                Bxrb = [Buf() for _ in range(8)]
                Bxc = [Buf() for _ in range(8)]
                Bxcb = [Buf() for _ in range(8)]
                Bra = [Buf() for _ in range(8)]
                Ba2 = [Buf() for _ in range(8)]
                Bib = [Buf() for _ in range(8)]
                Bhh = Ba2
                Bgg = [Buf() for _ in range(8)]
                Bys = [[Buf() for _ in range(8)] for _ in range(1)]
                Bps = [Buf() for _ in range(8)]
                Bhs = [Buf() for _ in range(8)]
                cast_bg(P, "w_dn", GF2, 0, DFF, 0, D)
                P.op("pool", lambda e: e.memset(xrb[:, :, 0:3], 0.0), [], Bxrb)
                P.op("pool", lambda e: e.memset(hst[:], 0.0), [], Bhs)
                P.dma("sp", htt[0][:], HT[:, :, 0:512].rearrange("c p t -> p c t"), Bhtt[0], True)
                pi = 0
                yi = 0
                for i in range(16):
                    s = i % 2
                    own = i >= 7
                    if i + 1 < 16:
                        P.dma("sp", htt[1 - s][:], HT[:, :, 512 * (i + 1):512 * (i + 2)].rearrange("c p t -> p c t"),
                              Bhtt[1 - s], True)
                    for ct in range(8):
                        b = pi % 8
                        pi += 1
                        mm8(P, ps[b][:], lambda k, ct=ct: wxr[:, k, 128 * ct:128 * (ct + 1)], lambda k, s=s: htt[s][:, k, :],
                            [Bwxr, Bhtt[s]], Bps[b])
                        P.op("act", lambda e, ct=ct, b=b: e.activation(out=xrb[:, ct, 3:515], in_=ps[b][:], func=AF.Copy),
                             [Bps[b]], [Bxrb[ct]])
                    for ct in range(8):
                        P.op("dve", lambda e, ct=ct: e.tensor_scalar(out=xc[:, ct, :], in0=xrb[:, ct, 0:512],
                                                                     scalar1=prn[:, 0, ct:ct + 1], scalar2=prn[:, 4, ct:ct + 1],
                                                                     op0=ALU.mult, op1=ALU.add), [Bxrb[ct]], [Bxc[ct]])
                    for k in range(1, 4):
                        for ct in range(8):
                            P.op("dve", lambda e, ct=ct, k=k: e.scalar_tensor_tensor(
                                out=xc[:, ct, :], in0=xrb[:, ct, k:k + 512], scalar=prn[:, k, ct:ct + 1], in1=xc[:, ct, :],
                                op0=ALU.mult, op1=ALU.add), [Bxrb[ct], Bxc[ct]], [Bxc[ct]])
                    for ct in range(8):
                        P.op("pool", lambda e, ct=ct: e.tensor_copy(out=xrb[:, ct, 0:3], in_=xrb[:, ct, 512:515]),
                             [Bxrb[ct]], [Bxrb[ct]])
                        P.op("act", lambda e, ct=ct: e.activation(out=xcb[:, ct, :], in_=xc[:, ct, :], func=AF.Copy),
                             [Bxc[ct]], [Bxcb[ct]])
                    gps = []
                    for ct in range(8):
                        b1 = pi % 8
                        pi += 1
                        P.op("pe", lambda e, ct=ct, b1=b1: e.matmul(ps[b1][:], lhsT=wa[:, ct, :], rhs=xcb[:, ct, :],
                                                                     start=True, stop=True), [Bwa, Bxcb[ct]], [Bps[b1]])
                        P.op("act", lambda e, ct=ct, b1=b1: e.activation(out=ra[:, ct, :], in_=ps[b1][:], func=AF.Sigmoid,
                                                                          bias=prn[:, 5, ct:ct + 1]), [Bps[b1]], [Bra[ct]])
                    for ct in range(8):
                        b2 = pi % 8
                        pi += 1
                        P.op("pe", lambda e, ct=ct, b2=b2: e.matmul(ps[b2][:], lhsT=wx[:, ct, :], rhs=xcb[:, ct, :],
                                                                     start=True, stop=True), [Bwx, Bxcb[ct]], [Bps[b2]])
                        P.op("act", lambda e, ct=ct, b2=b2: e.activation(out=ib[:, ct, :], in_=ps[b2][:], func=AF.Sigmoid,
                                                                          bias=prn[:, 6, ct:ct + 1]), [Bps[b2]], [Bib[ct]])
                    for ct in range(8):
                        P.op("act", lambda e, ct=ct: e.activation(out=a2[:, ct, :], in_=ra[:, ct, :], func=AF.Exp,
                                                                  scale=c12[:, 1, ct:ct + 1]), [Bra[ct]], [Ba2[ct]])
                    for ct in range(8):
                        P.op("act", lambda e, ct=ct: e.activation(out=ra[:, ct, :], in_=ra[:, ct, :], func=AF.Exp,
                                                                  scale=c12[:, 0, ct:ct + 1]), [Bra[ct], Ba2[ct]], [Bra[ct]])
                    for ct in range(8):
                        P.op("dve", lambda e, ct=ct: e.tensor_scalar(out=a2[:, ct, :], in0=a2[:, ct, :], scalar1=1.0,
                                                                     scalar2=-1.0, op0=ALU.min, op1=ALU.mult),
                             [Ba2[ct]], [Ba2[ct]])
                        P.op("pool", lambda e, ct=ct: e.tensor_tensor(out=ib[:, ct, :], in0=ib[:, ct, :], in1=xc[:, ct, :],
                                                                      op=ALU.mult), [Bib[ct], Bxc[ct]], [Bib[ct]])
                    for ct in range(8):
                        P.op("act", lambda e, ct=ct: e.activation(out=a2[:, ct, :], in_=a2[:, ct, :], func=AF.Sqrt, bias=1.0),
                             [Ba2[ct]], [Ba2[ct]])
                    for ct in range(8):
                        P.op("dve", lambda e, ct=ct: e.tensor_tensor(out=ib[:, ct, :], in0=ib[:, ct, :], in1=a2[:, ct, :],
                                                                     op=ALU.mult), [Bib[ct], Ba2[ct]], [Bib[ct]])
                    for ct in range(8):
                        P.op("dve", lambda e, ct=ct: e.tensor_tensor_scan(out=hh[:, ct, :], data0=ra[:, ct, :],
                                                                          data1=ib[:, ct, :], initial=hst[:, ct:ct + 1],
                                                                          op0=ALU.mult, op1=ALU.add),
                             [Bra[ct], Bib[ct], Bhs[ct]], [Bhh[ct]])
                        if i == 7:
                            P.op("pool", lambda e, ct=ct: e.tensor_scalar(out=hst[:, ct:ct + 1], in0=hh[:, ct, 511:512],
                                                                          scalar1=ctxf, scalar2=None, op0=ALU.mult),
                                 [Bhh[ct]], [Bhs[ct]])
                        else:
                            P.op("pool", lambda e, ct=ct: e.tensor_copy(out=hst[:, ct:ct + 1], in_=hh[:, ct, 511:512]),
                                 [Bhh[ct]], [Bhs[ct]])
                    if own:
                        y = 0
                        for ct in range(8):
                            b = pi % 8
                            pi += 1
                            mm8(P, ps[b][:], lambda k, ct=ct: wgr[:, k, 128 * ct:128 * (ct + 1)],
                                lambda k, s=s: htt[s][:, k, :], [Bwgr, Bhtt[s]], Bps[b])
                            P.op("act", lambda e, ct=ct, b=b: e.activation(out=gg[:, ct, :], in_=ps[b][:], func=GELU),
                                 [Bps[b]], [Bgg[ct]])
                        for ct in range(8):
                            P.op("pool", lambda e, ct=ct, y=y: e.tensor_tensor(out=ys[y][:, ct, :], in0=hh[:, ct, :],
                                                                               in1=gg[:, ct, :], op=ALU.mult),
                                 [Bhh[ct], Bgg[ct]], [Bys[y][ct]])
                        if i == 7:
                            o = P.dma("sp", YA[:, :, 0:128].rearrange("c p t -> p c t"), ys[y][:, :, 384:512], Bys[y][0], False)
                        else:
                            l0 = 128 + 512 * (i - 8)
                            o = P.dma("sp", YA[:, :, l0:l0 + 512].rearrange("c p t -> p c t"), ys[y][:], Bys[y][0], False)
                        for ct in range(1, 8):
                            o.deps = set(o.deps) | set(Bys[y][ct].w)
                            Bys[y][ct].rs.append(o)
                P.emit()

        stWB.close()
        stWE = ExitStack()
        wpa = WT(stWE, "wpa", [128, 8, 1024], "right")
        wpb = WT(stWE, "wpb", [128, 8, 1024], "right")
        wo = WT(stWE, "wo", [128, 8, 1024], "right")

        def prefetch_E(P):
            load_w(P, wpa, "w_pa", GE, 0, 0, 1024, 8, Buf())
            load_w(P, wpb, "w_pb", GE, 0, 0, 1024, 8, Buf())
            load_w(P, wo, "w_o", GE, 0, 0, 1024, 8, Buf())

        if upto >= 4:
            with ExitStack() as st:
                T = lambda n, s, d: st.enter_context(nc.sbuf_tensor(n, s, d))
                PT = lambda n, s, d: st.enter_context(nc.psum_tensor(n, s, d))
                kth = [T("kth%d" % i, [128, NT], BF16) for i in range(2)]
                vh = [T("vh%d" % i, [128, 64, 128], BF16) for i in range(2)]
                qh = [T("qh%d" % i, [128, NQ], BF16) for i in range(2)]
                pp = [T("pp%d" % i, [128, 2, 512], BF16) for i in range(8)]
                plc = T("plc", [128, 512], F32)
                Bplc = Buf()
                ls = [T("ls%d" % i, [128, 2, 512], F32) for i in range(2)]
                os_ = [T("os%d" % i, [128, 2, 512], F32) for i in range(2)]
                hdt = [T("hdt%d" % i, [128, 512], F32) for i in range(2)]
                accD = T("accD", [128, 512], F32)
                accP = T("accP", [128, 512], F32)
                BaccD, BaccP = Buf(), Buf()
                ps = [PT("psS%d" % i, [128, 2, 512], F32) for i in range(2)]
                po = PT("po", [128, 2, 512], F32)
                pl = PT("pl", [128, 2, 512], F32)
                P = Pass(ctx, "pD")
                Bk = [Buf(), Buf()]
                Bv = [Buf(), Buf()]
                Bq = [Buf(), Buf()]
                Bpp = [[Buf(), Buf()] for _ in range(8)]
                Bs = [Buf(), Buf()]
                Bo = [Buf(), Buf()]
                Bl = [Buf(), Buf()]
                Bls = [[Buf(), Buf()] for _ in range(2)]
                Bos = [[Buf(), Buf()] for _ in range(2)]
                Bhd = [Buf(), Buf()]

                def load_head(hd, s):
                    P.dma("sp", qh[s][:], QT[hd, :, :], Bq[s], True)
                    for q4 in range(4):
                        P.dma("sp", kth[s][:, 2048 * q4:2048 * (q4 + 1)], KT[hd, :, 2048 * q4:2048 * (q4 + 1)], Bk[s], True)
                    vsrc = VV[:, 128 * hd:128 * (hd + 1)].rearrange("(j p) e -> p j e", p=128)
                    for q4 in range(4):
                        P.dma("sp", vh[s][:, 16 * q4:16 * (q4 + 1), :], vsrc[:, 16 * q4:16 * (q4 + 1), :], Bv[s], True)

                LG = 4
                steps = []
                for hd in range(8):
                    for ci, (g0, n) in enumerate(CH):
                        nkt = (g0 + n) // 128
                        for kt in range(nkt):
                            steps.append((hd, ci, kt, nkt))

                def col0(i):
                    hd, ci, kt, nkt = steps[i]
                    g0, n = CH[ci]
                    return max(128 * kt - g0, 0)

                def emit_qk(i):
                    hd, ci, kt, nkt = steps[i]
                    g0, n = CH[ci]
                    s, sb, k0, loc = hd % 2, i % 2, 128 * kt, g0 - Q0
                    c0 = col0(i)
                    P.op("pe", lambda e: e.matmul(ps[sb][:, 0, c0:n], lhsT=kth[s][0:64, k0:k0 + 128],
                                                  rhs=qh[s][0:64, loc + c0:loc + n], start=True, stop=True),
                         [Bk[s], Bq[s]], [Bs[sb]])
                    P.op("pe", lambda e: e.matmul(ps[sb][:, 1, c0:n], lhsT=kth[s][64:128, k0:k0 + 128],
                                                  rhs=qh[s][64:128, loc + c0:loc + n], start=True, stop=True),
                         [Bk[s], Bq[s]], [Bs[sb]])

                def emit_exp(i):
                    hd, ci, kt, nkt = steps[i]
                    g0, n = CH[ci]
                    sb, pb, k0 = i % 2, i % 8, 128 * kt
                    bias = ctxb if kt < 32 else 0.0
                    c0 = col0(i)
                    P.op("act", lambda e: e.activation(out=pp[pb][:, :, c0:n], in_=ps[sb][:, :, c0:n], func=AF.Exp,
                                                       scale=0.125, bias=bias), [Bs[sb]], Bpp[pb])
                    if k0 >= g0:
                        j = (k0 - g0) // 128
                        c1 = c0 + 128
                        P.op("dve", lambda e: e.tensor_tensor(out=pp[pb][:, 0, c0:c1], in0=pp[pb][:, 0, c0:c1],
                                                              in1=masks[:, j, c0:c1], op=ALU.mult), [Bpp[pb][0]], [Bpp[pb][0]])
                        P.op("dve", lambda e: e.tensor_tensor(out=pp[pb][:, 1, c0:c1], in0=pp[pb][:, 1, c0:c1],
                                                              in1=masks[:, j, c0:c1], op=ALU.mult), [Bpp[pb][1]], [Bpp[pb][1]])

                def emit_av(i):
                    hd, ci, kt, nkt = steps[i]
                    g0, n = CH[ci]
                    s, pb = hd % 2, i % 8
                    first, last = kt == 0, kt == nkt - 1
                    c0 = col0(i)
                    for m in range(2):
                        P.op("pe", lambda e, m=m: e.matmul(po[:, m, c0:n], lhsT=vh[s][:, kt, :], rhs=pp[pb][:, m, c0:n],
                                                           start=first, stop=last), [Bv[s], Bpp[pb][m]], [Bo[m]])
                    if kt % LG == LG - 1:
                        for jj in range(LG):
                            slot = (i - (LG - 1) + jj) % 8
                            cj = col0(i - (LG - 1) + jj)
                            P.op("pe", lambda e, jj=jj, slot=slot, cj=cj: e.matmul(
                                pl[32 * jj:32 * (jj + 1), 0, cj:n], lhsT=ones_bf[:, 0:32], rhs=pp[slot][:, 0, cj:n],
                                start=(kt == LG - 1), stop=last, tile_position=(0, 32 * jj)), [Bpp[slot][0]], [Bl[0]])
                    if kt == 0:
                        P.op("dve", lambda e: e.tensor_copy(out=accP[:, 0:n], in_=pp[pb][:, 1, 0:n]), [Bpp[pb][1]], [BaccP])
                    else:
                        P.op("dve", lambda e: e.tensor_tensor(out=accP[:, c0:n], in0=accP[:, c0:n], in1=pp[pb][:, 1, c0:n],
                                                              op=ALU.add), [Bpp[pb][1], BaccP], [BaccP])
                    if last:
                        P.op("pe", lambda e: e.matmul(pl[:, 1, 0:n], lhsT=ones1f[:], rhs=accP[:, 0:n], start=True, stop=True),
                             [BaccP], [Bl[1]])
                        P.op("dve", lambda e: e.tensor_copy(out=plc[:, 0:n], in_=pl[:, 0, 0:n]), [Bl[0]], [Bplc])
                        P.op("pe", lambda e: e.matmul(pl[:, 0, 0:n], lhsT=onesf[:], rhs=plc[:, 0:n], start=True, stop=True),
                             [Bplc], [Bl[0]])

                fin = [0]

                def finalize(i):
                    hd, ci, kt, nkt = steps[i]
                    g0, n = CH[ci]
                    loc = g0 - Q0
                    y = fin[0] % 2
                    fin[0] += 1
                    for m in range(2):
                        P.op("dve", lambda e, m=m: e.tensor_scalar(out=ls[y][:, m, 0:n], in0=pl[:, m, 0:n],
                                                                   scalar1=(4.0 if m == 0 else 1.0), scalar2=1e-30,
                                                                   op0=ALU.mult, op1=ALU.add), [Bl[m]], [Bls[y][m]])
                        P.op("act", lambda e, m=m: e.activation(out=os_[y][:, m, 0:n], in_=po[:, m, 0:n], func=AF.Copy),
                             [Bo[m]], [Bos[y][m]])
                    def part_norm(m):
                        P.op("dve", lambda e: e.reciprocal(out=ls[y][:, m, 0:n], in_=ls[y][:, m, 0:n]),
                             [Bls[y][m]], [Bls[y][m]])
                        P.op("pool", lambda e: e.tensor_tensor(out=os_[y][:, m, 0:n], in0=os_[y][:, m, 0:n],
                                                               in1=ls[y][:, m, 0:n], op=ALU.mult),
                             [Bos[y][m], Bls[y][m]], [Bos[y][m]])

                    def part_out():
                        P.op("dve", lambda e: e.scalar_tensor_tensor(out=hdt[y][:, 0:n], in0=os_[y][:, 1, 0:n],
                                                                     scalar=lamt[:, 1:2], in1=os_[y][:, 0, 0:n],
                                                                     op0=ALU.mult, op1=ALU.add), Bos[y], [Bhd[y]])
                        P.dma("sp", HD[hd, :, loc:loc + n], hdt[y][:, 0:n], Bhd[y], False)

                    deferred.setdefault(i + 3, []).append(lambda: part_norm(0))
                    deferred.setdefault(i + 7, []).append(lambda: part_norm(1))
                    deferred.setdefault(i + 11, []).append(part_out)

                deferred = {}
                load_head(0, 0)
                prefetch_E(P)
                emit_qk(0)
                emit_qk(1)
                for i in range(len(steps)):
                    hd, ci, kt, nkt = steps[i]
                    if ci == 0 and kt == 0 and hd + 1 < 8:
                        load_head(hd + 1, 1 - hd % 2)
                    emit_exp(i)
                    if i + 2 < len(steps):
                        emit_qk(i + 2)
                    for fn in deferred.pop(i, []):
                        fn()
                    emit_av(i)
                    if kt == nkt - 1:
                        finalize(i)
                for j in sorted(deferred):
                    for fn in deferred[j]:
                        fn()
                P.emit()

        if upto >= 5:
            with ExitStack() as st:
                T = lambda n, s, d: st.enter_context(nc.sbuf_tensor(n, s, d))
                PT = lambda n, s, d: st.enter_context(nc.psum_tensor(n, s, d))
                gb = T("gb2", [128, D], F32)
                ya = [T("ya%d" % i, [128, 8, 512], BF16) for i in range(2)]
                hq = [T("hq%d" % i, [128, 8, 512], F32) for i in range(2)]
                yb = [T("yb%d" % i, [128, 8, 512], BF16) for i in range(2)]
                sqt = [T("sqt%d" % i, [128, 512], F32) for i in range(2)]
                lnt = [T("lnt%d" % i, [128, 512], F32) for i in range(2)]
                sg = [T("sg%d" % i, [128, 16, 512], BF16) for i in range(1)]
                t1 = [T("t1_%d" % i, [128, 512], F32) for i in range(2)]
                t2 = [T("t2_%d" % i, [128, 512], F32) for i in range(2)]
                mg = T("mg", [128, 8, 512], BF16)
                xt = [T("xtE%d" % i, [128, D], F32) for i in range(2)]
                x1 = [T("x1_%d" % i, [128, D], F32) for i in range(2)]
                junk = T("junkE", [128, D], F32)
                st4 = [T("st4E%d" % i, [128, 4], F32) for i in range(2)]
                hn = [T("hnE%d" % i, [128, D], BF16) for i in range(2)]
                h2s = [T("h2s%d" % i, [128, 8, 512], BF16) for i in range(1)]
                ppa = [PT("ppa%d" % i, [128, 512], F32) for i in range(2)]
                ppb = [PT("ppb%d" % i, [128, 512], F32) for i in range(2)]
                pw = [PT("pw%d" % i, [128, 512], F32) for i in range(2)]
                pT = [PT("pTE%d" % i, [128, 8, 128], BF16) for i in range(1)]
                pms = PT("pms", [128, 512], F32)
                P = Pass(ctx, "pE")
                Bwpa, Bwpb, Bwo, Bgb, Bjunk = Buf(), Buf(), Buf(), Buf(), Buf()
                Bya, Bhq, Bsg = [Buf(), Buf()], [Buf(), Buf()], [Buf()]
                Byb_ = [[Buf() for _ in range(8)] for _ in range(2)]
                Bsqt, Blnt, Bpms = [Buf(), Buf()], [Buf(), Buf()], Buf()
                Bt1, Bt2 = [Buf(), Buf()], [Buf(), Buf()]
                Bmg = [Buf() for _ in range(8)]
                Bxt, Bx1, Bst4, Bhn = [Buf(), Buf()], [Buf(), Buf()], [Buf(), Buf()], [Buf(), Buf()]
                Bh2s = [[Buf() for _ in range(4)] for _ in range(1)]
                Bppa, Bppb, Bpw, BpT = [Buf(), Buf()], [Buf(), Buf()], [Buf(), Buf()], [Buf()]
                P.dma("sp", gb[:], g2.partition_broadcast(128), Bgb, True)

                def load_chunk(ci):
                    g0, n = CH[ci]
                    loc = g0 - Q0
                    s = ci % 2
                    P.dma("sp", ya[s][:, :, 0:n], YA[:, :, loc:loc + n].rearrange("c p t -> p c t"), Bya[s], True)
                    P.dma("sp", hq[s][:, :, 0:n], HD[:, :, loc:loc + n].rearrange("c p t -> p c t"), Bhq[s], True)

                def load_sg(ci):
                    g0, n = CH[ci]
                    loc = g0 - Q0
                    P.dma("sp", sg[0][:, :, 0:n], SG[:, :, loc:loc + n].rearrange("c p t -> p c t"), Bsg[0], True)

                def sub_ln(ci, heads):
                    g0, n = CH[ci]
                    s = ci % 2
                    for hd in heads:
                        q2 = hd % 2
                        P.op("pool", lambda e, hd=hd, q2=q2: e.tensor_tensor(out=sqt[q2][:, 0:n], in0=hq[s][:, hd, 0:n],
                                                                             in1=hq[s][:, hd, 0:n], op=ALU.mult),
                             [Bhq[s]], [Bsqt[q2]])
                        P.op("pe", lambda e, q2=q2: e.matmul(pms[:, 0:n], lhsT=onesf[:], rhs=sqt[q2][:, 0:n], start=True,
                                                             stop=True), [Bsqt[q2]], [Bpms])
                        P.op("act", lambda e, q2=q2: e.activation(out=lnt[q2][:, 0:n], in_=pms[:, 0:n], func=AF.Ln, bias=1e-5),
                             [Bpms], [Blnt[q2]])
                        P.op("act", lambda e, q2=q2: e.activation(out=lnt[q2][:, 0:n], in_=lnt[q2][:, 0:n], func=AF.Exp,
                                                                  scale=-0.5), [Blnt[q2]], [Blnt[q2]])
                        P.op("dve", lambda e, hd=hd, q2=q2: e.scalar_tensor_tensor(
                            out=yb[s][:, hd, 0:n], in0=hq[s][:, hd, 0:n], scalar=sublg[:, 0:1], in1=lnt[q2][:, 0:n],
                            op0=ALU.mult, op1=ALU.mult), [Bhq[s], Blnt[q2]], [Byb_[s][hd]])

                load_chunk(0)
                ti = 0
                for ci, (g0, n) in enumerate(CH):
                    s = ci % 2
                    loc = g0 - Q0
                    if ci + 1 < len(CH):
                        load_chunk(ci + 1)
                    load_sg(ci)
                    sub_ln(ci, range(8))
                    for c in range(8):
                        b = c % 2
                        mm8(P, ppa[b][:, 0:n], lambda k, c=c: wpa[:, k, 128 * c:128 * (c + 1)],
                            lambda k, s=s, n=n: ya[s][:, k, 0:n], [Bwpa, Bya[s]], Bppa[b])
                        mm8(P, ppb[b][:, 0:n], lambda k, c=c: wpb[:, k, 128 * c:128 * (c + 1)],
                            lambda k, s=s, n=n: yb[s][:, k, 0:n], [Bwpb] + Byb_[s], Bppb[b])
                        P.op("dve", lambda e, b=b, c=c, n=n: e.tensor_tensor(out=t1[b][:, 0:n], in0=ppa[b][:, 0:n],
                                                                            in1=sg[0][:, c, 0:n], op=ALU.mult),
                             [Bppa[b], Bsg[0]], [Bt1[b]])
                        P.op("dve", lambda e, b=b, c=c, n=n: e.tensor_tensor(out=t2[b][:, 0:n], in0=ppb[b][:, 0:n],
                                                                            in1=sg[0][:, 8 + c, 0:n], op=ALU.mult),
                             [Bppb[b], Bsg[0]], [Bt2[b]])
                        P.op("pool", lambda e, b=b, c=c, n=n: e.tensor_tensor(out=mg[:, c, 0:n], in0=t1[b][:, 0:n],
                                                                              in1=t2[b][:, 0:n], op=ALU.add),
                             [Bt1[b], Bt2[b]], [Bmg[c]])
                    S = 0
                    nj = n // 128
                    pend = None

                    def emit_T(u, j):
                        for c in range(8):
                            P.op("pe", lambda e, c=c: e.transpose(out=pT[0][:, c, :], in_=hn[u][:, 128 * c:128 * (c + 1)],
                                                                  identity=ident[:]), [Bhn[u]], [BpT[0]])
                        P.op("act", lambda e: e.activation(out=h2s[S][:, :, 128 * j:128 * (j + 1)], in_=pT[0][:], func=AF.Copy),
                             [BpT[0]], [Bh2s[S][j]])
                    for j in range(n // 128):
                        u = ti % 2
                        ti += 1
                        r0 = g0 + 128 * j
                        P.dma("sp", xt[u][:], xin[r0:r0 + 128, :], Bxt[u], True)
                        for half in range(2):
                            mm8(P, pw[half][:], lambda k, j=j: mg[:, k, 128 * j:128 * (j + 1)],
                                lambda k, half=half: wo[:, k, 512 * half:512 * (half + 1)], [Bwo] + Bmg, Bpw[half])
                            P.op("dve", lambda e, u=u, half=half: e.tensor_tensor(
                                out=x1[u][:, 512 * half:512 * (half + 1)], in0=pw[half][:],
                                in1=xt[u][:, 512 * half:512 * (half + 1)], op=ALU.add), [Bpw[half], Bxt[u]], [Bx1[u]])
                        P.dma("sp", X1[loc + 128 * j:loc + 128 * (j + 1), :], x1[u][:], Bx1[u], False)
                        rms_T(P, x1[u][:], Bx1[u], gb, Bgb, junk, Bjunk, st4[u], Bst4[u], hn[u], Bhn[u], None, None)
                        if pend is not None:
                            emit_T(*pend)
                        pend = (u, j)
                    emit_T(*pend)
                    o = P.dma("sp", H2T[:, :, loc:loc + n].rearrange("c p t -> p c t"), h2s[S][:, :, 0:n], Bh2s[S][0], False)
                    for qq in range(1, n // 128):
                        o.deps = set(o.deps) | set(Bh2s[S][qq].w)
                        Bh2s[S][qq].rs.append(o)
                P.emit()

        stWE.close()
        stWD = ExitStack()
        wd = WT(stWD, "wd", [128, 24, D], "right")

        def prefetch_F2(P):
            for q in range(3):
                load_w(P, wd[:, 8 * q:8 * (q + 1), :], "w_dn", GF2, 1024 * q, 0, 1024, 8, Buf())

        if upto >= 6:
            with ExitStack() as st:
                T = lambda n, s, d: st.enter_context(nc.sbuf_tensor(n, s, d))
                PT = lambda n, s, d: st.enter_context(nc.psum_tensor(n, s, d))
                wu = T("wu", [128, 8, 2 * DFF], BF16)
                h2t = [T("h2t%d" % i, [128, 8, 512], BF16) for i in range(2)]
                carry = T("carry", [128, 24, 2], F32)
                ugb = [T("ugb%d" % i, [128, 514], F32) for i in range(3)]
                cv = [T("cv%d" % i, [128, 512], F32) for i in range(2)]
                ge = [T("ge%d" % i, [128, 512], F32) for i in range(2)]
                ats = [T("ats%d" % i, [128, 24, 512], BF16) for i in range(1)]
                pg = [PT("pg%d" % i, [128, 512], F32) for i in range(3)]
                pv = [PT("pv%d" % i, [128, 512], F32) for i in range(3)]
                P = Pass(ctx, "pF1")
                Bwu = Buf()
                Bwuv = Buf()
                Bh2t = [Buf(), Buf()]
                Bcar = [Buf() for _ in range(24)]
                Bug = [Buf() for _ in range(3)]
                Bcv, Bge = [Buf(), Buf()], [Buf(), Buf()]
                Bats = [[Buf() for _ in range(24)] for _ in range(1)]
                Bpg, Bpv = [Buf() for _ in range(3)], [Buf() for _ in range(3)]
                load_w(P, wu[:, :, 0:DFF], "w_up", GF1, 0, 0, DFF, 8, Bwu)
                load_w(P, wu[:, :, DFF:2 * DFF], "w_up", GF1, 0, DFF, DFF, 8, Bwuv)
                g0, n = CH[0]
                P.dma("sp", h2t[0][:, :, 0:n], H2T[:, :, 0:n].rearrange("c p t -> p c t"), Bh2t[0], True)
                it = 0
                for ci, (g0, n) in enumerate(CH):
                    s = ci % 2
                    loc = g0 - Q0
                    if ci + 1 < len(CH):
                        g1_, n1 = CH[ci + 1]
                        l1 = g1_ - Q0
                        P.dma("sp", h2t[1 - s][:, :, 0:n1], H2T[:, :, l1:l1 + n1].rearrange("c p t -> p c t"), Bh2t[1 - s], True)
                    A = 0
                    if ci == 2:
                        prefetch_F2(P)
                    for fc in range(24):
                        b = it % 3
                        c2 = it % 2
                        it += 1
                        mm8(P, pg[b][:, 0:n], lambda k, fc=fc: wu[:, k, 128 * fc:128 * (fc + 1)],
                            lambda k, s=s, n=n: h2t[s][:, k, 0:n], [Bwu, Bh2t[s]], Bpg[b])
                        if ci == 0:
                            P.op("dve", lambda e, fc=fc, b=b: e.tensor_scalar(out=carry[:, fc, :], in0=pg[b][:, 126:128],
                                                                              scalar1=ctxf, scalar2=None, op0=ALU.mult),
                                 [Bpg[b]], [Bcar[fc]])
                            continue
                        mm8(P, pv[b][:, 0:n], lambda k, fc=fc: wu[:, k, DFF + 128 * fc:DFF + 128 * (fc + 1)],
                            lambda k, s=s, n=n: h2t[s][:, k, 0:n], [Bwuv, Bh2t[s]], Bpv[b])
                        P.op("pool", lambda e, fc=fc, b=b: e.tensor_copy(out=ugb[b][:, 0:2], in_=carry[:, fc, :]),
                             [Bcar[fc]], [Bug[b]])
                        P.op("act", lambda e, b=b: e.activation(out=ugb[b][:, 2:514], in_=pg[b][:], func=AF.Copy),
                             [Bpg[b]], [Bug[b]])
                        P.op("pool", lambda e, fc=fc, b=b: e.tensor_copy(out=carry[:, fc, :], in_=ugb[b][:, 512:514]),
                             [Bug[b]], [Bcar[fc]])
                        P.op("dve", lambda e, fc=fc, b=b, c2=c2: e.tensor_scalar(
                            out=cv[c2][:], in0=ugb[b][:, 0:512], scalar1=pff[:, 0, fc:fc + 1], scalar2=pff[:, 3, fc:fc + 1],
                            op0=ALU.mult, op1=ALU.add), [Bug[b]], [Bcv[c2]])
                        for k in range(1, 3):
                            P.op("dve", lambda e, fc=fc, b=b, c2=c2, k=k: e.scalar_tensor_tensor(
                                out=cv[c2][:], in0=ugb[b][:, k:k + 512], scalar=pff[:, k, fc:fc + 1], in1=cv[c2][:],
                                op0=ALU.mult, op1=ALU.add), [Bug[b], Bcv[c2]], [Bcv[c2]])
                        P.op("act", lambda e, c2=c2: e.activation(out=ge[c2][:], in_=cv[c2][:], func=GELU), [Bcv[c2]], [Bge[c2]])
                        P.op("dve", lambda e, fc=fc, b=b, c2=c2, A=A: e.tensor_tensor(out=ats[A][:, fc, :], in0=pv[b][:],
                                                                                     in1=ge[c2][:], op=ALU.mult),
                             [Bpv[b], Bge[c2]], [Bats[A][fc]])
                        if ci >= 1 and fc % 12 == 11:
                            t0 = g0 - NOWN
                            f0 = fc - 11
                            o = P.dma("sp", AT[f0:f0 + 12, :, t0:t0 + 512].rearrange("c p t -> p c t"), ats[A][:, f0:f0 + 12, :],
                                      Bats[A][f0], False)
                            for f2 in range(f0 + 1, f0 + 12):
                                o.deps = set(o.deps) | set(Bats[A][f2].w)
                                Bats[A][f2].rs.append(o)
                P.emit()

        if upto >= 7:
            with ExitStack() as st:
                T = lambda n, s, d: st.enter_context(nc.sbuf_tensor(n, s, d))
                PT = lambda n, s, d: st.enter_context(nc.psum_tensor(n, s, d))
                gb = T("gb3", [128, D], F32)
                att = [T("att%d" % i, [128, 24, 512], BF16) for i in range(2)]
                x1t = [T("x1t%d" % i, [128, D], F32) for i in range(2)]
                x2 = [T("x2_%d" % i, [128, D], F32) for i in range(2)]
                junk = T("junkF", [128, D], F32)
                st4 = [T("st4F%d" % i, [128, 4], F32) for i in range(2)]
                ot = [T("ot%d" % i, [128, D], F32) for i in range(2)]
                pd = [PT("pd%d" % i, [128, 512], F32) for i in range(4)]
                P = Pass(ctx, "pF2")
                Bwd, Bgb, Bjunk = Buf(), Buf(), Buf()
                Batt, Bx1t, Bx2, Bst4, Bot = [Buf(), Buf()], [Buf(), Buf()], [Buf(), Buf()], [Buf(), Buf()], [Buf(), Buf()]
                Bpd = [Buf() for _ in range(4)]
                P.dma("sp", gb[:], g3.partition_broadcast(128), Bgb, True)
                P.dma("sp", att[0][:], AT[:, :, 0:512].rearrange("c p t -> p c t"), Batt[0], True)
                ti = 0
                pi = 0
                for ci in range(8):
                    s = ci % 2
                    if ci + 1 < 8:
                        P.dma("sp", att[1 - s][:], AT[:, :, 512 * (ci + 1):512 * (ci + 2)].rearrange("c p t -> p c t"),
                              Batt[1 - s], True)
                    for j in range(4):
                        u = ti % 2
                        ti += 1
                        t0 = 512 * ci + 128 * j
                        P.dma("sp", x1t[u][:], X1[128 + t0:128 + t0 + 128, :], Bx1t[u], True)
                        for half in range(2):
                            b = pi % 4
                            pi += 1
                            mm8(P, pd[b][:], lambda k, s=s, j=j: att[s][:, k, 128 * j:128 * (j + 1)],
                                lambda k, half=half: wd[:, k, 512 * half:512 * (half + 1)], [Bwd, Batt[s]], Bpd[b], n=24)
                            P.op("dve", lambda e, u=u, half=half, b=b: e.tensor_tensor(
                                out=x2[u][:, 512 * half:512 * (half + 1)], in0=pd[b][:],
                                in1=x1t[u][:, 512 * half:512 * (half + 1)], op=ALU.add), [Bpd[b], Bx1t[u]], [Bx2[u]])
                        rms_T(P, x2[u][:], Bx2[u], gb, Bgb, junk, Bjunk, st4[u], Bst4[u], ot[u], Bot[u], None, None)
                        P.dma("sp", out[t0:t0 + 128, :], ot[u][:], Bot[u], False)
                P.emit()
    nc._mk_nops = ctx.nops
    return nc


def make_in_maps(inputs):
    f = lambda a: np.ascontiguousarray(np.asarray(a, dtype=np.float32))
    x = f(inputs["x"])
    B, S, _ = x.shape
    half = S // 2
    shared = {
        "w_in": f(inputs["w_in"][0]),
        "w_pa": f(inputs["w_proj_rnn"][0]),
        "w_pb": f(inputs["w_proj_attn"][0]),
        "w_o": f(inputs["w_out"][0]),
        "w_up": f(inputs["w_up"][0]),
        "w_dn": f(inputs["w_down"][0]),
        "rg_wa": f(inputs["rg_wa"][0]),
        "rg_wx": f(inputs["rg_wx"][0]),
        "g1": f(inputs["attn_norm_g"][0]),
        "g2": f(inputs["mlp_norm_g"][0]),
        "g3": f(inputs["final_norm_g"]),
        "lamv": f(np.stack([inputs["lam_q1"][0], inputs["lam_k1"][0], inputs["lam_q2"][0], inputs["lam_k2"][0]])),
        "sublg": f(np.asarray(inputs["subln_g"][0]).reshape(128, 1)),
    }
    cw = np.asarray(inputs["rnn_conv_w"][0], np.float32)
    rows = [cw[0], cw[1], cw[2], cw[3], inputs["rnn_conv_b"][0], inputs["rg_ba"][0], inputs["rg_bx"][0],
            inputs["rg_lambda"][0]]
    pr = np.stack([np.asarray(r, np.float32).reshape(8, 128) for r in rows])
    shared["par_rnn"] = f(pr.transpose(2, 0, 1))
    fw_ = np.asarray(inputs["ffn_conv_w"][0], np.float32)
    rows = [fw_[0], fw_[1], fw_[2], inputs["ffn_conv_b"][0]]
    pf = np.stack([np.asarray(r, np.float32).reshape(24, 128) for r in rows])
    shared["par_ffn"] = f(pf.transpose(2, 0, 1))
    in_maps = []
    for b in range(B):
        for h in range(2):
            m = dict(shared)
            xi = np.zeros((NT, D), np.float32)
            if h == 1:
                xi[:] = x[b]
            else:
                xi[half:] = x[b, :half]
            m["xin"] = xi
            fl = np.zeros((128, 2), np.float32)
            fl[:, 0] = 0.0 if h == 1 else -30000.0
            fl[:, 1] = 1.0 if h == 1 else 0.0
            m["flags"] = fl
            in_maps.append(m)
    return in_maps


_NC_CACHE = {}


def kernel(**inputs):
    in_maps = make_in_maps(inputs)
    if "nc" not in _NC_CACHE:
        _NC_CACHE["nc"] = build_nc()
    nc = _NC_CACHE["nc"]
    res = run_bass_kernel_spmd(nc, in_maps, core_ids=list(range(8)))
    x = np.asarray(inputs["x"])
    B, S, _ = x.shape
    outp = np.empty((B, S, D), np.float32)
    k = 0
    for b in range(B):
        for h in range(2):
            outp[b, h * NOWN:(h + 1) * NOWN] = res.results[k]["out"]
            k += 1
    return outp
```

```python
import math
import os
from contextlib import ExitStack

import numpy as np
import concourse.bass as bass
import concourse.mybir as mybir
from concourse.bass_utils import run_bass_kernel_spmd

F32 = mybir.dt.float32
BF16 = mybir.dt.bfloat16
AF = mybir.ActivationFunctionType
ALU = mybir.AluOpType

D = 1024
NT = 8192
NOWN = 4096
Q0 = 3968
NQ = NT - Q0
CH = [(Q0, 128)] + [(NOWN + 512 * i, 512) for i in range(8)]
EPS = 1e-6
LAMBDA_INIT = 0.8 - 0.6 * math.exp(-0.3 * 0)
DFF = 3072
GELU = AF.Gelu_apprx_tanh

ENGS = ("sp", "act", "dve", "pool", "pe")
N_DSEM = 44
N_BG = 6


class Buf:
    __slots__ = ("name", "w", "rs", "dsem")

    def __init__(self, name="b"):
        self.name = name
        self.w = []
        self.rs = []
        self.dsem = None


class Op:
    __slots__ = ("eng", "fn", "deps", "sig", "sem", "val", "dma")

    def __init__(self, eng, fn, dma=False):
        self.eng = eng
        self.fn = fn
        self.deps = ()
        self.sig = dma
        self.sem = None
        self.val = 0
        self.dma = dma


class Ctx:
    def __init__(self, nc, stack):
        self.nc = nc
        self.esem = {e: stack.enter_context(nc.semaphore("es_" + e)) for e in ENGS}
        self.ecount = {e: 0 for e in ENGS}
        self.dsems = [stack.enter_context(nc.semaphore("ds%d" % i)) for i in range(N_DSEM)]
        self.dcount = [0] * N_DSEM
        self.nops = 0
        self.bgsems = [stack.enter_context(nc.semaphore("bg%d" % i)) for i in range(N_BG)]
        self.bgcount = [0] * N_BG
        self.bglast = [None] * N_BG


class Pass:
    def __init__(self, ctx, name="p"):
        self.ctx = ctx
        self.name = name
        self.ops = {e: [] for e in ENGS}
        self.next_dsem = 0
        self.used_dsems = set()

    def _record(self, o, reads, writes):
        deps = set()
        for b in reads:
            deps.update(b.w)
        for b in writes:
            deps.update(b.w)
            deps.update(b.rs)
        if o.eng == "pe":
            deps = {d for d in deps if d.eng != "pe"}
        o.deps = deps
        for b in reads:
            b.rs.append(o)
        for b in writes:
            b.w = [o]
            b.rs = []
        self.ops[o.eng].append(o)
        return o

    def op(self, eng, fn, reads=(), writes=()):
        return self._record(Op(eng, fn), reads, writes)

    def dma_bg(self, queue, out, in_, gid, extra_deps=()):
        ctx = self.ctx
        ctx.bgcount[gid] += 16
        o = Op(queue, lambda e: e.dma_start(out=out, in_=in_), dma=True)
        o.sem = ctx.bgsems[gid]
        o.val = ctx.bgcount[gid]
        o.deps = set(extra_deps)
        ctx.bglast[gid] = o
        self.ops[queue].append(o)
        return o

    def dma(self, queue, out, in_, sbuf, load, extra_deps=(), **kw):
        ctx = self.ctx
        if sbuf.dsem is None:
            assert self.next_dsem < N_DSEM, "out of DMA semaphores"
            sbuf.dsem = self.next_dsem
            self.next_dsem += 1
        i = sbuf.dsem
        self.used_dsems.add(i)
        ctx.dcount[i] += 16
        o = Op(queue, lambda e: e.dma_start(out=out, in_=in_, **kw), dma=True)
        o.sem = ctx.dsems[i]
        o.val = ctx.dcount[i]
        if load:
            self._record(o, [], [sbuf])
        else:
            self._record(o, [sbuf], [])
        if extra_deps:
            o.deps = set(o.deps) | {d for d in extra_deps if d is not None}
        return o

    def emit(self):
        ctx = self.ctx
        nc = ctx.nc
        for e in ENGS:
            for o in self.ops[e]:
                for d in o.deps:
                    d.sig = True
        for e in ENGS:
            for o in self.ops[e]:
                if o.dma:
                    continue
                if o.sig:
                    ctx.ecount[e] += 1
                    o.sem = ctx.esem[e]
                    o.val = ctx.ecount[e]
        final_d = [(ctx.dsems[i], ctx.dcount[i]) for i in sorted(self.used_dsems)]
        ops = self.ops
        engmap = {"sp": "sync", "act": "scalar", "dve": "vector", "pool": "gpsimd", "pe": "tensor"}

        def run(ename):
            def body(e):
                waited = {}
                for o in ops[ename]:
                    for d in o.deps:
                        k = id(d.sem)
                        if waited.get(k, -1) >= d.val:
                            continue
                        e.wait_ge(d.sem, d.val)
                        waited[k] = d.val
                    ins = o.fn(e)
                    if o.sig:
                        ins.then_inc(o.sem, 16 if o.dma else 1)
                if ename == "sp":
                    for (s, v) in final_d:
                        if v > 0 and waited.get(id(s), -1) < v:
                            e.wait_ge(s, v)
            return body

        with nc.Block(no_gpsimd_drain=True) as block:
            for ename in ENGS:
                if ops[ename] or ename == "sp":
                    getattr(block, engmap[ename])(run(ename))
        ctx.nops += sum(len(v) for v in ops.values())


def build_nc(debug=False, upto=99):
    nc = bass.Bass("TRN2", target_bir_lowering=False)
    IN = lambda n, s: nc.dram_tensor(n, s, F32, kind="ExternalInput").ap()
    xin = IN("xin", [NT, D])
    w_in = IN("w_in", [D, 7168])
    w_pa = IN("w_pa", [D, D])
    w_pb = IN("w_pb", [D, D])
    w_o = IN("w_o", [D, D])
    w_up = IN("w_up", [D, 2 * DFF])
    w_dn = IN("w_dn", [DFF, D])
    rg_wa = IN("rg_wa", [8, 128, 128])
    rg_wx = IN("rg_wx", [8, 128, 128])
    g1 = IN("g1", [D])
    g2 = IN("g2", [D])
    g3 = IN("g3", [D])
    par_rnn = IN("par_rnn", [128, 8, 8])
    par_ffn = IN("par_ffn", [128, 4, 24])
    lamv = IN("lamv", [4, 64])
    sublg_in = IN("sublg", [128, 1])
    flags = IN("flags", [128, 2])
    out = nc.dram_tensor("out", [NOWN, D], F32, kind="ExternalOutput").ap()

    def SCR(n, s, dt):
        if debug:
            return nc.dram_tensor(n, s, dt, kind="ExternalOutput").ap()
        return nc.dram_tensor(n, s, dt).ap()
    HT = SCR("HT", [8, 128, NT], BF16)
    KT = SCR("KT", [8, 128, NT], BF16)
    VV = SCR("VV", [NT, D], BF16)
    QT = SCR("QT", [8, 128, NQ], BF16)
    YA = SCR("YA", [8, 128, NQ], BF16)
    SG = SCR("SG", [16, 128, NQ], BF16)
    HD = SCR("HD", [8, 128, NQ], F32)
    X1 = SCR("X1", [NQ, D], F32)
    H2T = SCR("H2T", [8, 128, NQ], BF16)
    AT = SCR("AT", [24, 128, NOWN], BF16)

    WB = {"w_in": nc.dram_tensor("wb_in", [D, 7168], BF16).ap(),
          "w_pa": nc.dram_tensor("wb_pa", [D, D], BF16).ap(),
          "w_pb": nc.dram_tensor("wb_pb", [D, D], BF16).ap(),
          "w_o": nc.dram_tensor("wb_o", [D, D], BF16).ap(),
          "w_up": nc.dram_tensor("wb_up", [D, 2 * DFF], BF16).ap(),
          "w_dn": nc.dram_tensor("wb_dn", [DFF, D], BF16).ap(),
          "rg_wa": nc.dram_tensor("wb_wa", [8, 128, 128], BF16).ap(),
          "rg_wx": nc.dram_tensor("wb_wx", [8, 128, 128], BF16).ap()}
    WF = {"w_in": w_in, "w_pa": w_pa, "w_pb": w_pb, "w_o": w_o, "w_up": w_up, "w_dn": w_dn}
    GA, GC, GB, GE, GF1, GF2 = range(6)

    with ExitStack() as gst:
        ctx = Ctx(nc, gst)
        GT = lambda n, s, d: gst.enter_context(nc.sbuf_tensor(n, s, d))
        ident = GT("ident", [128, 128], BF16)
        ones_bf = GT("ones_bf", [128, 512], BF16)
        onesf = GT("onesf", [128, 128], F32)
        ones1f = GT("ones1f", [128, 128], F32)
        masks = GT("masks", [128, 4, 512], BF16)
        flg = GT("flg", [128, 2], F32)
        lamt = GT("lamt", [128, 4], F32)
        sublg = GT("sublg_t", [128, 1], F32)
        prn = GT("prn", [128, 8, 8], F32)
        c12 = GT("c12", [128, 2, 8], F32)
        pff = GT("pff", [128, 4, 24], F32)
        ctxb = flg[:, 0:1]
        ctxf = flg[:, 1:2]

        def cast_bg(P, name, gid, r0, r1, c0, c1, extra_deps=()):
            for r in range(r0, r1, 128):
                P.dma_bg("pool", WB[name][r:r + 128, c0:c1], WF[name][r:r + 128, c0:c1], gid, extra_deps=extra_deps)

        def load_w(P, dst, name, gid, row0, col0, ncols, kc_n, buf):
            v = WB[name][row0:row0 + 128 * kc_n, :].rearrange("(kc p) c -> p kc c", p=128)
            for kc in range(0, kc_n, 2):
                P.dma("sp", dst[:, kc:kc + 2, :], v[:, kc:kc + 2, col0:col0 + ncols], buf, True,
                      extra_deps=[ctx.bglast[gid]])

        stWA = ExitStack()
        wk = stWA.enter_context(nc.sbuf_tensor("wk", [128, 8, 1024], BF16, side="right"))
        wv = stWA.enter_context(nc.sbuf_tensor("wv", [128, 8, 1024], BF16, side="right"))
        with ExitStack() as st:
            T = lambda n, s, d: st.enter_context(nc.sbuf_tensor(n, s, d))
            lv = T("lv", [128, 4, 64], F32)
            junk = T("junk_s", [128, 64], F32)
            dots = T("dots", [128, 2], F32)
            tmp8 = T("tmp8", [128, 8], F32)
            P = Pass(ctx, "setup")
            B = {k: Buf(k) for k in ["ones", "ident", "onesf", "masks", "flg", "lv", "junk", "dots", "lamt", "sublg", "prn", "c12", "pff", "tmp8"]}
            wv_ = w_in.rearrange("(kc p) c -> p kc c", p=128)
            for kc in range(8):
                P.dma_bg("pool", wk[:, kc, :], wv_[:, kc, 3072:4096], GA)
            for kc in range(8):
                P.dma_bg("pool", wv[:, kc, :], wv_[:, kc, 4096:5120], GA)
            P.op("pool", lambda e: e.memset(ones_bf[:], 1.0), [], [B["ones"]])
            P.op("pool", lambda e: e.memset(onesf[:], 1.0 / 128.0), [], [B["onesf"]])
            P.op("pool", lambda e: e.memset(ones1f[:], 1.0), [], [Buf()])
            P.op("pool", lambda e: e.affine_select(out=ident[:], in_=ones_bf[:, 0:128], pattern=[[-1, 128]],
                                                   compare_op=ALU.is_equal, fill=0.0, base=0, channel_multiplier=1),
                 [B["ones"]], [B["ident"]])
            for j in range(4):
                P.op("pool", lambda e, j=j: e.affine_select(out=masks[:, j, :], in_=ones_bf[:], pattern=[[1, 512]],
                                                            compare_op=ALU.is_ge, fill=0.0, base=-128 * j,
                                                            channel_multiplier=-1),
                     [B["ones"]], [B["masks"]])
            P.dma("sp", flg[:], flags, B["flg"], True)
            P.dma("sp", prn[:], par_rnn, B["prn"], True)
            P.dma("sp", pff[:], par_ffn, B["pff"], True)
            P.dma("sp", sublg[:], sublg_in, B["sublg"], True)
            for i in range(4):
                P.dma("sp", lv[:, i, :], lamv[i, :].partition_broadcast(128), B["lv"], True)
            for i in range(2):
                P.op("dve", lambda e, i=i: e.scalar_tensor_tensor(out=junk[:], in0=lv[:, 2 * i, :], scalar=1.0,
                                                                  in1=lv[:, 2 * i + 1, :], op0=ALU.mult, op1=ALU.mult,
                                                                  accum_out=dots[:, i:i + 1]),
                     [B["lv"]], [B["junk"], B["dots"]])
            P.op("act", lambda e: e.activation(out=dots[:], in_=dots[:], func=AF.Exp), [B["dots"]], [B["dots"]])
            P.op("dve", lambda e: e.scalar_tensor_tensor(out=lamt[:, 0:1], in0=dots[:, 0:1], scalar=LAMBDA_INIT,
                                                         in1=dots[:, 1:2], op0=ALU.add, op1=ALU.subtract),
                 [B["dots"]], [B["lamt"]])
            P.op("dve", lambda e: e.tensor_scalar(out=lamt[:, 1:2], in0=lamt[:, 0:1], scalar1=-1.0, scalar2=None,
                                                  op0=ALU.mult), [B["lamt"]], [B["lamt"]])
            P.op("dve", lambda e: e.tensor_scalar(out=sublg[:], in0=sublg[:], scalar1=(1.0 - LAMBDA_INIT), scalar2=None,
                                                  op0=ALU.mult), [B["sublg"]], [B["sublg"]])
            P.op("act", lambda e: e.activation(out=tmp8[:], in_=prn[:, 7, :], func=AF.Exp, scale=-1.0),
                 [B["prn"]], [B["tmp8"]])
            P.op("act", lambda e: e.activation(out=tmp8[:], in_=tmp8[:], func=AF.Ln, bias=1.0),
                 [B["tmp8"]], [B["tmp8"]])
            P.op("dve", lambda e: e.tensor_scalar(out=c12[:, 0, :], in0=tmp8[:], scalar1=-8.0, scalar2=None, op0=ALU.mult),
                 [B["tmp8"]], [B["c12"]])
            P.op("dve", lambda e: e.tensor_scalar(out=c12[:, 1, :], in0=tmp8[:], scalar1=-16.0, scalar2=None, op0=ALU.mult),
                 [B["tmp8"]], [B["c12"]])
            P.emit()

        def rms_T(P, xt, Bx, gb, Bg, junk, Bjunk, st4, Bst4, hn, Bhn, pT, BpT):
            P.op("dve", lambda e: e.scalar_tensor_tensor(out=junk[:], in0=xt, scalar=1.0, in1=xt, op0=ALU.mult,
                                                         op1=ALU.mult, accum_out=st4[:, 0:1]),
                 [Bx], [Bjunk, Bst4])
            P.op("dve", lambda e: e.tensor_scalar(out=st4[:, 1:2], in0=st4[:, 0:1], scalar1=1.0 / D, scalar2=EPS,
                                                  op0=ALU.mult, op1=ALU.add), [Bst4], [Bst4])
            P.op("act", lambda e: e.activation(out=st4[:, 2:3], in_=st4[:, 1:2], func=AF.Sqrt), [Bst4], [Bst4])
            P.op("dve", lambda e: e.reciprocal(out=st4[:, 3:4], in_=st4[:, 2:3]), [Bst4], [Bst4])
            P.op("dve", lambda e: e.scalar_tensor_tensor(out=hn[:], in0=xt, scalar=st4[:, 3:4], in1=gb[:],
                                                         op0=ALU.mult, op1=ALU.mult), [Bx, Bst4, Bg], [Bhn])
            if pT is not None:
                for c in range(8):
                    P.op("pe", lambda e, c=c: e.transpose(out=pT[:, c, :], in_=hn[:, 128 * c:128 * (c + 1)],
                                                          identity=ident[:]), [Bhn], [BpT])

        def WT(stk, name, shape, side):
            return stk.enter_context(nc.sbuf_tensor(name, shape, BF16, side=side))


        def prefetch_A(P):
            load_w(P, wk, "w_in", GA, 0, 2048 + 1024, 1024, 8, Buf())
            load_w(P, wv, "w_in", GA, 0, 2048 + 2048, 1024, 8, Buf())

        def mm8(P, ps, lhs_fn, rhs_fn, reads, Bps, n=8):
            for k in range(n):
                P.op("pe", lambda e, k=k: e.matmul(ps, lhsT=lhs_fn(k), rhs=rhs_fn(k), start=(k == 0), stop=(k == n - 1)),
                     reads, [Bps])

        evac_rr = [0]

        def evac(P, out_ap, in_ap, reads, writes):
            evac_rr[0] += 1
            if evac_rr[0] % 2 == 0:
                P.op("act", lambda e: e.activation(out=out_ap, in_=in_ap, func=AF.Copy), reads, writes)
            else:
                P.op("dve", lambda e: e.tensor_copy(out=out_ap, in_=in_ap), reads, writes)

        stWC = ExitStack()
        wq = WT(stWC, "wq", [128, 8, 1024], "left")
        wg = WT(stWC, "wg", [128, 8, 2048], "left")

        def prefetch_C(P):
            load_w(P, wq, "w_in", GC, 0, 2048, 1024, 8, Buf())
            load_w(P, wg, "w_in", GC, 0, 5120, 2048, 8, Buf())

        if upto >= 1:
            with ExitStack() as st:
                T = lambda n, s, d: st.enter_context(nc.sbuf_tensor(n, s, d))
                PT = lambda n, s, d: st.enter_context(nc.psum_tensor(n, s, d))
                gb = T("gb", [128, D], F32)
                xt = [T("xt%d" % i, [128, D], F32) for i in range(3)]
                junk = T("junk", [128, D], F32)
                st4 = [T("st4_%d" % i, [128, 4], F32) for i in range(2)]
                hn = [T("hn%d" % i, [128, D], BF16) for i in range(2)]
                pT = [PT("pT%d" % i, [128, 8, 128], BF16) for i in range(2)]
                hts = [T("hts%d" % i, [128, 8, 512], BF16) for i in range(2)]
                kts = [T("kts%d" % i, [128, 8, 512], BF16) for i in range(2)]
                vs = [T("vs%d" % i, [128, 1024], BF16) for i in range(2)]
                ps = [PT("psA%d" % i, [128, 512], F32) for i in range(4)]
                P = Pass(ctx, "p0A")
                Bgb, Bjunk, Bwk, Bwv = Buf(), Buf(), Buf(), Buf()
                Bxt = [Buf() for _ in range(3)]
                Bst4 = [Buf() for _ in range(2)]
                Bhn = [Buf() for _ in range(2)]
                BpT = [Buf() for _ in range(2)]
                Bhts = [[Buf() for _ in range(4)] for _ in range(2)]
                Bkts = [Buf(), Buf()]
                Bvs = [Buf(), Buf()]
                Bps = [Buf() for _ in range(4)]
                P.dma("sp", gb[:], g1.partition_broadcast(128), Bgb, True)
                NTILE = NT // 128
                P.dma("sp", xt[0][:], xin[0:128, :], Bxt[0], True)
                P.dma("sp", xt[1][:], xin[128:256, :], Bxt[1], True)
                Bwk.w = [ctx.bglast[GA]]
                Bwv.w = [ctx.bglast[GA]]
                cast_bg(P, "w_in", GC, 0, D, 2048, 3072)
                cast_bg(P, "w_in", GC, 0, D, 5120, 7168)
                cast_bg(P, "w_in", GB, 0, D, 0, 2048)
                P.dma_bg("pool", WB["rg_wa"], rg_wa, GB)
                P.dma_bg("pool", WB["rg_wx"], rg_wx, GB)
                cast_bg(P, "w_pa", GE, 0, D, 0, D)
                cast_bg(P, "w_pb", GE, 0, D, 0, D)
                cast_bg(P, "w_o", GE, 0, D, 0, D)

                def norm_a(i):
                    if i >= NTILE:
                        return
                    s3, s2 = i % 3, i % 2
                    if i + 2 < NTILE:
                        P.dma("sp", xt[(i + 2) % 3][:], xin[128 * (i + 2):128 * (i + 3), :], Bxt[(i + 2) % 3], True)
                    rms_T(P, xt[s3][:], Bxt[s3], gb, Bgb, junk, Bjunk, st4[s2], Bst4[s2], hn[s2], Bhn[s2], None, None)

                def norm_sub(t, q):
                    S = t % 2
                    i = 4 * t + q
                    s2 = i % 2
                    for c in range(8):
                        P.op("pe", lambda e, c=c: e.transpose(out=pT[s2][:, c, :], in_=hn[s2][:, 128 * c:128 * (c + 1)],
                                                              identity=ident[:]), [Bhn[s2]], [BpT[s2]])
                    P.op("act", lambda e: e.activation(out=hts[S][:, :, 128 * q:128 * (q + 1)], in_=pT[s2][:], func=AF.Copy),
                         [BpT[s2]], [Bhts[S][q]])
                    norm_a(i + 2)
                    if q == 3:
                        t0 = 512 * t
                        o = P.dma("sp", HT[:, :, t0:t0 + 512].rearrange("c p t -> p c t"), hts[S][:], Bhts[S][0], False)
                        for qq in range(1, 4):
                            o.deps = set(o.deps) | set(Bhts[S][qq].w)
                            Bhts[S][qq].rs.append(o)

                cnt = {"pi": 0, "vi": 0}

                def kv_part(i, p):
                    s = i % 2
                    for hd in (2 * p, 2 * p + 1):
                        b = cnt["pi"] % 4
                        cnt["pi"] += 1
                        mm8(P, ps[b][:], lambda k, hd=hd: wk[:, k, 128 * hd:128 * (hd + 1)], lambda k: hts[s][:, k, :],
                            [Bwk] + Bhts[s], Bps[b])
                        evac(P, kts[s][:, hd, :], ps[b][:], [Bps[b]], [Bkts[s]])
                    if p == 3:
                        P.dma("sp", KT[:, :, 512 * i:512 * (i + 1)].rearrange("h p t -> p h t"), kts[s][:], Bkts[s], False)
                    j = p
                    v = cnt["vi"] % 2
                    cnt["vi"] += 1
                    for half in range(2):
                        b = cnt["pi"] % 4
                        cnt["pi"] += 1
                        mm8(P, ps[b][:], lambda k: hts[s][:, k, 128 * j:128 * (j + 1)],
                            lambda k, half=half: wv[:, k, 512 * half:512 * (half + 1)], [Bwv] + Bhts[s], Bps[b])
                        evac(P, vs[v][:, 512 * half:512 * (half + 1)], ps[b][:], [Bps[b]], [Bvs[v]])
                    r0 = 512 * i + 128 * j
                    P.dma("sp", VV[r0:r0 + 128, :], vs[v][:], Bvs[v], False)

                norm_a(0)
                norm_a(1)
                for q in range(4):
                    norm_sub(0, q)
                for t in range(16):
                    if t == 2:
                        prefetch_C(P)
                    for p in range(4):
                        if t + 1 < 16:
                            norm_sub(t + 1, p)
                        kv_part(t, p)
                P.emit()

        stWA.close()
        stWB = ExitStack()
        wxr = WT(stWB, "wxr", [128, 8, 1024], "right")
        wgr = WT(stWB, "wgr", [128, 8, 1024], "right")
        wa = WT(stWB, "wa", [128, 8, 128], "right")
        wx = WT(stWB, "wx", [128, 8, 128], "right")

        def prefetch_B(P):
            load_w(P, wxr, "w_in", GB, 0, 0, 1024, 8, Buf())
            load_w(P, wgr, "w_in", GB, 0, 1024, 1024, 8, Buf())
            P.dma("sp", wa[:], WB["rg_wa"].rearrange("n i j -> i n j"), Buf(), True, extra_deps=[ctx.bglast[GB]])
            P.dma("sp", wx[:], WB["rg_wx"].rearrange("n i j -> i n j"), Buf(), True, extra_deps=[ctx.bglast[GB]])

        if upto >= 3:
            with ExitStack() as st:
                T = lambda n, s, d: st.enter_context(nc.sbuf_tensor(n, s, d))
                PT = lambda n, s, d: st.enter_context(nc.psum_tensor(n, s, d))
                htt = [T("httC%d" % i, [128, 8, 512], BF16) for i in range(2)]
                qs = [T("qs%d" % i, [128, 8, 512], BF16) for i in range(2)]
                sgs = [T("sgs%d" % i, [128, 16, 512], BF16) for i in range(2)]
                ps = [PT("psC%d" % i, [128, 512], F32) for i in range(4)]
                P = Pass(ctx, "pC")
                Bwq, Bwg = Buf(), Buf()
                Bhtt = [Buf(), Buf()]
                Bqs = [Buf(), Buf()]
                Bsgs = [Buf(), Buf()]
                Bps = [Buf() for _ in range(4)]
                cast_bg(P, "w_up", GF1, 0, D, 0, 2 * DFF)
                g0, n = CH[0]
                P.dma("sp", htt[0][:, :, 0:n], HT[:, :, g0:g0 + n].rearrange("c p t -> p c t"), Bhtt[0], True)
                pi = 0
                for ci, (g0, n) in enumerate(CH):
                    s = ci % 2
                    loc = g0 - Q0
                    if ci + 1 < len(CH):
                        g1_, n1 = CH[ci + 1]
                        P.dma("sp", htt[1 - s][:, :, 0:n1], HT[:, :, g1_:g1_ + n1].rearrange("c p t -> p c t"), Bhtt[1 - s], True)
                    if ci == 2:
                        prefetch_B(P)
                    for hd in range(8):
                        b = pi % 4
                        pi += 1
                        mm8(P, ps[b][:, 0:n], lambda k, hd=hd: wq[:, k, 128 * hd:128 * (hd + 1)],
                            lambda k, s=s, n=n: htt[s][:, k, 0:n], [Bwq, Bhtt[s]], Bps[b])
                        evac(P, qs[s][:, hd, 0:n], ps[b][:, 0:n], [Bps[b]], [Bqs[s]])
                    P.dma("sp", QT[:, :, loc:loc + n].rearrange("h p t -> p h t"), qs[s][:, :, 0:n], Bqs[s], False)
                    for c in range(16):
                        b = pi % 4
                        pi += 1
                        mm8(P, ps[b][:, 0:n], lambda k, c=c: wg[:, k, 128 * c:128 * (c + 1)],
                            lambda k, s=s, n=n: htt[s][:, k, 0:n], [Bwg, Bhtt[s]], Bps[b])
                        P.op("act", lambda e, c=c, b=b, s=s, n=n: e.activation(out=sgs[s][:, c, 0:n], in_=ps[b][:, 0:n],
                                                                                func=AF.Sigmoid), [Bps[b]], [Bsgs[s]])
                    P.dma("sp", SG[:, :, loc:loc + n].rearrange("h p t -> p h t"), sgs[s][:, :, 0:n], Bsgs[s], False)
                P.emit()

        stWC.close()
        if upto >= 2:
            with ExitStack() as st:
                T = lambda n, s, d: st.enter_context(nc.sbuf_tensor(n, s, d))
                PT = lambda n, s, d: st.enter_context(nc.psum_tensor(n, s, d))
                htt = [T("httB%d" % i, [128, 8, 512], BF16) for i in range(2)]
                xrb = T("xrb", [128, 8, 515], F32)
                xc = T("xc", [128, 8, 512], F32)
                xcb = T("xcb", [128, 8, 512], BF16)
                ra = T("ra", [128, 8, 512], F32)
                a2 = T("a2", [128, 8, 512], F32)
                ib = T("ib", [128, 8, 512], F32)
                hh = a2
                gg = T("gg", [128, 8, 512], BF16)
                ys = [T("ys%d" % i, [128, 8, 512], BF16) for i in range(1)]
                hst = T("hst", [128, 8], F32)
                ps = [PT("psB%d" % i, [128, 512], F32) for i in range(8)]
                P = Pass(ctx, "pB")
                Bwxr, Bwgr, Bwa, Bwx, Bhst = Buf(), Buf(), Buf(), Buf(), Buf()
                Bhtt = [Buf(), Buf()]
                Bxrb = [Buf() for _ in range(8)]
                Bxc = [Buf() for _ in range(8)]
                Bxcb = [Buf() for _ in range(8)]
                Bra = [Buf() for _ in range(8)]
                Ba2 = [Buf() for _ in range(8)]
                Bib = [Buf() for _ in range(8)]
                Bhh = Ba2
                Bgg = [Buf() for _ in range(8)]
                Bys = [[Buf() for _ in range(8)] for _ in range(1)]
                Bps = [Buf() for _ in range(8)]
                Bhs = [Buf() for _ in range(8)]
                cast_bg(P, "w_dn", GF2, 0, DFF, 0, D)
                P.op("pool", lambda e: e.memset(xrb[:, :, 0:3], 0.0), [], Bxrb)
                P.op("pool", lambda e: e.memset(hst[:], 0.0), [], Bhs)
                P.dma("sp", htt[0][:], HT[:, :, 0:512].rearrange("c p t -> p c t"), Bhtt[0], True)
                pi = 0
                yi = 0
                for i in range(16):
                    s = i % 2
                    own = i >= 7
                    if i + 1 < 16:
                        P.dma("sp", htt[1 - s][:], HT[:, :, 512 * (i + 1):512 * (i + 2)].rearrange("c p t -> p c t"),
                              Bhtt[1 - s], True)
                    for ct in range(8):
                        b = pi % 8
                        pi += 1
                        mm8(P, ps[b][:], lambda k, ct=ct: wxr[:, k, 128 * ct:128 * (ct + 1)], lambda k, s=s: htt[s][:, k, :],
                            [Bwxr, Bhtt[s]], Bps[b])
                        P.op("act", lambda e, ct=ct, b=b: e.activation(out=xrb[:, ct, 3:515], in_=ps[b][:], func=AF.Copy),
                             [Bps[b]], [Bxrb[ct]])
                    for ct in range(8):
                        P.op("dve", lambda e, ct=ct: e.tensor_scalar(out=xc[:, ct, :], in0=xrb[:, ct, 0:512],
                                                                     scalar1=prn[:, 0, ct:ct + 1], scalar2=prn[:, 4, ct:ct + 1],
                                                                     op0=ALU.mult, op1=ALU.add), [Bxrb[ct]], [Bxc[ct]])
                    for k in range(1, 4):
                        for ct in range(8):
                            P.op("dve", lambda e, ct=ct, k=k: e.scalar_tensor_tensor(
                                out=xc[:, ct, :], in0=xrb[:, ct, k:k + 512], scalar=prn[:, k, ct:ct + 1], in1=xc[:, ct, :],
                                op0=ALU.mult, op1=ALU.add), [Bxrb[ct], Bxc[ct]], [Bxc[ct]])
                    for ct in range(8):
                        P.op("pool", lambda e, ct=ct: e.tensor_copy(out=xrb[:, ct, 0:3], in_=xrb[:, ct, 512:515]),
                             [Bxrb[ct]], [Bxrb[ct]])
                        P.op("act", lambda e, ct=ct: e.activation(out=xcb[:, ct, :], in_=xc[:, ct, :], func=AF.Copy),
                             [Bxc[ct]], [Bxcb[ct]])
                    gps = []
                    for ct in range(8):
                        b1 = pi % 8
                        pi += 1
                        P.op("pe", lambda e, ct=ct, b1=b1: e.matmul(ps[b1][:], lhsT=wa[:, ct, :], rhs=xcb[:, ct, :],
                                                                     start=True, stop=True), [Bwa, Bxcb[ct]], [Bps[b1]])
                        P.op("act", lambda e, ct=ct, b1=b1: e.activation(out=ra[:, ct, :], in_=ps[b1][:], func=AF.Sigmoid,
                                                                          bias=prn[:, 5, ct:ct + 1]), [Bps[b1]], [Bra[ct]])
                    for ct in range(8):
                        b2 = pi % 8
                        pi += 1
                        P.op("pe", lambda e, ct=ct, b2=b2: e.matmul(ps[b2][:], lhsT=wx[:, ct, :], rhs=xcb[:, ct, :],
                                                                     start=True, stop=True), [Bwx, Bxcb[ct]], [Bps[b2]])
                        P.op("act", lambda e, ct=ct, b2=b2: e.activation(out=ib[:, ct, :], in_=ps[b2][:], func=AF.Sigmoid,
                                                                          bias=prn[:, 6, ct:ct + 1]), [Bps[b2]], [Bib[ct]])
                    for ct in range(8):
                        P.op("act", lambda e, ct=ct: e.activation(out=a2[:, ct, :], in_=ra[:, ct, :], func=AF.Exp,
                                                                  scale=c12[:, 1, ct:ct + 1]), [Bra[ct]], [Ba2[ct]])
                    for ct in range(8):
                        P.op("act", lambda e, ct=ct: e.activation(out=ra[:, ct, :], in_=ra[:, ct, :], func=AF.Exp,
                                                                  scale=c12[:, 0, ct:ct + 1]), [Bra[ct], Ba2[ct]], [Bra[ct]])
                    for ct in range(8):
                        P.op("dve", lambda e, ct=ct: e.tensor_scalar(out=a2[:, ct, :], in0=a2[:, ct, :], scalar1=1.0,
                                                                     scalar2=-1.0, op0=ALU.min, op1=ALU.mult),
                             [Ba2[ct]], [Ba2[ct]])
                        P.op("pool", lambda e, ct=ct: e.tensor_tensor(out=ib[:, ct, :], in0=ib[:, ct, :], in1=xc[:, ct, :],
                                                                      op=ALU.mult), [Bib[ct], Bxc[ct]], [Bib[ct]])
                    for ct in range(8):
                        P.op("act", lambda e, ct=ct: e.activation(out=a2[:, ct, :], in_=a2[:, ct, :], func=AF.Sqrt, bias=1.0),
                             [Ba2[ct]], [Ba2[ct]])
                    for ct in range(8):
                        P.op("dve", lambda e, ct=ct: e.tensor_tensor(out=ib[:, ct, :], in0=ib[:, ct, :], in1=a2[:, ct, :],
                                                                     op=ALU.mult), [Bib[ct], Ba2[ct]], [Bib[ct]])
                    for ct in range(8):
                        P.op("dve", lambda e, ct=ct: e.tensor_tensor_scan(out=hh[:, ct, :], data0=ra[:, ct, :],
                                                                          data1=ib[:, ct, :], initial=hst[:, ct:ct + 1],
                                                                          op0=ALU.mult, op1=ALU.add),
                             [Bra[ct], Bib[ct], Bhs[ct]], [Bhh[ct]])
                        if i == 7:
                            P.op("pool", lambda e, ct=ct: e.tensor_scalar(out=hst[:, ct:ct + 1], in0=hh[:, ct, 511:512],
                                                                          scalar1=ctxf, scalar2=None, op0=ALU.mult),
                                 [Bhh[ct]], [Bhs[ct]])
                        else:
                            P.op("pool", lambda e, ct=ct: e.tensor_copy(out=hst[:, ct:ct + 1], in_=hh[:, ct, 511:512]),
                                 [Bhh[ct]], [Bhs[ct]])
                    if own:
                        y = 0
                        for ct in range(8):
                            b = pi % 8
                            pi += 1
                            mm8(P, ps[b][:], lambda k, ct=ct: wgr[:, k, 128 * ct:128 * (ct + 1)],
                                lambda k, s=s: htt[s][:, k, :], [Bwgr, Bhtt[s]], Bps[b])
                            P.op("act", lambda e, ct=ct, b=b: e.activation(out=gg[:, ct, :], in_=ps[b][:], func=GELU),
                                 [Bps[b]], [Bgg[ct]])
                        for ct in range(8):
                            P.op("pool", lambda e, ct=ct, y=y: e.tensor_tensor(out=ys[y][:, ct, :], in0=hh[:, ct, :],
                                                                               in1=gg[:, ct, :], op=ALU.mult),
                                 [Bhh[ct], Bgg[ct]], [Bys[y][ct]])
                        if i == 7:
                            o = P.dma("sp", YA[:, :, 0:128].rearrange("c p t -> p c t"), ys[y][:, :, 384:512], Bys[y][0], False)
                        else:
                            l0 = 128 + 512 * (i - 8)
                            o = P.dma("sp", YA[:, :, l0:l0 + 512].rearrange("c p t -> p c t"), ys[y][:], Bys[y][0], False)
                        for ct in range(1, 8):
                            o.deps = set(o.deps) | set(Bys[y][ct].w)
                            Bys[y][ct].rs.append(o)
                P.emit()

        stWB.close()
        stWE = ExitStack()
        wpa = WT(stWE, "wpa", [128, 8, 1024], "right")
        wpb = WT(stWE, "wpb", [128, 8, 1024], "right")
        wo = WT(stWE, "wo", [128, 8, 1024], "right")

        def prefetch_E(P):
            load_w(P, wpa, "w_pa", GE, 0, 0, 1024, 8, Buf())
            load_w(P, wpb, "w_pb", GE, 0, 0, 1024, 8, Buf())
            load_w(P, wo, "w_o", GE, 0, 0, 1024, 8, Buf())

        if upto >= 4:
            with ExitStack() as st:
                T = lambda n, s, d: st.enter_context(nc.sbuf_tensor(n, s, d))
                PT = lambda n, s, d: st.enter_context(nc.psum_tensor(n, s, d))
                kth = [T("kth%d" % i, [128, NT], BF16) for i in range(2)]
                vh = [T("vh%d" % i, [128, 64, 128], BF16) for i in range(2)]
                qh = [T("qh%d" % i, [128, NQ], BF16) for i in range(2)]
                pp = [T("pp%d" % i, [128, 2, 512], BF16) for i in range(12)]
                plc = T("plc", [128, 512], F32)
                Bplc = Buf()
                ls = [T("ls%d" % i, [128, 2, 512], F32) for i in range(2)]
                os_ = [T("os%d" % i, [128, 2, 512], F32) for i in range(2)]
                hdt = [T("hdt%d" % i, [128, 512], F32) for i in range(2)]
                accD = T("accD", [128, 512], F32)
                accP = T("accP", [128, 512], F32)
                BaccD, BaccP = Buf(), Buf()
                ps = [PT("psS%d" % i, [128, 2, 512], F32) for i in range(2)]
                po = PT("po", [128, 2, 512], F32)
                pl = PT("pl", [128, 2, 512], F32)
                P = Pass(ctx, "pD")
                Bk = [Buf(), Buf()]
                Bv = [Buf(), Buf()]
                Bq = [Buf(), Buf()]
                Bpp = [[Buf(), Buf()] for _ in range(12)]
                Bs = [Buf(), Buf()]
                Bo = [Buf(), Buf()]
                Bl = [Buf(), Buf()]
                Bls = [[Buf(), Buf()] for _ in range(2)]
                Bos = [[Buf(), Buf()] for _ in range(2)]
                Bhd = [Buf(), Buf()]

                def load_head(hd, s):
                    P.dma("sp", qh[s][:], QT[hd, :, :], Bq[s], True)
                    for q4 in range(4):
                        P.dma("sp", kth[s][:, 2048 * q4:2048 * (q4 + 1)], KT[hd, :, 2048 * q4:2048 * (q4 + 1)], Bk[s], True)
                    vsrc = VV[:, 128 * hd:128 * (hd + 1)].rearrange("(j p) e -> p j e", p=128)
                    for q4 in range(4):
                        P.dma("sp", vh[s][:, 16 * q4:16 * (q4 + 1), :], vsrc[:, 16 * q4:16 * (q4 + 1), :], Bv[s], True)

                LG = 4
                steps = []
                for hd in range(8):
                    for ci, (g0, n) in enumerate(CH):
                        nkt = (g0 + n) // 128
                        for kt in range(nkt):
                            steps.append((hd, ci, kt, nkt))

                def col0(i):
                    hd, ci, kt, nkt = steps[i]
                    g0, n = CH[ci]
                    return max(128 * kt - g0, 0)

                def emit_qk(i):
                    hd, ci, kt, nkt = steps[i]
                    g0, n = CH[ci]
                    s, sb, k0, loc = hd % 2, i % 2, 128 * kt, g0 - Q0
                    c0 = col0(i)
                    P.op("pe", lambda e: e.matmul(ps[sb][:, 0, c0:n], lhsT=kth[s][0:64, k0:k0 + 128],
                                                  rhs=qh[s][0:64, loc + c0:loc + n], start=True, stop=True),
                         [Bk[s], Bq[s]], [Bs[sb]])
                    P.op("pe", lambda e: e.matmul(ps[sb][:, 1, c0:n], lhsT=kth[s][64:128, k0:k0 + 128],
                                                  rhs=qh[s][64:128, loc + c0:loc + n], start=True, stop=True),
                         [Bk[s], Bq[s]], [Bs[sb]])

                def emit_exp(i):
                    hd, ci, kt, nkt = steps[i]
                    g0, n = CH[ci]
                    sb, pb, k0 = i % 2, i % 12, 128 * kt
                    bias = ctxb if kt < 32 else 0.0
                    c0 = col0(i)
                    P.op("act", lambda e: e.activation(out=pp[pb][:, :, c0:n], in_=ps[sb][:, :, c0:n], func=AF.Exp,
                                                       scale=0.125, bias=bias), [Bs[sb]], Bpp[pb])
                    if k0 >= g0:
                        j = (k0 - g0) // 128
                        c1 = c0 + 128
                        P.op("dve", lambda e: e.tensor_tensor(out=pp[pb][:, 0, c0:c1], in0=pp[pb][:, 0, c0:c1],
                                                              in1=masks[:, j, c0:c1], op=ALU.mult), [Bpp[pb][0]], [Bpp[pb][0]])
                        P.op("dve", lambda e: e.tensor_tensor(out=pp[pb][:, 1, c0:c1], in0=pp[pb][:, 1, c0:c1],
                                                              in1=masks[:, j, c0:c1], op=ALU.mult), [Bpp[pb][1]], [Bpp[pb][1]])

                def emit_av(i):
                    hd, ci, kt, nkt = steps[i]
                    g0, n = CH[ci]
                    s, pb = hd % 2, i % 12
                    first, last = kt == 0, kt == nkt - 1
                    c0 = col0(i)
                    for m in range(2):
                        P.op("pe", lambda e, m=m: e.matmul(po[:, m, c0:n], lhsT=vh[s][:, kt, :], rhs=pp[pb][:, m, c0:n],
                                                           start=first, stop=last), [Bv[s], Bpp[pb][m]], [Bo[m]])
                    if kt % LG == LG - 1:
                        for jj in range(LG):
                            slot = (i - (LG - 1) + jj) % 12
                            cj = col0(i - (LG - 1) + jj)
                            P.op("pe", lambda e, jj=jj, slot=slot, cj=cj: e.matmul(
                                pl[32 * jj:32 * (jj + 1), 0, cj:n], lhsT=ones_bf[:, 0:32], rhs=pp[slot][:, 0, cj:n],
                                start=(kt == LG - 1), stop=last, tile_position=(0, 32 * jj)), [Bpp[slot][0]], [Bl[0]])
                    if kt == 0:
                        P.op("dve", lambda e: e.tensor_copy(out=accP[:, 0:n], in_=pp[pb][:, 1, 0:n]), [Bpp[pb][1]], [BaccP])
                    else:
                        P.op("dve", lambda e: e.tensor_tensor(out=accP[:, c0:n], in0=accP[:, c0:n], in1=pp[pb][:, 1, c0:n],
                                                              op=ALU.add), [Bpp[pb][1], BaccP], [BaccP])
                    if last:
                        P.op("pe", lambda e: e.matmul(pl[:, 1, 0:n], lhsT=ones1f[:], rhs=accP[:, 0:n], start=True, stop=True),
                             [BaccP], [Bl[1]])
                        P.op("dve", lambda e: e.tensor_copy(out=plc[:, 0:n], in_=pl[:, 0, 0:n]), [Bl[0]], [Bplc])
                        P.op("pe", lambda e: e.matmul(pl[:, 0, 0:n], lhsT=onesf[:], rhs=plc[:, 0:n], start=True, stop=True),
                             [Bplc], [Bl[0]])

                fin = [0]

                def finalize(i):
                    hd, ci, kt, nkt = steps[i]
                    g0, n = CH[ci]
                    loc = g0 - Q0
                    y = fin[0] % 2
                    fin[0] += 1
                    for m in range(2):
                        P.op("dve", lambda e, m=m: e.tensor_scalar(out=ls[y][:, m, 0:n], in0=pl[:, m, 0:n],
                                                                   scalar1=(4.0 if m == 0 else 1.0), scalar2=1e-30,
                                                                   op0=ALU.mult, op1=ALU.add), [Bl[m]], [Bls[y][m]])
                        P.op("act", lambda e, m=m: e.activation(out=os_[y][:, m, 0:n], in_=po[:, m, 0:n], func=AF.Copy),
                             [Bo[m]], [Bos[y][m]])
                    def part_norm(m):
                        P.op("dve", lambda e: e.reciprocal(out=ls[y][:, m, 0:n], in_=ls[y][:, m, 0:n]),
                             [Bls[y][m]], [Bls[y][m]])
                        P.op("pool", lambda e: e.tensor_tensor(out=os_[y][:, m, 0:n], in0=os_[y][:, m, 0:n],
                                                               in1=ls[y][:, m, 0:n], op=ALU.mult),
                             [Bos[y][m], Bls[y][m]], [Bos[y][m]])

                    def part_out():
                        P.op("dve", lambda e: e.scalar_tensor_tensor(out=hdt[y][:, 0:n], in0=os_[y][:, 1, 0:n],
                                                                     scalar=lamt[:, 1:2], in1=os_[y][:, 0, 0:n],
                                                                     op0=ALU.mult, op1=ALU.add), Bos[y], [Bhd[y]])
                        P.dma("sp", HD[hd, :, loc:loc + n], hdt[y][:, 0:n], Bhd[y], False)

                    deferred.setdefault(i + 3, []).append(lambda: part_norm(0))
                    deferred.setdefault(i + 7, []).append(lambda: part_norm(1))
                    deferred.setdefault(i + 11, []).append(part_out)

                deferred = {}
                load_head(0, 0)
                prefetch_E(P)
                emit_qk(0)
                emit_qk(1)
                for i in range(len(steps)):
                    hd, ci, kt, nkt = steps[i]
                    if ci == 0 and kt == 0 and hd + 1 < 8:
                        load_head(hd + 1, 1 - hd % 2)
                    emit_exp(i)
                    if i + 2 < len(steps):
                        emit_qk(i + 2)
                    for fn in deferred.pop(i, []):
                        fn()
                    emit_av(i)
                    if kt == nkt - 1:
                        finalize(i)
                for j in sorted(deferred):
                    for fn in deferred[j]:
                        fn()
                P.emit()

        if upto >= 5:
            with ExitStack() as st:
                T = lambda n, s, d: st.enter_context(nc.sbuf_tensor(n, s, d))
                PT = lambda n, s, d: st.enter_context(nc.psum_tensor(n, s, d))
                gb = T("gb2", [128, D], F32)
                ya = [T("ya%d" % i, [128, 8, 512], BF16) for i in range(2)]
                hq = [T("hq%d" % i, [128, 8, 512], F32) for i in range(2)]
                yb = [T("yb%d" % i, [128, 8, 512], BF16) for i in range(2)]
                sqt = [T("sqt%d" % i, [128, 512], F32) for i in range(2)]
                lnt = [T("lnt%d" % i, [128, 512], F32) for i in range(2)]
                sg = [T("sg%d" % i, [128, 16, 512], BF16) for i in range(1)]
                t1 = [T("t1_%d" % i, [128, 512], F32) for i in range(2)]
                t2 = [T("t2_%d" % i, [128, 512], F32) for i in range(2)]
                mg = T("mg", [128, 8, 512], BF16)
                xt = [T("xtE%d" % i, [128, D], F32) for i in range(2)]
                x1 = [T("x1_%d" % i, [128, D], F32) for i in range(2)]
                junk = T("junkE", [128, D], F32)
                st4 = [T("st4E%d" % i, [128, 4], F32) for i in range(2)]
                hn = [T("hnE%d" % i, [128, D], BF16) for i in range(2)]
                h2s = [T("h2s%d" % i, [128, 8, 512], BF16) for i in range(1)]
                ppa = [PT("ppa%d" % i, [128, 512], F32) for i in range(2)]
                ppb = [PT("ppb%d" % i, [128, 512], F32) for i in range(2)]
                pw = [PT("pw%d" % i, [128, 512], F32) for i in range(2)]
                pT = [PT("pTE%d" % i, [128, 8, 128], BF16) for i in range(1)]
                pms = PT("pms", [128, 512], F32)
                P = Pass(ctx, "pE")
                Bwpa, Bwpb, Bwo, Bgb, Bjunk = Buf(), Buf(), Buf(), Buf(), Buf()
                Bya, Bhq, Bsg = [Buf(), Buf()], [Buf(), Buf()], [Buf()]
                Byb_ = [[Buf() for _ in range(8)] for _ in range(2)]
                Bsqt, Blnt, Bpms = [Buf(), Buf()], [Buf(), Buf()], Buf()
                Bt1, Bt2 = [Buf(), Buf()], [Buf(), Buf()]
                Bmg = [Buf() for _ in range(8)]
                Bxt, Bx1, Bst4, Bhn = [Buf(), Buf()], [Buf(), Buf()], [Buf(), Buf()], [Buf(), Buf()]
                Bh2s = [[Buf() for _ in range(4)] for _ in range(1)]
                Bppa, Bppb, Bpw, BpT = [Buf(), Buf()], [Buf(), Buf()], [Buf(), Buf()], [Buf()]
                P.dma("sp", gb[:], g2.partition_broadcast(128), Bgb, True)

                def load_chunk(ci):
                    g0, n = CH[ci]
                    loc = g0 - Q0
                    s = ci % 2
                    P.dma("sp", ya[s][:, :, 0:n], YA[:, :, loc:loc + n].rearrange("c p t -> p c t"), Bya[s], True)
                    P.dma("sp", hq[s][:, :, 0:n], HD[:, :, loc:loc + n].rearrange("c p t -> p c t"), Bhq[s], True)

                def load_sg(ci):
                    g0, n = CH[ci]
                    loc = g0 - Q0
                    P.dma("sp", sg[0][:, :, 0:n], SG[:, :, loc:loc + n].rearrange("c p t -> p c t"), Bsg[0], True)

                def sub_ln(ci, heads):
                    g0, n = CH[ci]
                    s = ci % 2
                    for hd in heads:
                        q2 = hd % 2
                        P.op("pool", lambda e, hd=hd, q2=q2: e.tensor_tensor(out=sqt[q2][:, 0:n], in0=hq[s][:, hd, 0:n],
                                                                             in1=hq[s][:, hd, 0:n], op=ALU.mult),
                             [Bhq[s]], [Bsqt[q2]])
                        P.op("pe", lambda e, q2=q2: e.matmul(pms[:, 0:n], lhsT=onesf[:], rhs=sqt[q2][:, 0:n], start=True,
                                                             stop=True), [Bsqt[q2]], [Bpms])
                        P.op("act", lambda e, q2=q2: e.activation(out=lnt[q2][:, 0:n], in_=pms[:, 0:n], func=AF.Ln, bias=1e-5),
                             [Bpms], [Blnt[q2]])
                        P.op("act", lambda e, q2=q2: e.activation(out=lnt[q2][:, 0:n], in_=lnt[q2][:, 0:n], func=AF.Exp,
                                                                  scale=-0.5), [Blnt[q2]], [Blnt[q2]])
                        P.op("dve", lambda e, hd=hd, q2=q2: e.scalar_tensor_tensor(
                            out=yb[s][:, hd, 0:n], in0=hq[s][:, hd, 0:n], scalar=sublg[:, 0:1], in1=lnt[q2][:, 0:n],
                            op0=ALU.mult, op1=ALU.mult), [Bhq[s], Blnt[q2]], [Byb_[s][hd]])

                load_chunk(0)
                ti = 0
                for ci, (g0, n) in enumerate(CH):
                    s = ci % 2
                    loc = g0 - Q0
                    if ci + 1 < len(CH):
                        load_chunk(ci + 1)
                    load_sg(ci)
                    sub_ln(ci, range(8))
                    for c in range(8):
                        b = c % 2
                        mm8(P, ppa[b][:, 0:n], lambda k, c=c: wpa[:, k, 128 * c:128 * (c + 1)],
                            lambda k, s=s, n=n: ya[s][:, k, 0:n], [Bwpa, Bya[s]], Bppa[b])
                        mm8(P, ppb[b][:, 0:n], lambda k, c=c: wpb[:, k, 128 * c:128 * (c + 1)],
                            lambda k, s=s, n=n: yb[s][:, k, 0:n], [Bwpb] + Byb_[s], Bppb[b])
                        P.op("dve", lambda e, b=b, c=c, n=n: e.tensor_tensor(out=t1[b][:, 0:n], in0=ppa[b][:, 0:n],
                                                                            in1=sg[0][:, c, 0:n], op=ALU.mult),
                             [Bppa[b], Bsg[0]], [Bt1[b]])
                        P.op("dve", lambda e, b=b, c=c, n=n: e.tensor_tensor(out=t2[b][:, 0:n], in0=ppb[b][:, 0:n],
                                                                            in1=sg[0][:, 8 + c, 0:n], op=ALU.mult),
                             [Bppb[b], Bsg[0]], [Bt2[b]])
                        P.op("pool", lambda e, b=b, c=c, n=n: e.tensor_tensor(out=mg[:, c, 0:n], in0=t1[b][:, 0:n],
                                                                              in1=t2[b][:, 0:n], op=ALU.add),
                             [Bt1[b], Bt2[b]], [Bmg[c]])
                    S = 0
                    nj = n // 128
                    pend = None

                    def emit_T(u, j):
                        for c in range(8):
                            P.op("pe", lambda e, c=c: e.transpose(out=pT[0][:, c, :], in_=hn[u][:, 128 * c:128 * (c + 1)],
                                                                  identity=ident[:]), [Bhn[u]], [BpT[0]])
                        P.op("act", lambda e: e.activation(out=h2s[S][:, :, 128 * j:128 * (j + 1)], in_=pT[0][:], func=AF.Copy),
                             [BpT[0]], [Bh2s[S][j]])
                    for j in range(n // 128):
                        u = ti % 2
                        ti += 1
                        r0 = g0 + 128 * j
                        P.dma("sp", xt[u][:], xin[r0:r0 + 128, :], Bxt[u], True)
                        for half in range(2):
                            mm8(P, pw[half][:], lambda k, j=j: mg[:, k, 128 * j:128 * (j + 1)],
                                lambda k, half=half: wo[:, k, 512 * half:512 * (half + 1)], [Bwo] + Bmg, Bpw[half])
                            P.op("dve", lambda e, u=u, half=half: e.tensor_tensor(
                                out=x1[u][:, 512 * half:512 * (half + 1)], in0=pw[half][:],
                                in1=xt[u][:, 512 * half:512 * (half + 1)], op=ALU.add), [Bpw[half], Bxt[u]], [Bx1[u]])
                        P.dma("sp", X1[loc + 128 * j:loc + 128 * (j + 1), :], x1[u][:], Bx1[u], False)
                        rms_T(P, x1[u][:], Bx1[u], gb, Bgb, junk, Bjunk, st4[u], Bst4[u], hn[u], Bhn[u], None, None)
                        if pend is not None:
                            emit_T(*pend)
                        pend = (u, j)
                    emit_T(*pend)
                    o = P.dma("sp", H2T[:, :, loc:loc + n].rearrange("c p t -> p c t"), h2s[S][:, :, 0:n], Bh2s[S][0], False)
                    for qq in range(1, n // 128):
                        o.deps = set(o.deps) | set(Bh2s[S][qq].w)
                        Bh2s[S][qq].rs.append(o)
                P.emit()

        stWE.close()
        stWD = ExitStack()
        wd = WT(stWD, "wd", [128, 24, D], "right")

        def prefetch_F2(P):
            for q in range(3):
                load_w(P, wd[:, 8 * q:8 * (q + 1), :], "w_dn", GF2, 1024 * q, 0, 1024, 8, Buf())

        if upto >= 6:
            with ExitStack() as st:
                T = lambda n, s, d: st.enter_context(nc.sbuf_tensor(n, s, d))
                PT = lambda n, s, d: st.enter_context(nc.psum_tensor(n, s, d))
                wu = T("wu", [128, 8, 2 * DFF], BF16)
                h2t = [T("h2t%d" % i, [128, 8, 512], BF16) for i in range(2)]
                carry = T("carry", [128, 24, 2], F32)
                ugb = [T("ugb%d" % i, [128, 514], F32) for i in range(3)]
                cv = [T("cv%d" % i, [128, 512], F32) for i in range(2)]
                ge = [T("ge%d" % i, [128, 512], F32) for i in range(2)]
                ats = [T("ats%d" % i, [128, 24, 512], BF16) for i in range(1)]
                pg = [PT("pg%d" % i, [128, 512], F32) for i in range(3)]
                pv = [PT("pv%d" % i, [128, 512], F32) for i in range(3)]
                P = Pass(ctx, "pF1")
                Bwu = Buf()
                Bwuv = Buf()
                Bh2t = [Buf(), Buf()]
                Bcar = [Buf() for _ in range(24)]
                Bug = [Buf() for _ in range(3)]
                Bcv, Bge = [Buf(), Buf()], [Buf(), Buf()]
                Bats = [[Buf() for _ in range(24)] for _ in range(1)]
                Bpg, Bpv = [Buf() for _ in range(3)], [Buf() for _ in range(3)]
                load_w(P, wu[:, :, 0:DFF], "w_up", GF1, 0, 0, DFF, 8, Bwu)
                load_w(P, wu[:, :, DFF:2 * DFF], "w_up", GF1, 0, DFF, DFF, 8, Bwuv)
                g0, n = CH[0]
                P.dma("sp", h2t[0][:, :, 0:n], H2T[:, :, 0:n].rearrange("c p t -> p c t"), Bh2t[0], True)
                it = 0
                for ci, (g0, n) in enumerate(CH):
                    s = ci % 2
                    loc = g0 - Q0
                    if ci + 1 < len(CH):
                        g1_, n1 = CH[ci + 1]
                        l1 = g1_ - Q0
                        P.dma("sp", h2t[1 - s][:, :, 0:n1], H2T[:, :, l1:l1 + n1].rearrange("c p t -> p c t"), Bh2t[1 - s], True)
                    A = 0
                    if ci == 2:
                        prefetch_F2(P)
                    for fc in range(24):
                        b = it % 3
                        c2 = it % 2
                        it += 1
                        mm8(P, pg[b][:, 0:n], lambda k, fc=fc: wu[:, k, 128 * fc:128 * (fc + 1)],
                            lambda k, s=s, n=n: h2t[s][:, k, 0:n], [Bwu, Bh2t[s]], Bpg[b])
                        if ci == 0:
                            P.op("dve", lambda e, fc=fc, b=b: e.tensor_scalar(out=carry[:, fc, :], in0=pg[b][:, 126:128],
                                                                              scalar1=ctxf, scalar2=None, op0=ALU.mult),
                                 [Bpg[b]], [Bcar[fc]])
                            continue
                        mm8(P, pv[b][:, 0:n], lambda k, fc=fc: wu[:, k, DFF + 128 * fc:DFF + 128 * (fc + 1)],
                            lambda k, s=s, n=n: h2t[s][:, k, 0:n], [Bwuv, Bh2t[s]], Bpv[b])
                        P.op("pool", lambda e, fc=fc, b=b: e.tensor_copy(out=ugb[b][:, 0:2], in_=carry[:, fc, :]),
                             [Bcar[fc]], [Bug[b]])
                        P.op("act", lambda e, b=b: e.activation(out=ugb[b][:, 2:514], in_=pg[b][:], func=AF.Copy),
                             [Bpg[b]], [Bug[b]])
                        P.op("pool", lambda e, fc=fc, b=b: e.tensor_copy(out=carry[:, fc, :], in_=ugb[b][:, 512:514]),
                             [Bug[b]], [Bcar[fc]])
                        P.op("dve", lambda e, fc=fc, b=b, c2=c2: e.tensor_scalar(
                            out=cv[c2][:], in0=ugb[b][:, 0:512], scalar1=pff[:, 0, fc:fc + 1], scalar2=pff[:, 3, fc:fc + 1],
                            op0=ALU.mult, op1=ALU.add), [Bug[b]], [Bcv[c2]])
                        for k in range(1, 3):
                            P.op("dve", lambda e, fc=fc, b=b, c2=c2, k=k: e.scalar_tensor_tensor(
                                out=cv[c2][:], in0=ugb[b][:, k:k + 512], scalar=pff[:, k, fc:fc + 1], in1=cv[c2][:],
                                op0=ALU.mult, op1=ALU.add), [Bug[b], Bcv[c2]], [Bcv[c2]])
                        P.op("act", lambda e, c2=c2: e.activation(out=ge[c2][:], in_=cv[c2][:], func=GELU), [Bcv[c2]], [Bge[c2]])
                        P.op("dve", lambda e, fc=fc, b=b, c2=c2, A=A: e.tensor_tensor(out=ats[A][:, fc, :], in0=pv[b][:],
                                                                                     in1=ge[c2][:], op=ALU.mult),
                             [Bpv[b], Bge[c2]], [Bats[A][fc]])
                        if ci >= 1 and fc % 12 == 11:
                            t0 = g0 - NOWN
                            f0 = fc - 11
                            o = P.dma("sp", AT[f0:f0 + 12, :, t0:t0 + 512].rearrange("c p t -> p c t"), ats[A][:, f0:f0 + 12, :],
                                      Bats[A][f0], False)
                            for f2 in range(f0 + 1, f0 + 12):
                                o.deps = set(o.deps) | set(Bats[A][f2].w)
                                Bats[A][f2].rs.append(o)
                P.emit()

        if upto >= 7:
            with ExitStack() as st:
                T = lambda n, s, d: st.enter_context(nc.sbuf_tensor(n, s, d))
                PT = lambda n, s, d: st.enter_context(nc.psum_tensor(n, s, d))
                gb = T("gb3", [128, D], F32)
                att = [T("att%d" % i, [128, 24, 512], BF16) for i in range(2)]
                x1t = [T("x1t%d" % i, [128, D], F32) for i in range(2)]
                x2 = [T("x2_%d" % i, [128, D], F32) for i in range(2)]
                junk = T("junkF", [128, D], F32)
                st4 = [T("st4F%d" % i, [128, 4], F32) for i in range(2)]
                ot = [T("ot%d" % i, [128, D], F32) for i in range(2)]
                pd = [PT("pd%d" % i, [128, 512], F32) for i in range(4)]
                P = Pass(ctx, "pF2")
                Bwd, Bgb, Bjunk = Buf(), Buf(), Buf()
                Batt, Bx1t, Bx2, Bst4, Bot = [Buf(), Buf()], [Buf(), Buf()], [Buf(), Buf()], [Buf(), Buf()], [Buf(), Buf()]
                Bpd = [Buf() for _ in range(4)]
                P.dma("sp", gb[:], g3.partition_broadcast(128), Bgb, True)
                P.dma("sp", att[0][:], AT[:, :, 0:512].rearrange("c p t -> p c t"), Batt[0], True)
                ti = 0
                pi = 0
                for ci in range(8):
                    s = ci % 2
                    if ci + 1 < 8:
                        P.dma("sp", att[1 - s][:], AT[:, :, 512 * (ci + 1):512 * (ci + 2)].rearrange("c p t -> p c t"),
                              Batt[1 - s], True)
                    for j in range(4):
                        u = ti % 2
                        ti += 1
                        t0 = 512 * ci + 128 * j
                        P.dma("sp", x1t[u][:], X1[128 + t0:128 + t0 + 128, :], Bx1t[u], True)
                        for half in range(2):
                            b = pi % 4
                            pi += 1
                            mm8(P, pd[b][:], lambda k, s=s, j=j: att[s][:, k, 128 * j:128 * (j + 1)],
                                lambda k, half=half: wd[:, k, 512 * half:512 * (half + 1)], [Bwd, Batt[s]], Bpd[b], n=24)
                            P.op("dve", lambda e, u=u, half=half, b=b: e.tensor_tensor(
                                out=x2[u][:, 512 * half:512 * (half + 1)], in0=pd[b][:],
                                in1=x1t[u][:, 512 * half:512 * (half + 1)], op=ALU.add), [Bpd[b], Bx1t[u]], [Bx2[u]])
                        rms_T(P, x2[u][:], Bx2[u], gb, Bgb, junk, Bjunk, st4[u], Bst4[u], ot[u], Bot[u], None, None)
                        P.dma("sp", out[t0:t0 + 128, :], ot[u][:], Bot[u], False)
                P.emit()
    nc._mk_nops = ctx.nops
    return nc


def make_in_maps(inputs):
    f = lambda a: np.ascontiguousarray(np.asarray(a, dtype=np.float32))
    x = f(inputs["x"])
    B, S, _ = x.shape
    half = S // 2
    shared = {
        "w_in": f(inputs["w_in"][0]),
        "w_pa": f(inputs["w_proj_rnn"][0]),
        "w_pb": f(inputs["w_proj_attn"][0]),
        "w_o": f(inputs["w_out"][0]),
        "w_up": f(inputs["w_up"][0]),
        "w_dn": f(inputs["w_down"][0]),
        "rg_wa": f(inputs["rg_wa"][0]),
        "rg_wx": f(inputs["rg_wx"][0]),
        "g1": f(inputs["attn_norm_g"][0]),
        "g2": f(inputs["mlp_norm_g"][0]),
        "g3": f(inputs["final_norm_g"]),
        "lamv": f(np.stack([inputs["lam_q1"][0], inputs["lam_k1"][0], inputs["lam_q2"][0], inputs["lam_k2"][0]])),
        "sublg": f(np.asarray(inputs["subln_g"][0]).reshape(128, 1)),
    }
    cw = np.asarray(inputs["rnn_conv_w"][0], np.float32)
    rows = [cw[0], cw[1], cw[2], cw[3], inputs["rnn_conv_b"][0], inputs["rg_ba"][0], inputs["rg_bx"][0],
            inputs["rg_lambda"][0]]
    pr = np.stack([np.asarray(r, np.float32).reshape(8, 128) for r in rows])
    shared["par_rnn"] = f(pr.transpose(2, 0, 1))
    fw_ = np.asarray(inputs["ffn_conv_w"][0], np.float32)
    rows = [fw_[0], fw_[1], fw_[2], inputs["ffn_conv_b"][0]]
    pf = np.stack([np.asarray(r, np.float32).reshape(24, 128) for r in rows])
    shared["par_ffn"] = f(pf.transpose(2, 0, 1))
    in_maps = []
    for b in range(B):
        for h in range(2):
            m = dict(shared)
            xi = np.zeros((NT, D), np.float32)
            if h == 1:
                xi[:] = x[b]
            else:
                xi[half:] = x[b, :half]
            m["xin"] = xi
            fl = np.zeros((128, 2), np.float32)
            fl[:, 0] = 0.0 if h == 1 else -30000.0
            fl[:, 1] = 1.0 if h == 1 else 0.0
            m["flags"] = fl
            in_maps.append(m)
    return in_maps


_NC_CACHE = {}


def kernel(**inputs):
    in_maps = make_in_maps(inputs)
    if "nc" not in _NC_CACHE:
        _NC_CACHE["nc"] = build_nc()
    nc = _NC_CACHE["nc"]
    res = run_bass_kernel_spmd(nc, in_maps, core_ids=list(range(8)))
    x = np.asarray(inputs["x"])
    B, S, _ = x.shape
    outp = np.empty((B, S, D), np.float32)
    k = 0
    for b in range(B):
        for h in range(2):
            outp[b, h * NOWN:(h + 1) * NOWN] = res.results[k]["out"]
            k += 1
    return outp
```

```python
import math
import os
from contextlib import ExitStack

import numpy as np
import concourse.bass as bass
import concourse.mybir as mybir
from concourse.bass_utils import run_bass_kernel_spmd

F32 = mybir.dt.float32
BF16 = mybir.dt.bfloat16
AF = mybir.ActivationFunctionType
ALU = mybir.AluOpType

D = 1024
NT = 8192
NOWN = 4096
Q0 = 3968
NQ = NT - Q0
CH = [(Q0, 128)] + [(NOWN + 512 * i, 512) for i in range(8)]
EPS = 1e-6
LAMBDA_INIT = 0.8 - 0.6 * math.exp(-0.3 * 0)
DFF = 3072
GELU = AF.Gelu_apprx_tanh

ENGS = ("sp", "act", "dve", "pool", "pe")
N_DSEM = 44
N_BG = 6


class Buf:
    __slots__ = ("name", "w", "rs", "dsem")

    def __init__(self, name="b"):
        self.name = name
        self.w = []
        self.rs = []
        self.dsem = None


class Op:
    __slots__ = ("eng", "fn", "deps", "sig", "sem", "val", "dma")

    def __init__(self, eng, fn, dma=False):
        self.eng = eng
        self.fn = fn
        self.deps = ()
        self.sig = dma
        self.sem = None
        self.val = 0
        self.dma = dma


class Ctx:
    def __init__(self, nc, stack):
        self.nc = nc
        self.esem = {e: stack.enter_context(nc.semaphore("es_" + e)) for e in ENGS}
        self.ecount = {e: 0 for e in ENGS}
        self.dsems = [stack.enter_context(nc.semaphore("ds%d" % i)) for i in range(N_DSEM)]
        self.dcount = [0] * N_DSEM
        self.nops = 0
        self.bgsems = [stack.enter_context(nc.semaphore("bg%d" % i)) for i in range(N_BG)]
        self.bgcount = [0] * N_BG
        self.bglast = [None] * N_BG


class Pass:
    def __init__(self, ctx, name="p"):
        self.ctx = ctx
        self.name = name
        self.ops = {e: [] for e in ENGS}
        self.next_dsem = 0
        self.used_dsems = set()

    def _record(self, o, reads, writes):
        deps = set()
        for b in reads:
            deps.update(b.w)
        for b in writes:
            deps.update(b.w)
            deps.update(b.rs)
        if o.eng == "pe":
            deps = {d for d in deps if d.eng != "pe"}
        o.deps = deps
        for b in reads:
            b.rs.append(o)
        for b in writes:
            b.w = [o]
            b.rs = []
        self.ops[o.eng].append(o)
        return o

    def op(self, eng, fn, reads=(), writes=()):
        return self._record(Op(eng, fn), reads, writes)

    def dma_bg(self, queue, out, in_, gid, extra_deps=()):
        ctx = self.ctx
        ctx.bgcount[gid] += 16
        o = Op(queue, lambda e: e.dma_start(out=out, in_=in_), dma=True)
        o.sem = ctx.bgsems[gid]
        o.val = ctx.bgcount[gid]
        o.deps = set(extra_deps)
        ctx.bglast[gid] = o
        self.ops[queue].append(o)
        return o

    def dma(self, queue, out, in_, sbuf, load, extra_deps=(), **kw):
        ctx = self.ctx
        if sbuf.dsem is None:
            assert self.next_dsem < N_DSEM, "out of DMA semaphores"
            sbuf.dsem = self.next_dsem
            self.next_dsem += 1
        i = sbuf.dsem
        self.used_dsems.add(i)
        ctx.dcount[i] += 16
        o = Op(queue, lambda e: e.dma_start(out=out, in_=in_, **kw), dma=True)
        o.sem = ctx.dsems[i]
        o.val = ctx.dcount[i]
        if load:
            self._record(o, [], [sbuf])
        else:
            self._record(o, [sbuf], [])
        if extra_deps:
            o.deps = set(o.deps) | {d for d in extra_deps if d is not None}
        return o

    def emit(self):
        ctx = self.ctx
        nc = ctx.nc
        for e in ENGS:
            for o in self.ops[e]:
                for d in o.deps:
                    d.sig = True
        for e in ENGS:
            for o in self.ops[e]:
                if o.dma:
                    continue
                if o.sig:
                    ctx.ecount[e] += 1
                    o.sem = ctx.esem[e]
                    o.val = ctx.ecount[e]
        final_d = [(ctx.dsems[i], ctx.dcount[i]) for i in sorted(self.used_dsems)]
        ops = self.ops
        engmap = {"sp": "sync", "act": "scalar", "dve": "vector", "pool": "gpsimd", "pe": "tensor"}

        def run(ename):
            def body(e):
                waited = {}
                for o in ops[ename]:
                    for d in o.deps:
                        k = id(d.sem)
                        if waited.get(k, -1) >= d.val:
                            continue
                        e.wait_ge(d.sem, d.val)
                        waited[k] = d.val
                    ins = o.fn(e)
                    if o.sig:
                        ins.then_inc(o.sem, 16 if o.dma else 1)
                if ename == "sp":
                    for (s, v) in final_d:
                        if v > 0 and waited.get(id(s), -1) < v:
                            e.wait_ge(s, v)
            return body

        with nc.Block(no_gpsimd_drain=True) as block:
            for ename in ENGS:
                if ops[ename] or ename == "sp":
                    getattr(block, engmap[ename])(run(ename))
        ctx.nops += sum(len(v) for v in ops.values())


def build_nc(debug=False, upto=99):
    nc = bass.Bass("TRN2", target_bir_lowering=False)
    IN = lambda n, s: nc.dram_tensor(n, s, F32, kind="ExternalInput").ap()
    xin = IN("xin", [NT, D])
    w_in = IN("w_in", [D, 7168])
    w_pa = IN("w_pa", [D, D])
    w_pb = IN("w_pb", [D, D])
    w_o = IN("w_o", [D, D])
    w_up = IN("w_up", [D, 2 * DFF])
    w_dn = IN("w_dn", [DFF, D])
    rg_wa = IN("rg_wa", [8, 128, 128])
    rg_wx = IN("rg_wx", [8, 128, 128])
    g1 = IN("g1", [D])
    g2 = IN("g2", [D])
    g3 = IN("g3", [D])
    par_rnn = IN("par_rnn", [128, 8, 8])
    par_ffn = IN("par_ffn", [128, 4, 24])
    lamv = IN("lamv", [4, 64])
    sublg_in = IN("sublg", [128, 1])
    flags = IN("flags", [128, 2])
    out = nc.dram_tensor("out", [NOWN, D], F32, kind="ExternalOutput").ap()

    def SCR(n, s, dt):
        if debug:
            return nc.dram_tensor(n, s, dt, kind="ExternalOutput").ap()
        return nc.dram_tensor(n, s, dt).ap()
    HT = SCR("HT", [8, 128, NT], BF16)
    KT = SCR("KT", [8, 128, NT], BF16)
    VV = SCR("VV", [NT, D], BF16)
    QT = SCR("QT", [8, 128, NQ], BF16)
    YA = SCR("YA", [8, 128, NQ], BF16)
    SG = SCR("SG", [16, 128, NQ], BF16)
    HD = SCR("HD", [8, 128, NQ], F32)
    X1 = SCR("X1", [NQ, D], F32)
    H2T = SCR("H2T", [8, 128, NQ], BF16)
    AT = SCR("AT", [24, 128, NOWN], BF16)

    WB = {"w_in": nc.dram_tensor("wb_in", [D, 7168], BF16).ap(),
          "w_pa": nc.dram_tensor("wb_pa", [D, D], BF16).ap(),
          "w_pb": nc.dram_tensor("wb_pb", [D, D], BF16).ap(),
          "w_o": nc.dram_tensor("wb_o", [D, D], BF16).ap(),
          "w_up": nc.dram_tensor("wb_up", [D, 2 * DFF], BF16).ap(),
          "w_dn": nc.dram_tensor("wb_dn", [DFF, D], BF16).ap(),
          "rg_wa": nc.dram_tensor("wb_wa", [8, 128, 128], BF16).ap(),
          "rg_wx": nc.dram_tensor("wb_wx", [8, 128, 128], BF16).ap()}
    WF = {"w_in": w_in, "w_pa": w_pa, "w_pb": w_pb, "w_o": w_o, "w_up": w_up, "w_dn": w_dn}
    GA, GC, GB, GE, GF1, GF2 = range(6)

    with ExitStack() as gst:
        ctx = Ctx(nc, gst)
        GT = lambda n, s, d: gst.enter_context(nc.sbuf_tensor(n, s, d))
        ident = GT("ident", [128, 128], BF16)
        ones_bf = GT("ones_bf", [128, 512], BF16)
        onesf = GT("onesf", [128, 128], F32)
        ones1f = GT("ones1f", [128, 128], F32)
        masks = GT("masks", [128, 4, 512], BF16)
        flg = GT("flg", [128, 2], F32)
        lamt = GT("lamt", [128, 4], F32)
        sublg = GT("sublg_t", [128, 1], F32)
        prn = GT("prn", [128, 8, 8], F32)
        c12 = GT("c12", [128, 2, 8], F32)
        pff = GT("pff", [128, 4, 24], F32)
        ctxb = flg[:, 0:1]
        ctxf = flg[:, 1:2]

        def cast_bg(P, name, gid, r0, r1, c0, c1, extra_deps=()):
            for r in range(r0, r1, 128):
                P.dma_bg("pool", WB[name][r:r + 128, c0:c1], WF[name][r:r + 128, c0:c1], gid, extra_deps=extra_deps)

        def load_w(P, dst, name, gid, row0, col0, ncols, kc_n, buf):
            v = WB[name][row0:row0 + 128 * kc_n, :].rearrange("(kc p) c -> p kc c", p=128)
            for kc in range(0, kc_n, 2):
                P.dma("sp", dst[:, kc:kc + 2, :], v[:, kc:kc + 2, col0:col0 + ncols], buf, True,
                      extra_deps=[ctx.bglast[gid]])

        stWA = ExitStack()
        wk = stWA.enter_context(nc.sbuf_tensor("wk", [128, 8, 1024], BF16, side="right"))
        wv = stWA.enter_context(nc.sbuf_tensor("wv", [128, 8, 1024], BF16, side="right"))
        with ExitStack() as st:
            T = lambda n, s, d: st.enter_context(nc.sbuf_tensor(n, s, d))
            lv = T("lv", [128, 4, 64], F32)
            junk = T("junk_s", [128, 64], F32)
            dots = T("dots", [128, 2], F32)
            tmp8 = T("tmp8", [128, 8], F32)
            P = Pass(ctx, "setup")
            B = {k: Buf(k) for k in ["ones", "ident", "onesf", "masks", "flg", "lv", "junk", "dots", "lamt", "sublg", "prn", "c12", "pff", "tmp8"]}
            wv_ = w_in.rearrange("(kc p) c -> p kc c", p=128)
            for kc in range(8):
                P.dma_bg("pool", wk[:, kc, :], wv_[:, kc, 3072:4096], GA)
            for kc in range(8):
                P.dma_bg("pool", wv[:, kc, :], wv_[:, kc, 4096:5120], GA)
            P.op("pool", lambda e: e.memset(ones_bf[:], 1.0), [], [B["ones"]])
            P.op("pool", lambda e: e.memset(onesf[:], 1.0 / 128.0), [], [B["onesf"]])
            P.op("pool", lambda e: e.memset(ones1f[:], 1.0), [], [Buf()])
            P.op("pool", lambda e: e.affine_select(out=ident[:], in_=ones_bf[:, 0:128], pattern=[[-1, 128]],
                                                   compare_op=ALU.is_equal, fill=0.0, base=0, channel_multiplier=1),
                 [B["ones"]], [B["ident"]])
            for j in range(4):
                P.op("pool", lambda e, j=j: e.affine_select(out=masks[:, j, :], in_=ones_bf[:], pattern=[[1, 512]],
                                                            compare_op=ALU.is_ge, fill=0.0, base=-128 * j,
                                                            channel_multiplier=-1),
                     [B["ones"]], [B["masks"]])
            P.dma("sp", flg[:], flags, B["flg"], True)
            P.dma("sp", prn[:], par_rnn, B["prn"], True)
            P.dma("sp", pff[:], par_ffn, B["pff"], True)
            P.dma("sp", sublg[:], sublg_in, B["sublg"], True)
            for i in range(4):
                P.dma("sp", lv[:, i, :], lamv[i, :].partition_broadcast(128), B["lv"], True)
            for i in range(2):
                P.op("dve", lambda e, i=i: e.scalar_tensor_tensor(out=junk[:], in0=lv[:, 2 * i, :], scalar=1.0,
                                                                  in1=lv[:, 2 * i + 1, :], op0=ALU.mult, op1=ALU.mult,
                                                                  accum_out=dots[:, i:i + 1]),
                     [B["lv"]], [B["junk"], B["dots"]])
            P.op("act", lambda e: e.activation(out=dots[:], in_=dots[:], func=AF.Exp), [B["dots"]], [B["dots"]])
            P.op("dve", lambda e: e.scalar_tensor_tensor(out=lamt[:, 0:1], in0=dots[:, 0:1], scalar=LAMBDA_INIT,
                                                         in1=dots[:, 1:2], op0=ALU.add, op1=ALU.subtract),
                 [B["dots"]], [B["lamt"]])
            P.op("dve", lambda e: e.tensor_scalar(out=lamt[:, 1:2], in0=lamt[:, 0:1], scalar1=-1.0, scalar2=None,
                                                  op0=ALU.mult), [B["lamt"]], [B["lamt"]])
            P.op("dve", lambda e: e.tensor_scalar(out=sublg[:], in0=sublg[:], scalar1=(1.0 - LAMBDA_INIT), scalar2=None,
                                                  op0=ALU.mult), [B["sublg"]], [B["sublg"]])
            P.op("act", lambda e: e.activation(out=tmp8[:], in_=prn[:, 7, :], func=AF.Exp, scale=-1.0),
                 [B["prn"]], [B["tmp8"]])
            P.op("act", lambda e: e.activation(out=tmp8[:], in_=tmp8[:], func=AF.Ln, bias=1.0),
                 [B["tmp8"]], [B["tmp8"]])
            P.op("dve", lambda e: e.tensor_scalar(out=c12[:, 0, :], in0=tmp8[:], scalar1=-8.0, scalar2=None, op0=ALU.mult),
                 [B["tmp8"]], [B["c12"]])
            P.op("dve", lambda e: e.tensor_scalar(out=c12[:, 1, :], in0=tmp8[:], scalar1=-16.0, scalar2=None, op0=ALU.mult),
                 [B["tmp8"]], [B["c12"]])
            P.emit()

        def rms_T(P, xt, Bx, gb, Bg, junk, Bjunk, st4, Bst4, hn, Bhn, pT, BpT):
            P.op("dve", lambda e: e.scalar_tensor_tensor(out=junk[:], in0=xt, scalar=1.0, in1=xt, op0=ALU.mult,
                                                         op1=ALU.mult, accum_out=st4[:, 0:1]),
                 [Bx], [Bjunk, Bst4])
            P.op("dve", lambda e: e.tensor_scalar(out=st4[:, 1:2], in0=st4[:, 0:1], scalar1=1.0 / D, scalar2=EPS,
                                                  op0=ALU.mult, op1=ALU.add), [Bst4], [Bst4])
            P.op("act", lambda e: e.activation(out=st4[:, 2:3], in_=st4[:, 1:2], func=AF.Sqrt), [Bst4], [Bst4])
            P.op("dve", lambda e: e.reciprocal(out=st4[:, 3:4], in_=st4[:, 2:3]), [Bst4], [Bst4])
            P.op("dve", lambda e: e.scalar_tensor_tensor(out=hn[:], in0=xt, scalar=st4[:, 3:4], in1=gb[:],
                                                         op0=ALU.mult, op1=ALU.mult), [Bx, Bst4, Bg], [Bhn])
            if pT is not None:
                for c in range(8):
                    P.op("pe", lambda e, c=c: e.transpose(out=pT[:, c, :], in_=hn[:, 128 * c:128 * (c + 1)],
                                                          identity=ident[:]), [Bhn], [BpT])

        def WT(stk, name, shape, side):
            return stk.enter_context(nc.sbuf_tensor(name, shape, BF16, side=side))


        def prefetch_A(P):
            load_w(P, wk, "w_in", GA, 0, 2048 + 1024, 1024, 8, Buf())
            load_w(P, wv, "w_in", GA, 0, 2048 + 2048, 1024, 8, Buf())

        def mm8(P, ps, lhs_fn, rhs_fn, reads, Bps, n=8):
            for k in range(n):
                P.op("pe", lambda e, k=k: e.matmul(ps, lhsT=lhs_fn(k), rhs=rhs_fn(k), start=(k == 0), stop=(k == n - 1)),
                     reads, [Bps])

        evac_rr = [0]

        def evac(P, out_ap, in_ap, reads, writes):
            evac_rr[0] += 1
            if evac_rr[0] % 2 == 0:
                P.op("act", lambda e: e.activation(out=out_ap, in_=in_ap, func=AF.Copy), reads, writes)
            else:
                P.op("dve", lambda e: e.tensor_copy(out=out_ap, in_=in_ap), reads, writes)

        stWC = ExitStack()
        wq = WT(stWC, "wq", [128, 8, 1024], "left")
        wg = WT(stWC, "wg", [128, 8, 2048], "left")

        def prefetch_C(P):
            load_w(P, wq, "w_in", GC, 0, 2048, 1024, 8, Buf())
            load_w(P, wg, "w_in", GC, 0, 5120, 2048, 8, Buf())

        if upto >= 1:
            with ExitStack() as st:
                T = lambda n, s, d: st.enter_context(nc.sbuf_tensor(n, s, d))
                PT = lambda n, s, d: st.enter_context(nc.psum_tensor(n, s, d))
                gb = T("gb", [128, D], F32)
                xt = [T("xt%d" % i, [128, D], F32) for i in range(3)]
                junk = T("junk", [128, D], F32)
                st4 = [T("st4_%d" % i, [128, 4], F32) for i in range(2)]
                hn = [T("hn%d" % i, [128, D], BF16) for i in range(2)]
                pT = [PT("pT%d" % i, [128, 8, 128], BF16) for i in range(2)]
                hts = [T("hts%d" % i, [128, 8, 512], BF16) for i in range(2)]
                kts = [T("kts%d" % i, [128, 8, 512], BF16) for i in range(2)]
                vs = [T("vs%d" % i, [128, 1024], BF16) for i in range(2)]
                ps = [PT("psA%d" % i, [128, 512], F32) for i in range(4)]
                P = Pass(ctx, "p0A")
                Bgb, Bjunk, Bwk, Bwv = Buf(), Buf(), Buf(), Buf()
                Bxt = [Buf() for _ in range(3)]
                Bst4 = [Buf() for _ in range(2)]
                Bhn = [Buf() for _ in range(2)]
                BpT = [Buf() for _ in range(2)]
                Bhts = [[Buf() for _ in range(4)] for _ in range(2)]
                Bkts = [Buf(), Buf()]
                Bvs = [Buf(), Buf()]
                Bps = [Buf() for _ in range(4)]
                P.dma("sp", gb[:], g1.partition_broadcast(128), Bgb, True)
                NTILE = NT // 128
                P.dma("sp", xt[0][:], xin[0:128, :], Bxt[0], True)
                P.dma("sp", xt[1][:], xin[128:256, :], Bxt[1], True)
                Bwk.w = [ctx.bglast[GA]]
                Bwv.w = [ctx.bglast[GA]]
                cast_bg(P, "w_in", GC, 0, D, 2048, 3072)
                cast_bg(P, "w_in", GC, 0, D, 5120, 7168)
                cast_bg(P, "w_in", GB, 0, D, 0, 2048)
                P.dma_bg("pool", WB["rg_wa"], rg_wa, GB)
                P.dma_bg("pool", WB["rg_wx"], rg_wx, GB)
                cast_bg(P, "w_pa", GE, 0, D, 0, D)
                cast_bg(P, "w_pb", GE, 0, D, 0, D)
                cast_bg(P, "w_o", GE, 0, D, 0, D)

                def norm_a(i):
                    if i >= NTILE:
                        return
                    s3, s2 = i % 3, i % 2
                    if i + 2 < NTILE:
                        P.dma("sp", xt[(i + 2) % 3][:], xin[128 * (i + 2):128 * (i + 3), :], Bxt[(i + 2) % 3], True)
                    rms_T(P, xt[s3][:], Bxt[s3], gb, Bgb, junk, Bjunk, st4[s2], Bst4[s2], hn[s2], Bhn[s2], None, None)

                def norm_sub(t, q):
                    S = t % 2
                    i = 4 * t + q
                    s2 = i % 2
                    for c in range(8):
                        P.op("pe", lambda e, c=c: e.transpose(out=pT[s2][:, c, :], in_=hn[s2][:, 128 * c:128 * (c + 1)],
                                                              identity=ident[:]), [Bhn[s2]], [BpT[s2]])
                    P.op("act", lambda e: e.activation(out=hts[S][:, :, 128 * q:128 * (q + 1)], in_=pT[s2][:], func=AF.Copy),
                         [BpT[s2]], [Bhts[S][q]])
                    norm_a(i + 2)
                    if q == 3:
                        t0 = 512 * t
                        o = P.dma("sp", HT[:, :, t0:t0 + 512].rearrange("c p t -> p c t"), hts[S][:], Bhts[S][0], False)
                        for qq in range(1, 4):
                            o.deps = set(o.deps) | set(Bhts[S][qq].w)
                            Bhts[S][qq].rs.append(o)

                cnt = {"pi": 0, "vi": 0}

                def kv_part(i, p):
                    s = i % 2
                    for hd in (2 * p, 2 * p + 1):
                        b = cnt["pi"] % 4
                        cnt["pi"] += 1
                        mm8(P, ps[b][:], lambda k, hd=hd: wk[:, k, 128 * hd:128 * (hd + 1)], lambda k: hts[s][:, k, :],
                            [Bwk] + Bhts[s], Bps[b])
                        evac(P, kts[s][:, hd, :], ps[b][:], [Bps[b]], [Bkts[s]])
                    if p == 3:
                        P.dma("sp", KT[:, :, 512 * i:512 * (i + 1)].rearrange("h p t -> p h t"), kts[s][:], Bkts[s], False)
                    j = p
                    v = cnt["vi"] % 2
                    cnt["vi"] += 1
                    for half in range(2):
                        b = cnt["pi"] % 4
                        cnt["pi"] += 1
                        mm8(P, ps[b][:], lambda k: hts[s][:, k, 128 * j:128 * (j + 1)],
                            lambda k, half=half: wv[:, k, 512 * half:512 * (half + 1)], [Bwv] + Bhts[s], Bps[b])
                        evac(P, vs[v][:, 512 * half:512 * (half + 1)], ps[b][:], [Bps[b]], [Bvs[v]])
                    r0 = 512 * i + 128 * j
                    P.dma("sp", VV[r0:r0 + 128, :], vs[v][:], Bvs[v], False)

                norm_a(0)
                norm_a(1)
                for q in range(4):
                    norm_sub(0, q)
                for t in range(16):
                    if t == 2:
                        prefetch_C(P)
                    for p in range(4):
                        if t + 1 < 16:
                            norm_sub(t + 1, p)
                        kv_part(t, p)
                P.emit()

        stWA.close()
        stWB = ExitStack()
        wxr = WT(stWB, "wxr", [128, 8, 1024], "right")
        wgr = WT(stWB, "wgr", [128, 8, 1024], "right")
        wa = WT(stWB, "wa", [128, 8, 128], "right")
        wx = WT(stWB, "wx", [128, 8, 128], "right")

        def prefetch_B(P):
            load_w(P, wxr, "w_in", GB, 0, 0, 1024, 8, Buf())
            load_w(P, wgr, "w_in", GB, 0, 1024, 1024, 8, Buf())
            P.dma("sp", wa[:], WB["rg_wa"].rearrange("n i j -> i n j"), Buf(), True, extra_deps=[ctx.bglast[GB]])
            P.dma("sp", wx[:], WB["rg_wx"].rearrange("n i j -> i n j"), Buf(), True, extra_deps=[ctx.bglast[GB]])

        if upto >= 3:
            with ExitStack() as st:
                T = lambda n, s, d: st.enter_context(nc.sbuf_tensor(n, s, d))
                PT = lambda n, s, d: st.enter_context(nc.psum_tensor(n, s, d))
                htt = [T("httC%d" % i, [128, 8, 512], BF16) for i in range(2)]
                qs = [T("qs%d" % i, [128, 8, 512], BF16) for i in range(2)]
                sgs = [T("sgs%d" % i, [128, 16, 512], BF16) for i in range(2)]
                ps = [PT("psC%d" % i, [128, 512], F32) for i in range(4)]
                P = Pass(ctx, "pC")
                Bwq, Bwg = Buf(), Buf()
                Bhtt = [Buf(), Buf()]
                Bqs = [Buf(), Buf()]
                Bsgs = [Buf(), Buf()]
                Bps = [Buf() for _ in range(4)]
                cast_bg(P, "w_up", GF1, 0, D, 0, 2 * DFF)
                g0, n = CH[0]
                P.dma("sp", htt[0][:, :, 0:n], HT[:, :, g0:g0 + n].rearrange("c p t -> p c t"), Bhtt[0], True)
                pi = 0
                for ci, (g0, n) in enumerate(CH):
                    s = ci % 2
                    loc = g0 - Q0
                    if ci + 1 < len(CH):
                        g1_, n1 = CH[ci + 1]
                        P.dma("sp", htt[1 - s][:, :, 0:n1], HT[:, :, g1_:g1_ + n1].rearrange("c p t -> p c t"), Bhtt[1 - s], True)
                    if ci == 2:
                        prefetch_B(P)
                    for hd in range(8):
                        b = pi % 4
                        pi += 1
                        mm8(P, ps[b][:, 0:n], lambda k, hd=hd: wq[:, k, 128 * hd:128 * (hd + 1)],
                            lambda k, s=s, n=n: htt[s][:, k, 0:n], [Bwq, Bhtt[s]], Bps[b])
                        evac(P, qs[s][:, hd, 0:n], ps[b][:, 0:n], [Bps[b]], [Bqs[s]])
                    P.dma("sp", QT[:, :, loc:loc + n].rearrange("h p t -> p h t"), qs[s][:, :, 0:n], Bqs[s], False)
                    for c in range(16):
                        b = pi % 4
                        pi += 1
                        mm8(P, ps[b][:, 0:n], lambda k, c=c: wg[:, k, 128 * c:128 * (c + 1)],
                            lambda k, s=s, n=n: htt[s][:, k, 0:n], [Bwg, Bhtt[s]], Bps[b])
                        P.op("act", lambda e, c=c, b=b, s=s, n=n: e.activation(out=sgs[s][:, c, 0:n], in_=ps[b][:, 0:n],
                                                                                func=AF.Sigmoid), [Bps[b]], [Bsgs[s]])
                    P.dma("sp", SG[:, :, loc:loc + n].rearrange("h p t -> p h t"), sgs[s][:, :, 0:n], Bsgs[s], False)
                P.emit()

        stWC.close()
        if upto >= 2:
            with ExitStack() as st:
                T = lambda n, s, d: st.enter_context(nc.sbuf_tensor(n, s, d))
                PT = lambda n, s, d: st.enter_context(nc.psum_tensor(n, s, d))
                htt = [T("httB%d" % i, [128, 8, 512], BF16) for i in range(2)]
                xrb = T("xrb", [128, 8, 515], F32)
                xc = T("xc", [128, 8, 512], F32)
                xcb = T("xcb", [128, 8, 512], BF16)
                ra = T("ra", [128, 8, 512], F32)
                a2 = T("a2", [128, 8, 512], F32)
                ib = T("ib", [128, 8, 512], F32)
                hh = a2
                gg = T("gg", [128, 8, 512], BF16)
                ys = [T("ys%d" % i, [128, 8, 512], BF16) for i in range(1)]
                hst = T("hst", [128, 8], F32)
                ps = [PT("psB%d" % i, [128, 512], F32) for i in range(8)]
                P = Pass(ctx, "pB")
                Bwxr, Bwgr, Bwa, Bwx, Bhst = Buf(), Buf(), Buf(), Buf(), Buf()
                Bhtt = [Buf(), Buf()]
                Bxrb = [Buf() for _ in range(8)]
                Bxc = [Buf() for _ in range(8)]
                Bxcb = [Buf() for _ in range(8)]
                Bra = [Buf() for _ in range(8)]
                Ba2 = [Buf() for _ in range(8)]
                Bib = [Buf() for _ in range(8)]
                Bhh = Ba2
                Bgg = [Buf() for _ in range(8)]
                Bys = [[Buf() for _ in range(8)] for _ in range(1)]
                Bps = [Buf() for _ in range(8)]
                Bhs = [Buf() for _ in range(8)]
                cast_bg(P, "w_dn", GF2, 0, DFF, 0, D)
                P.op("pool", lambda e: e.memset(xrb[:, :, 0:3], 0.0), [], Bxrb)
                P.op("pool", lambda e: e.memset(hst[:], 0.0), [], Bhs)
                P.dma("sp", htt[0][:], HT[:, :, 0:512].rearrange("c p t -> p c t"), Bhtt[0], True)
                pi = 0
                yi = 0
                for i in range(16):
                    s = i % 2
                    own = i >= 7
                    if i + 1 < 16:
                        P.dma("sp", htt[1 - s][:], HT[:, :, 512 * (i + 1):512 * (i + 2)].rearrange("c p t -> p c t"),
                              Bhtt[1 - s], True)
                    for ct in range(8):
                        b = pi % 8
                        pi += 1
                        mm8(P, ps[b][:], lambda k, ct=ct: wxr[:, k, 128 * ct:128 * (ct + 1)], lambda k, s=s: htt[s][:, k, :],
                            [Bwxr, Bhtt[s]], Bps[b])
                        P.op("act", lambda e, ct=ct, b=b: e.activation(out=xrb[:, ct, 3:515], in_=ps[b][:], func=AF.Copy),
                             [Bps[b]], [Bxrb[ct]])
                    for ct in range(8):
                        P.op("dve", lambda e, ct=ct: e.tensor_scalar(out=xc[:, ct, :], in0=xrb[:, ct, 0:512],
                                                                     scalar1=prn[:, 0, ct:ct + 1], scalar2=prn[:, 4, ct:ct + 1],
                                                                     op0=ALU.mult, op1=ALU.add), [Bxrb[ct]], [Bxc[ct]])
                    for k in range(1, 4):
                        for ct in range(8):
                            P.op("dve", lambda e, ct=ct, k=k: e.scalar_tensor_tensor(
                                out=xc[:, ct, :], in0=xrb[:, ct, k:k + 512], scalar=prn[:, k, ct:ct + 1], in1=xc[:, ct, :],
                                op0=ALU.mult, op1=ALU.add), [Bxrb[ct], Bxc[ct]], [Bxc[ct]])
                    for ct in range(8):
                        P.op("pool", lambda e, ct=ct: e.tensor_copy(out=xrb[:, ct, 0:3], in_=xrb[:, ct, 512:515]),
                             [Bxrb[ct]], [Bxrb[ct]])
                        P.op("act", lambda e, ct=ct: e.activation(out=xcb[:, ct, :], in_=xc[:, ct, :], func=AF.Copy),
                             [Bxc[ct]], [Bxcb[ct]])
                    gps = []
                    for ct in range(8):
                        b1 = pi % 8
                        pi += 1
                        P.op("pe", lambda e, ct=ct, b1=b1: e.matmul(ps[b1][:], lhsT=wa[:, ct, :], rhs=xcb[:, ct, :],
                                                                     start=True, stop=True), [Bwa, Bxcb[ct]], [Bps[b1]])
                        P.op("act", lambda e, ct=ct, b1=b1: e.activation(out=ra[:, ct, :], in_=ps[b1][:], func=AF.Sigmoid,
                                                                          bias=prn[:, 5, ct:ct + 1]), [Bps[b1]], [Bra[ct]])
                    for ct in range(8):
                        b2 = pi % 8
                        pi += 1
                        P.op("pe", lambda e, ct=ct, b2=b2: e.matmul(ps[b2][:], lhsT=wx[:, ct, :], rhs=xcb[:, ct, :],
                                                                     start=True, stop=True), [Bwx, Bxcb[ct]], [Bps[b2]])
                        P.op("act", lambda e, ct=ct, b2=b2: e.activation(out=ib[:, ct, :], in_=ps[b2][:], func=AF.Sigmoid,
                                                                          bias=prn[:, 6, ct:ct + 1]), [Bps[b2]], [Bib[ct]])
                    for ct in range(8):
                        P.op("act", lambda e, ct=ct: e.activation(out=a2[:, ct, :], in_=ra[:, ct, :], func=AF.Exp,
                                                                  scale=c12[:, 1, ct:ct + 1]), [Bra[ct]], [Ba2[ct]])
                    for ct in range(8):
                        P.op("act", lambda e, ct=ct: e.activation(out=ra[:, ct, :], in_=ra[:, ct, :], func=AF.Exp,
                                                                  scale=c12[:, 0, ct:ct + 1]), [Bra[ct], Ba2[ct]], [Bra[ct]])
                    for ct in range(8):
                        P.op("dve", lambda e, ct=ct: e.tensor_scalar(out=a2[:, ct, :], in0=a2[:, ct, :], scalar1=1.0,
                                                                     scalar2=-1.0, op0=ALU.min, op1=ALU.mult),
                             [Ba2[ct]], [Ba2[ct]])
                        P.op("pool", lambda e, ct=ct: e.tensor_tensor(out=ib[:, ct, :], in0=ib[:, ct, :], in1=xc[:, ct, :],
                                                                      op=ALU.mult), [Bib[ct], Bxc[ct]], [Bib[ct]])
                    for ct in range(8):
                        P.op("act", lambda e, ct=ct: e.activation(out=a2[:, ct, :], in_=a2[:, ct, :], func=AF.Sqrt, bias=1.0),
                             [Ba2[ct]], [Ba2[ct]])
                    for ct in range(8):
                        P.op("dve", lambda e, ct=ct: e.tensor_tensor(out=ib[:, ct, :], in0=ib[:, ct, :], in1=a2[:, ct, :],
                                                                     op=ALU.mult), [Bib[ct], Ba2[ct]], [Bib[ct]])
                    for ct in range(8):
                        P.op("dve", lambda e, ct=ct: e.tensor_tensor_scan(out=hh[:, ct, :], data0=ra[:, ct, :],
                                                                          data1=ib[:, ct, :], initial=hst[:, ct:ct + 1],
                                                                          op0=ALU.mult, op1=ALU.add),
                             [Bra[ct], Bib[ct], Bhs[ct]], [Bhh[ct]])
                        if i == 7:
                            P.op("pool", lambda e, ct=ct: e.tensor_scalar(out=hst[:, ct:ct + 1], in0=hh[:, ct, 511:512],
                                                                          scalar1=ctxf, scalar2=None, op0=ALU.mult),
                                 [Bhh[ct]], [Bhs[ct]])
                        else:
                            P.op("pool", lambda e, ct=ct: e.tensor_copy(out=hst[:, ct:ct + 1], in_=hh[:, ct, 511:512]),
                                 [Bhh[ct]], [Bhs[ct]])
                    if own:
                        y = 0
                        for ct in range(8):
                            b = pi % 8
                            pi += 1
                            mm8(P, ps[b][:], lambda k, ct=ct: wgr[:, k, 128 * ct:128 * (ct + 1)],
                                lambda k, s=s: htt[s][:, k, :], [Bwgr, Bhtt[s]], Bps[b])
                            P.op("act", lambda e, ct=ct, b=b: e.activation(out=gg[:, ct, :], in_=ps[b][:], func=GELU),
                                 [Bps[b]], [Bgg[ct]])
                        for ct in range(8):
                            P.op("pool", lambda e, ct=ct, y=y: e.tensor_tensor(out=ys[y][:, ct, :], in0=hh[:, ct, :],
                                                                               in1=gg[:, ct, :], op=ALU.mult),
                                 [Bhh[ct], Bgg[ct]], [Bys[y][ct]])
                        if i == 7:
                            o = P.dma("sp", YA[:, :, 0:128].rearrange("c p t -> p c t"), ys[y][:, :, 384:512], Bys[y][0], False)
                        else:
                            l0 = 128 + 512 * (i - 8)
                            o = P.dma("sp", YA[:, :, l0:l0 + 512].rearrange("c p t -> p c t"), ys[y][:], Bys[y][0], False)
                        for ct in range(1, 8):
                            o.deps = set(o.deps) | set(Bys[y][ct].w)
                            Bys[y][ct].rs.append(o)
                P.emit()

        stWB.close()
        stWE = ExitStack()
        wpa = WT(stWE, "wpa", [128, 8, 1024], "right")
        wpb = WT(stWE, "wpb", [128, 8, 1024], "right")
        wo = WT(stWE, "wo", [128, 8, 1024], "right")

        def prefetch_E(P):
            load_w(P, wpa, "w_pa", GE, 0, 0, 1024, 8, Buf())
            load_w(P, wpb, "w_pb", GE, 0, 0, 1024, 8, Buf())
            load_w(P, wo, "w_o", GE, 0, 0, 1024, 8, Buf())

        if upto >= 4:
            with ExitStack() as st:
                T = lambda n, s, d: st.enter_context(nc.sbuf_tensor(n, s, d))
                PT = lambda n, s, d: st.enter_context(nc.psum_tensor(n, s, d))
                kth = [T("kth%d" % i, [128, NT], BF16) for i in range(2)]
                vh = [T("vh%d" % i, [128, 64, 128], BF16) for i in range(2)]
                qh = [T("qh%d" % i, [128, NQ], BF16) for i in range(2)]
                pp = [T("pp%d" % i, [128, 2, 512], BF16) for i in range(16)]
                plc = T("plc", [128, 512], F32)
                Bplc = Buf()
                ls = [T("ls%d" % i, [128, 2, 512], F32) for i in range(2)]
                os_ = [T("os%d" % i, [128, 2, 512], F32) for i in range(2)]
                hdt = [T("hdt%d" % i, [128, 512], F32) for i in range(2)]
                accD = T("accD", [128, 512], F32)
                accP = T("accP", [128, 512], F32)
                BaccD, BaccP = Buf(), Buf()
                ps = [PT("psS%d" % i, [128, 2, 512], F32) for i in range(2)]
                po = PT("po", [128, 2, 512], F32)
                pl = PT("pl", [128, 2, 512], F32)
                P = Pass(ctx, "pD")
                Bk = [Buf(), Buf()]
                Bv = [Buf(), Buf()]
                Bq = [Buf(), Buf()]
                Bpp = [[Buf(), Buf()] for _ in range(16)]
                Bs = [Buf(), Buf()]
                Bo = [Buf(), Buf()]
                Bl = [Buf(), Buf()]
                Bls = [[Buf(), Buf()] for _ in range(2)]
                Bos = [[Buf(), Buf()] for _ in range(2)]
                Bhd = [Buf(), Buf()]

                def load_head(hd, s):
                    P.dma("sp", qh[s][:], QT[hd, :, :], Bq[s], True)
                    for q4 in range(4):
                        P.dma("sp", kth[s][:, 2048 * q4:2048 * (q4 + 1)], KT[hd, :, 2048 * q4:2048 * (q4 + 1)], Bk[s], True)
                    vsrc = VV[:, 128 * hd:128 * (hd + 1)].rearrange("(j p) e -> p j e", p=128)
                    for q4 in range(4):
                        P.dma("sp", vh[s][:, 16 * q4:16 * (q4 + 1), :], vsrc[:, 16 * q4:16 * (q4 + 1), :], Bv[s], True)

                LG = 4
                steps = []
                for hd in range(8):
                    for ci, (g0, n) in enumerate(CH):
                        nkt = (g0 + n) // 128
                        for kt in range(nkt):
                            steps.append((hd, ci, kt, nkt))

                def col0(i):
                    hd, ci, kt, nkt = steps[i]
                    g0, n = CH[ci]
                    return max(128 * kt - g0, 0)

                def emit_qk(i):
                    hd, ci, kt, nkt = steps[i]
                    g0, n = CH[ci]
                    s, sb, k0, loc = hd % 2, i % 2, 128 * kt, g0 - Q0
                    c0 = col0(i)
                    P.op("pe", lambda e: e.matmul(ps[sb][:, 0, c0:n], lhsT=kth[s][0:64, k0:k0 + 128],
                                                  rhs=qh[s][0:64, loc + c0:loc + n], start=True, stop=True),
                         [Bk[s], Bq[s]], [Bs[sb]])
                    P.op("pe", lambda e: e.matmul(ps[sb][:, 1, c0:n], lhsT=kth[s][64:128, k0:k0 + 128],
                                                  rhs=qh[s][64:128, loc + c0:loc + n], start=True, stop=True),
                         [Bk[s], Bq[s]], [Bs[sb]])

                def emit_exp(i):
                    hd, ci, kt, nkt = steps[i]
                    g0, n = CH[ci]
                    sb, pb, k0 = i % 2, i % 16, 128 * kt
                    bias = ctxb if kt < 32 else 0.0
                    c0 = col0(i)
                    P.op("act", lambda e: e.activation(out=pp[pb][:, :, c0:n], in_=ps[sb][:, :, c0:n], func=AF.Exp,
                                                       scale=0.125, bias=bias), [Bs[sb]], Bpp[pb])
                    if k0 >= g0:
                        j = (k0 - g0) // 128
                        c1 = c0 + 128
                        P.op("dve", lambda e: e.tensor_tensor(out=pp[pb][:, 0, c0:c1], in0=pp[pb][:, 0, c0:c1],
                                                              in1=masks[:, j, c0:c1], op=ALU.mult), [Bpp[pb][0]], [Bpp[pb][0]])
                        P.op("dve", lambda e: e.tensor_tensor(out=pp[pb][:, 1, c0:c1], in0=pp[pb][:, 1, c0:c1],
                                                              in1=masks[:, j, c0:c1], op=ALU.mult), [Bpp[pb][1]], [Bpp[pb][1]])

                def emit_av(i):
                    hd, ci, kt, nkt = steps[i]
                    g0, n = CH[ci]
                    s, pb = hd % 2, i % 16
                    first, last = kt == 0, kt == nkt - 1
                    c0 = col0(i)
                    for m in range(2):
                        P.op("pe", lambda e, m=m: e.matmul(po[:, m, c0:n], lhsT=vh[s][:, kt, :], rhs=pp[pb][:, m, c0:n],
                                                           start=first, stop=last), [Bv[s], Bpp[pb][m]], [Bo[m]])
                    if kt % LG == LG - 1:
                        for jj in range(LG):
                            slot = (i - (LG - 1) + jj) % 16
                            cj = col0(i - (LG - 1) + jj)
                            P.op("pe", lambda e, jj=jj, slot=slot, cj=cj: e.matmul(
                                pl[32 * jj:32 * (jj + 1), 0, cj:n], lhsT=ones_bf[:, 0:32], rhs=pp[slot][:, 0, cj:n],
                                start=(kt == LG - 1), stop=last, tile_position=(0, 32 * jj)), [Bpp[slot][0]], [Bl[0]])
                    if kt == 0:
                        P.op("dve", lambda e: e.tensor_copy(out=accP[:, 0:n], in_=pp[pb][:, 1, 0:n]), [Bpp[pb][1]], [BaccP])
                    else:
                        P.op("dve", lambda e: e.tensor_tensor(out=accP[:, c0:n], in0=accP[:, c0:n], in1=pp[pb][:, 1, c0:n],
                                                              op=ALU.add), [Bpp[pb][1], BaccP], [BaccP])
                    if last:
                        P.op("pe", lambda e: e.matmul(pl[:, 1, 0:n], lhsT=ones1f[:], rhs=accP[:, 0:n], start=True, stop=True),
                             [BaccP], [Bl[1]])
                        P.op("dve", lambda e: e.tensor_copy(out=plc[:, 0:n], in_=pl[:, 0, 0:n]), [Bl[0]], [Bplc])
                        P.op("pe", lambda e: e.matmul(pl[:, 0, 0:n], lhsT=onesf[:], rhs=plc[:, 0:n], start=True, stop=True),
                             [Bplc], [Bl[0]])

                fin = [0]

                def finalize(i):
                    hd, ci, kt, nkt = steps[i]
                    g0, n = CH[ci]
                    loc = g0 - Q0
                    y = fin[0] % 2
                    fin[0] += 1
                    for m in range(2):
                        P.op("dve", lambda e, m=m: e.tensor_scalar(out=ls[y][:, m, 0:n], in0=pl[:, m, 0:n],
                                                                   scalar1=(4.0 if m == 0 else 1.0), scalar2=1e-30,
                                                                   op0=ALU.mult, op1=ALU.add), [Bl[m]], [Bls[y][m]])
                        P.op("act", lambda e, m=m: e.activation(out=os_[y][:, m, 0:n], in_=po[:, m, 0:n], func=AF.Copy),
                             [Bo[m]], [Bos[y][m]])
                    def part_norm(m):
                        P.op("dve", lambda e: e.reciprocal(out=ls[y][:, m, 0:n], in_=ls[y][:, m, 0:n]),
                             [Bls[y][m]], [Bls[y][m]])
                        P.op("pool", lambda e: e.tensor_tensor(out=os_[y][:, m, 0:n], in0=os_[y][:, m, 0:n],
                                                               in1=ls[y][:, m, 0:n], op=ALU.mult),
                             [Bos[y][m], Bls[y][m]], [Bos[y][m]])

                    def part_out():
                        P.op("dve", lambda e: e.scalar_tensor_tensor(out=hdt[y][:, 0:n], in0=os_[y][:, 1, 0:n],
                                                                     scalar=lamt[:, 1:2], in1=os_[y][:, 0, 0:n],
                                                                     op0=ALU.mult, op1=ALU.add), Bos[y], [Bhd[y]])
                        P.dma("sp", HD[hd, :, loc:loc + n], hdt[y][:, 0:n], Bhd[y], False)

                    deferred.setdefault(i + 3, []).append(lambda: part_norm(0))
                    deferred.setdefault(i + 7, []).append(lambda: part_norm(1))
                    deferred.setdefault(i + 11, []).append(part_out)

                deferred = {}
                load_head(0, 0)
                prefetch_E(P)
                emit_qk(0)
                emit_qk(1)
                for i in range(len(steps)):
                    hd, ci, kt, nkt = steps[i]
                    if ci == 0 and kt == 0 and hd + 1 < 8:
                        load_head(hd + 1, 1 - hd % 2)
                    emit_exp(i)
                    if i + 2 < len(steps):
                        emit_qk(i + 2)
                    for fn in deferred.pop(i, []):
                        fn()
                    emit_av(i)
                    if kt == nkt - 1:
                        finalize(i)
                for j in sorted(deferred):
                    for fn in deferred[j]:
                        fn()
                P.emit()

        if upto >= 5:
            with ExitStack() as st:
                T = lambda n, s, d: st.enter_context(nc.sbuf_tensor(n, s, d))
                PT = lambda n, s, d: st.enter_context(nc.psum_tensor(n, s, d))
                gb = T("gb2", [128, D], F32)
                ya = [T("ya%d" % i, [128, 8, 512], BF16) for i in range(2)]
                hq = [T("hq%d" % i, [128, 8, 512], F32) for i in range(2)]
                yb = [T("yb%d" % i, [128, 8, 512], BF16) for i in range(2)]
                sqt = [T("sqt%d" % i, [128, 512], F32) for i in range(2)]
                lnt = [T("lnt%d" % i, [128, 512], F32) for i in range(2)]
                sg = [T("sg%d" % i, [128, 16, 512], BF16) for i in range(1)]
                t1 = [T("t1_%d" % i, [128, 512], F32) for i in range(2)]
                t2 = [T("t2_%d" % i, [128, 512], F32) for i in range(2)]
                mg = T("mg", [128, 8, 512], BF16)
                xt = [T("xtE%d" % i, [128, D], F32) for i in range(2)]
                x1 = [T("x1_%d" % i, [128, D], F32) for i in range(2)]
                junk = T("junkE", [128, D], F32)
                st4 = [T("st4E%d" % i, [128, 4], F32) for i in range(2)]
                hn = [T("hnE%d" % i, [128, D], BF16) for i in range(2)]
                h2s = [T("h2s%d" % i, [128, 8, 512], BF16) for i in range(1)]
                ppa = [PT("ppa%d" % i, [128, 512], F32) for i in range(2)]
                ppb = [PT("ppb%d" % i, [128, 512], F32) for i in range(2)]
                pw = [PT("pw%d" % i, [128, 512], F32) for i in range(2)]
                pT = [PT("pTE%d" % i, [128, 8, 128], BF16) for i in range(1)]
                pms = PT("pms", [128, 512], F32)
                P = Pass(ctx, "pE")
                Bwpa, Bwpb, Bwo, Bgb, Bjunk = Buf(), Buf(), Buf(), Buf(), Buf()
                Bya, Bhq, Bsg = [Buf(), Buf()], [Buf(), Buf()], [Buf()]
                Byb_ = [[Buf() for _ in range(8)] for _ in range(2)]
                Bsqt, Blnt, Bpms = [Buf(), Buf()], [Buf(), Buf()], Buf()
                Bt1, Bt2 = [Buf(), Buf()], [Buf(), Buf()]
                Bmg = [Buf() for _ in range(8)]
                Bxt, Bx1, Bst4, Bhn = [Buf(), Buf()], [Buf(), Buf()], [Buf(), Buf()], [Buf(), Buf()]
                Bh2s = [[Buf() for _ in range(4)] for _ in range(1)]
                Bppa, Bppb, Bpw, BpT = [Buf(), Buf()], [Buf(), Buf()], [Buf(), Buf()], [Buf()]
                P.dma("sp", gb[:], g2.partition_broadcast(128), Bgb, True)

                def load_chunk(ci):
                    g0, n = CH[ci]
                    loc = g0 - Q0
                    s = ci % 2
                    P.dma("sp", ya[s][:, :, 0:n], YA[:, :, loc:loc + n].rearrange("c p t -> p c t"), Bya[s], True)
                    P.dma("sp", hq[s][:, :, 0:n], HD[:, :, loc:loc + n].rearrange("c p t -> p c t"), Bhq[s], True)

                def load_sg(ci):
                    g0, n = CH[ci]
                    loc = g0 - Q0
                    P.dma("sp", sg[0][:, :, 0:n], SG[:, :, loc:loc + n].rearrange("c p t -> p c t"), Bsg[0], True)

                def sub_ln(ci, heads):
                    g0, n = CH[ci]
                    s = ci % 2
                    for hd in heads:
                        q2 = hd % 2
                        P.op("pool", lambda e, hd=hd, q2=q2: e.tensor_tensor(out=sqt[q2][:, 0:n], in0=hq[s][:, hd, 0:n],
                                                                             in1=hq[s][:, hd, 0:n], op=ALU.mult),
                             [Bhq[s]], [Bsqt[q2]])
                        P.op("pe", lambda e, q2=q2: e.matmul(pms[:, 0:n], lhsT=onesf[:], rhs=sqt[q2][:, 0:n], start=True,
                                                             stop=True), [Bsqt[q2]], [Bpms])
                        P.op("act", lambda e, q2=q2: e.activation(out=lnt[q2][:, 0:n], in_=pms[:, 0:n], func=AF.Ln, bias=1e-5),
                             [Bpms], [Blnt[q2]])
                        P.op("act", lambda e, q2=q2: e.activation(out=lnt[q2][:, 0:n], in_=lnt[q2][:, 0:n], func=AF.Exp,
                                                                  scale=-0.5), [Blnt[q2]], [Blnt[q2]])
                        P.op("dve", lambda e, hd=hd, q2=q2: e.scalar_tensor_tensor(
                            out=yb[s][:, hd, 0:n], in0=hq[s][:, hd, 0:n], scalar=sublg[:, 0:1], in1=lnt[q2][:, 0:n],
                            op0=ALU.mult, op1=ALU.mult), [Bhq[s], Blnt[q2]], [Byb_[s][hd]])

                load_chunk(0)
                ti = 0
                for ci, (g0, n) in enumerate(CH):
                    s = ci % 2
                    loc = g0 - Q0
                    if ci + 1 < len(CH):
                        load_chunk(ci + 1)
                    load_sg(ci)
                    sub_ln(ci, range(8))
                    for c in range(8):
                        b = c % 2
                        mm8(P, ppa[b][:, 0:n], lambda k, c=c: wpa[:, k, 128 * c:128 * (c + 1)],
                            lambda k, s=s, n=n: ya[s][:, k, 0:n], [Bwpa, Bya[s]], Bppa[b])
                        mm8(P, ppb[b][:, 0:n], lambda k, c=c: wpb[:, k, 128 * c:128 * (c + 1)],
                            lambda k, s=s, n=n: yb[s][:, k, 0:n], [Bwpb] + Byb_[s], Bppb[b])
                        P.op("dve", lambda e, b=b, c=c, n=n: e.tensor_tensor(out=t1[b][:, 0:n], in0=ppa[b][:, 0:n],
                                                                            in1=sg[0][:, c, 0:n], op=ALU.mult),
                             [Bppa[b], Bsg[0]], [Bt1[b]])
                        P.op("dve", lambda e, b=b, c=c, n=n: e.tensor_tensor(out=t2[b][:, 0:n], in0=ppb[b][:, 0:n],
                                                                            in1=sg[0][:, 8 + c, 0:n], op=ALU.mult),
                             [Bppb[b], Bsg[0]], [Bt2[b]])
                        P.op("pool", lambda e, b=b, c=c, n=n: e.tensor_tensor(out=mg[:, c, 0:n], in0=t1[b][:, 0:n],
                                                                              in1=t2[b][:, 0:n], op=ALU.add),
                             [Bt1[b], Bt2[b]], [Bmg[c]])
                    S = 0
                    nj = n // 128
                    pend = None

                    def emit_T(u, j):
                        for c in range(8):
                            P.op("pe", lambda e, c=c: e.transpose(out=pT[0][:, c, :], in_=hn[u][:, 128 * c:128 * (c + 1)],
                                                                  identity=ident[:]), [Bhn[u]], [BpT[0]])
                        P.op("act", lambda e: e.activation(out=h2s[S][:, :, 128 * j:128 * (j + 1)], in_=pT[0][:], func=AF.Copy),
                             [BpT[0]], [Bh2s[S][j]])
                    for j in range(n // 128):
                        u = ti % 2
                        ti += 1
                        r0 = g0 + 128 * j
                        P.dma("sp", xt[u][:], xin[r0:r0 + 128, :], Bxt[u], True)
                        for half in range(2):
                            mm8(P, pw[half][:], lambda k, j=j: mg[:, k, 128 * j:128 * (j + 1)],
                                lambda k, half=half: wo[:, k, 512 * half:512 * (half + 1)], [Bwo] + Bmg, Bpw[half])
                            P.op("dve", lambda e, u=u, half=half: e.tensor_tensor(
                                out=x1[u][:, 512 * half:512 * (half + 1)], in0=pw[half][:],
                                in1=xt[u][:, 512 * half:512 * (half + 1)], op=ALU.add), [Bpw[half], Bxt[u]], [Bx1[u]])
                        P.dma("sp", X1[loc + 128 * j:loc + 128 * (j + 1), :], x1[u][:], Bx1[u], False)
                        rms_T(P, x1[u][:], Bx1[u], gb, Bgb, junk, Bjunk, st4[u], Bst4[u], hn[u], Bhn[u], None, None)
                        if pend is not None:
                            emit_T(*pend)
                        pend = (u, j)
                    emit_T(*pend)
                    o = P.dma("sp", H2T[:, :, loc:loc + n].rearrange("c p t -> p c t"), h2s[S][:, :, 0:n], Bh2s[S][0], False)
                    for qq in range(1, n // 128):
                        o.deps = set(o.deps) | set(Bh2s[S][qq].w)
                        Bh2s[S][qq].rs.append(o)
                P.emit()

        stWE.close()
        stWD = ExitStack()
        wd = WT(stWD, "wd", [128, 24, D], "right")

        def prefetch_F2(P):
            for q in range(3):
                load_w(P, wd[:, 8 * q:8 * (q + 1), :], "w_dn", GF2, 1024 * q, 0, 1024, 8, Buf())

        if upto >= 6:
            with ExitStack() as st:
                T = lambda n, s, d: st.enter_context(nc.sbuf_tensor(n, s, d))
                PT = lambda n, s, d: st.enter_context(nc.psum_tensor(n, s, d))
                wu = T("wu", [128, 8, 2 * DFF], BF16)
                h2t = [T("h2t%d" % i, [128, 8, 512], BF16) for i in range(2)]
                carry = T("carry", [128, 24, 2], F32)
                ugb = [T("ugb%d" % i, [128, 514], F32) for i in range(3)]
                cv = [T("cv%d" % i, [128, 512], F32) for i in range(2)]
                ge = [T("ge%d" % i, [128, 512], F32) for i in range(2)]
                ats = [T("ats%d" % i, [128, 24, 512], BF16) for i in range(1)]
                pg = [PT("pg%d" % i, [128, 512], F32) for i in range(3)]
                pv = [PT("pv%d" % i, [128, 512], F32) for i in range(3)]
                P = Pass(ctx, "pF1")
                Bwu = Buf()
                Bwuv = Buf()
                Bh2t = [Buf(), Buf()]
                Bcar = [Buf() for _ in range(24)]
                Bug = [Buf() for _ in range(3)]
                Bcv, Bge = [Buf(), Buf()], [Buf(), Buf()]
                Bats = [[Buf() for _ in range(24)] for _ in range(1)]
                Bpg, Bpv = [Buf() for _ in range(3)], [Buf() for _ in range(3)]
                load_w(P, wu[:, :, 0:DFF], "w_up", GF1, 0, 0, DFF, 8, Bwu)
                load_w(P, wu[:, :, DFF:2 * DFF], "w_up", GF1, 0, DFF, DFF, 8, Bwuv)
                g0, n = CH[0]
                P.dma("sp", h2t[0][:, :, 0:n], H2T[:, :, 0:n].rearrange("c p t -> p c t"), Bh2t[0], True)
                it = 0
                for ci, (g0, n) in enumerate(CH):
                    s = ci % 2
                    loc = g0 - Q0
                    if ci + 1 < len(CH):
                        g1_, n1 = CH[ci + 1]
                        l1 = g1_ - Q0
                        P.dma("sp", h2t[1 - s][:, :, 0:n1], H2T[:, :, l1:l1 + n1].rearrange("c p t -> p c t"), Bh2t[1 - s], True)
                    A = 0
                    if ci == 2:
                        prefetch_F2(P)
                    for fc in range(24):
                        b = it % 3
                        c2 = it % 2
                        it += 1
                        mm8(P, pg[b][:, 0:n], lambda k, fc=fc: wu[:, k, 128 * fc:128 * (fc + 1)],
                            lambda k, s=s, n=n: h2t[s][:, k, 0:n], [Bwu, Bh2t[s]], Bpg[b])
                        if ci == 0:
                            P.op("dve", lambda e, fc=fc, b=b: e.tensor_scalar(out=carry[:, fc, :], in0=pg[b][:, 126:128],
                                                                              scalar1=ctxf, scalar2=None, op0=ALU.mult),
                                 [Bpg[b]], [Bcar[fc]])
                            continue
                        mm8(P, pv[b][:, 0:n], lambda k, fc=fc: wu[:, k, DFF + 128 * fc:DFF + 128 * (fc + 1)],
                            lambda k, s=s, n=n: h2t[s][:, k, 0:n], [Bwuv, Bh2t[s]], Bpv[b])
                        P.op("pool", lambda e, fc=fc, b=b: e.tensor_copy(out=ugb[b][:, 0:2], in_=carry[:, fc, :]),
                             [Bcar[fc]], [Bug[b]])
                        P.op("act", lambda e, b=b: e.activation(out=ugb[b][:, 2:514], in_=pg[b][:], func=AF.Copy),
                             [Bpg[b]], [Bug[b]])
                        P.op("pool", lambda e, fc=fc, b=b: e.tensor_copy(out=carry[:, fc, :], in_=ugb[b][:, 512:514]),
                             [Bug[b]], [Bcar[fc]])
                        P.op("dve", lambda e, fc=fc, b=b, c2=c2: e.tensor_scalar(
                            out=cv[c2][:], in0=ugb[b][:, 0:512], scalar1=pff[:, 0, fc:fc + 1], scalar2=pff[:, 3, fc:fc + 1],
                            op0=ALU.mult, op1=ALU.add), [Bug[b]], [Bcv[c2]])
                        for k in range(1, 3):
                            P.op("dve", lambda e, fc=fc, b=b, c2=c2, k=k: e.scalar_tensor_tensor(
                                out=cv[c2][:], in0=ugb[b][:, k:k + 512], scalar=pff[:, k, fc:fc + 1], in1=cv[c2][:],
                                op0=ALU.mult, op1=ALU.add), [Bug[b], Bcv[c2]], [Bcv[c2]])
                        P.op("act", lambda e, c2=c2: e.activation(out=ge[c2][:], in_=cv[c2][:], func=GELU), [Bcv[c2]], [Bge[c2]])
                        P.op("dve", lambda e, fc=fc, b=b, c2=c2, A=A: e.tensor_tensor(out=ats[A][:, fc, :], in0=pv[b][:],
                                                                                     in1=ge[c2][:], op=ALU.mult),
                             [Bpv[b], Bge[c2]], [Bats[A][fc]])
                        if ci >= 1 and fc % 12 == 11:
                            t0 = g0 - NOWN
                            f0 = fc - 11
                            o = P.dma("sp", AT[f0:f0 + 12, :, t0:t0 + 512].rearrange("c p t -> p c t"), ats[A][:, f0:f0 + 12, :],
                                      Bats[A][f0], False)
                            for f2 in range(f0 + 1, f0 + 12):
                                o.deps = set(o.deps) | set(Bats[A][f2].w)
                                Bats[A][f2].rs.append(o)
                P.emit()

        if upto >= 7:
            with ExitStack() as st:
                T = lambda n, s, d: st.enter_context(nc.sbuf_tensor(n, s, d))
                PT = lambda n, s, d: st.enter_context(nc.psum_tensor(n, s, d))
                gb = T("gb3", [128, D], F32)
                att = [T("att%d" % i, [128, 24, 512], BF16) for i in range(2)]
                x1t = [T("x1t%d" % i, [128, D], F32) for i in range(2)]
                x2 = [T("x2_%d" % i, [128, D], F32) for i in range(2)]
                junk = T("junkF", [128, D], F32)
                st4 = [T("st4F%d" % i, [128, 4], F32) for i in range(2)]
                ot = [T("ot%d" % i, [128, D], F32) for i in range(2)]
                pd = [PT("pd%d" % i, [128, 512], F32) for i in range(4)]
                P = Pass(ctx, "pF2")
                Bwd, Bgb, Bjunk = Buf(), Buf(), Buf()
                Batt, Bx1t, Bx2, Bst4, Bot = [Buf(), Buf()], [Buf(), Buf()], [Buf(), Buf()], [Buf(), Buf()], [Buf(), Buf()]
                Bpd = [Buf() for _ in range(4)]
                P.dma("sp", gb[:], g3.partition_broadcast(128), Bgb, True)
                P.dma("sp", att[0][:], AT[:, :, 0:512].rearrange("c p t -> p c t"), Batt[0], True)
                ti = 0
                pi = 0
                for ci in range(8):
                    s = ci % 2
                    if ci + 1 < 8:
                        P.dma("sp", att[1 - s][:], AT[:, :, 512 * (ci + 1):512 * (ci + 2)].rearrange("c p t -> p c t"),
                              Batt[1 - s], True)
                    for j in range(4):
                        u = ti % 2
                        ti += 1
                        t0 = 512 * ci + 128 * j
                        P.dma("sp", x1t[u][:], X1[128 + t0:128 + t0 + 128, :], Bx1t[u], True)
                        for half in range(2):
                            b = pi % 4
                            pi += 1
                            mm8(P, pd[b][:], lambda k, s=s, j=j: att[s][:, k, 128 * j:128 * (j + 1)],
                                lambda k, half=half: wd[:, k, 512 * half:512 * (half + 1)], [Bwd, Batt[s]], Bpd[b], n=24)
                            P.op("dve", lambda e, u=u, half=half, b=b: e.tensor_tensor(
                                out=x2[u][:, 512 * half:512 * (half + 1)], in0=pd[b][:],
                                in1=x1t[u][:, 512 * half:512 * (half + 1)], op=ALU.add), [Bpd[b], Bx1t[u]], [Bx2[u]])
                        rms_T(P, x2[u][:], Bx2[u], gb, Bgb, junk, Bjunk, st4[u], Bst4[u], ot[u], Bot[u], None, None)
                        P.dma("sp", out[t0:t0 + 128, :], ot[u][:], Bot[u], False)
                P.emit()
    nc._mk_nops = ctx.nops
    return nc


def make_in_maps(inputs):
    f = lambda a: np.ascontiguousarray(np.asarray(a, dtype=np.float32))
    x = f(inputs["x"])
    B, S, _ = x.shape
    half = S // 2
    shared = {
        "w_in": f(inputs["w_in"][0]),
        "w_pa": f(inputs["w_proj_rnn"][0]),
        "w_pb": f(inputs["w_proj_attn"][0]),
        "w_o": f(inputs["w_out"][0]),
        "w_up": f(inputs["w_up"][0]),
        "w_dn": f(inputs["w_down"][0]),
        "rg_wa": f(inputs["rg_wa"][0]),
        "rg_wx": f(inputs["rg_wx"][0]),
        "g1": f(inputs["attn_norm_g"][0]),
        "g2": f(inputs["mlp_norm_g"][0]),
        "g3": f(inputs["final_norm_g"]),
        "lamv": f(np.stack([inputs["lam_q1"][0], inputs["lam_k1"][0], inputs["lam_q2"][0], inputs["lam_k2"][0]])),
        "sublg": f(np.asarray(inputs["subln_g"][0]).reshape(128, 1)),
    }
    cw = np.asarray(inputs["rnn_conv_w"][0], np.float32)
    rows = [cw[0], cw[1], cw[2], cw[3], inputs["rnn_conv_b"][0], inputs["rg_ba"][0], inputs["rg_bx"][0],
            inputs["rg_lambda"][0]]
    pr = np.stack([np.asarray(r, np.float32).reshape(8, 128) for r in rows])
    shared["par_rnn"] = f(pr.transpose(2, 0, 1))
    fw_ = np.asarray(inputs["ffn_conv_w"][0], np.float32)
    rows = [fw_[0], fw_[1], fw_[2], inputs["ffn_conv_b"][0]]
    pf = np.stack([np.asarray(r, np.float32).reshape(24, 128) for r in rows])
    shared["par_ffn"] = f(pf.transpose(2, 0, 1))
    in_maps = []
    for b in range(B):
        for h in range(2):
            m = dict(shared)
            xi = np.zeros((NT, D), np.float32)
            if h == 1:
                xi[:] = x[b]
            else:
                xi[half:] = x[b, :half]
            m["xin"] = xi
            fl = np.zeros((128, 2), np.float32)
            fl[:, 0] = 0.0 if h == 1 else -30000.0
            fl[:, 1] = 1.0 if h == 1 else 0.0
            m["flags"] = fl
            in_maps.append(m)
    return in_maps


_NC_CACHE = {}


def kernel(**inputs):
    in_maps = make_in_maps(inputs)
    if "nc" not in _NC_CACHE:
        _NC_CACHE["nc"] = build_nc()
    nc = _NC_CACHE["nc"]
    res = run_bass_kernel_spmd(nc, in_maps, core_ids=list(range(8)))
    x = np.asarray(inputs["x"])
    B, S, _ = x.shape
    outp = np.empty((B, S, D), np.float32)
    k = 0
    for b in range(B):
        for h in range(2):
            outp[b, h * NOWN:(h + 1) * NOWN] = res.results[k]["out"]
            k += 1
    return outp
```
